# Optimizing a Trainium2 kernel written in Bass

```python
import jax, jax.numpy as jnp
from jax import lax
import numpy as np

D_MODEL = 1024
BATCH = 2
SEQ = 16384
DEPTH = 2

CTX_LEN = 256
GRID_W = 64
N_MIXERS = 2
N_A = (DEPTH + 1) // 2
N_B = DEPTH // 2
D_RNN = 1280
RG_HEADS = 16
RG_HEAD_DIM = D_RNN // RG_HEADS
RG_CONV_W = 4
RG_CONV_PAD = (2, 1)
RG_C = 8.0
CF_CONV_W = 31
CF_CONV_PAD = (15, 15)
D_FF = 2816
FFN_CONV_W = 3
EPS = 1e-6

kernel_name = "hybrid_rglru_conformer_convffn_prefix_ctx"


def rmsnorm(x, g):
    xf = x.astype(jnp.float32)
    y = xf * lax.rsqrt(jnp.mean(xf * xf, axis=-1, keepdims=True) + EPS)
    return (y * g.astype(jnp.float32)).astype(x.dtype)


def layernorm(x, g, b):
    xf = x.astype(jnp.float32)
    mu = jnp.mean(xf, axis=-1, keepdims=True)
    var = jnp.mean(jnp.square(xf - mu), axis=-1, keepdims=True)
    y = (xf - mu) * lax.rsqrt(var + EPS)
    return (y * g.astype(jnp.float32) + b.astype(jnp.float32)).astype(x.dtype)


def dwconv1d(u, w, b, pad):
    y = lax.conv_general_dilated(u, w[:, None, :].astype(u.dtype), window_strides=(1,), padding=[pad],
                                 dimension_numbers=('NWC', 'WIO', 'NWC'), feature_group_count=u.shape[-1])
    return y + b


def dwconv2d(u, w, b, rows, width):
    bsz, seq, ch = u.shape
    g = u.reshape(bsz, rows, width, ch)
    y = lax.conv_general_dilated(g, w[:, :, None, :].astype(u.dtype), window_strides=(1, 1),
                                 padding=((1, 1), (1, 1)), dimension_numbers=('NHWC', 'HWIO', 'NHWC'),
                                 feature_group_count=ch)
    return y.reshape(bsz, seq, ch) + b


def _scan_combine(left, right):
    a_l, b_l = left
    a_r, b_r = right
    return a_l * a_r, a_r * b_l + b_r


def linear_scan(log_a, b, h0, reverse):
    A, H = lax.associative_scan(_scan_combine, (jnp.exp(log_a), b), axis=1, reverse=reverse)
    return H + A * h0[:, None, :]


def rglru_core(xm, h0_f, h0_b, w_in, conv_w, conv_b, wa, ba, wx, bx, lam):
    u = xm @ w_in
    ug, ux = jnp.split(u, 2, axis=-1)
    ux = dwconv1d(ux, conv_w, conv_b, RG_CONV_PAD).astype(jnp.float32)
    bsz, seq, _ = ux.shape
    uh = ux.reshape(bsz, seq, RG_HEADS, RG_HEAD_DIM)
    r = jax.nn.sigmoid(jnp.einsum('bshd,zhde->zbshe', uh, wa.astype(jnp.float32))
                       + ba.astype(jnp.float32)[:, None, None]).reshape(2, bsz, seq, D_RNN)
    ig = jax.nn.sigmoid(jnp.einsum('bshd,zhde->zbshe', uh, wx.astype(jnp.float32))
                        + bx.astype(jnp.float32)[:, None, None]).reshape(2, bsz, seq, D_RNN)
    log_a = -RG_C * r * jax.nn.softplus(-lam.astype(jnp.float32))[:, None, None, :]
    b = jnp.sqrt(-jnp.expm1(2.0 * log_a)) * (ig * ux[None])
    hf = linear_scan(log_a[0], b[0], h0_f, reverse=False)
    hb = linear_scan(log_a[1], b[1], h0_b, reverse=True)
    return hf, hb, ug


def rglru_out(y, ug, w_out):
    v = (y * jax.nn.gelu(ug.astype(jnp.float32))).astype(ug.dtype)
    return v @ w_out


def conformer_conv(xm, w1, b1, cw, cb, lng, lnb, w2, b2):
    u = xm @ w1 + b1
    a, g = jnp.split(u, 2, axis=-1)
    v = a * jax.nn.sigmoid(g)
    v = dwconv1d(v, cw, cb, CF_CONV_PAD)
    v = jax.nn.silu(layernorm(v, lng, lnb))
    return v @ w2 + b2


def conv_ffn(xm, w_up, cw, cb, w_down, rows, width):
    u = dwconv2d(xm @ w_up, cw, cb, rows, width)
    v, g = jnp.split(u, 2, axis=-1)
    return (jax.nn.silu(g) * v) @ w_down


def setup_inputs(seed: int = 0) -> dict:
    key = jax.random.key(seed)
    ks = jax.random.split(key, 32)

    def nrm(k, shape, scale):
        return jax.random.normal(k, shape, jnp.float32) * scale

    D = D_MODEL
    a_init = jax.random.uniform(ks[14], (N_A, 2, D_RNN), jnp.float32, 0.9, 0.999)
    return {
        "x": nrm(ks[0], (BATCH, SEQ, D), 1.0),
        "c": nrm(ks[1], (BATCH, D), 1.0),
        "ctx": nrm(ks[2], (BATCH, CTX_LEN, D), 1.0),
        "c_ctx": nrm(ks[3], (D,), 1.0),
        "ada_w": nrm(ks[4], (DEPTH, D, 6 * D), 0.02),
        "ada_b": nrm(ks[5], (DEPTH, 6 * D), 0.02),
        "norm_mix": 1.0 + nrm(ks[6], (DEPTH, D), 0.05),
        "norm_ffn": 1.0 + nrm(ks[7], (DEPTH, D), 0.05),
        "rg_w_in": nrm(ks[8], (N_A, D, 2 * D_RNN), D ** -0.5),
        "rg_conv_w": nrm(ks[9], (N_A, RG_CONV_W, D_RNN), RG_CONV_W ** -0.5),
        "rg_conv_b": nrm(ks[10], (N_A, D_RNN), 0.02),
        "rg_wa": nrm(ks[11], (N_A, 2, RG_HEADS, RG_HEAD_DIM, RG_HEAD_DIM), RG_HEAD_DIM ** -0.5),
        "rg_ba": nrm(ks[12], (N_A, 2, RG_HEADS, RG_HEAD_DIM), 0.02),
        "rg_wx": nrm(ks[13], (N_A, 2, RG_HEADS, RG_HEAD_DIM, RG_HEAD_DIM), RG_HEAD_DIM ** -0.5),
        "rg_bx": nrm(ks[15], (N_A, 2, RG_HEADS, RG_HEAD_DIM), 0.02),
        "rg_lam": jnp.log(a_init) - jnp.log1p(-a_init),
        "rg_w_out": nrm(ks[16], (N_A, D_RNN, D), D_RNN ** -0.5),
        "cf_w_pw1": nrm(ks[17], (N_B, D, 2 * D), D ** -0.5),
        "cf_b_pw1": nrm(ks[18], (N_B, 2 * D), 0.02),
        "cf_conv_w": nrm(ks[19], (N_B, CF_CONV_W, D), CF_CONV_W ** -0.5),
        "cf_conv_b": nrm(ks[20], (N_B, D), 0.02),
        "cf_ln_g": 1.0 + nrm(ks[21], (N_B, D), 0.05),
        "cf_ln_b": nrm(ks[22], (N_B, D), 0.02),
        "cf_w_pw2": nrm(ks[23], (N_B, D, D), D ** -0.5),
        "cf_b_pw2": nrm(ks[24], (N_B, D), 0.02),
        "ffn_w_up": nrm(ks[25], (DEPTH, D, 2 * D_FF), D ** -0.5),
        "ffn_conv_w": nrm(ks[26], (DEPTH, FFN_CONV_W, FFN_CONV_W, 2 * D_FF), 1.0 / FFN_CONV_W),
        "ffn_conv_b": nrm(ks[27], (DEPTH, 2 * D_FF), 0.02),
        "ffn_w_down": nrm(ks[28], (DEPTH, D_FF, D), D_FF ** -0.5),
        "norm_final": 1.0 + nrm(ks[29], (D,), 0.05),
    }


def reference(x, c, ctx, c_ctx, ada_w, ada_b, norm_mix, norm_ffn,
              rg_w_in, rg_conv_w, rg_conv_b, rg_wa, rg_ba, rg_wx, rg_bx, rg_lam, rg_w_out,
              cf_w_pw1, cf_b_pw1, cf_conv_w, cf_conv_b, cf_ln_g, cf_ln_b, cf_w_pw2, cf_b_pw2,
              ffn_w_up, ffn_conv_w, ffn_conv_b, ffn_w_down, norm_final):
    bsz, seq, _ = x.shape
    ROWS = seq // GRID_W
    ctx_len = ctx.shape[1]
    h, hc = x, ctx
    s_c = jax.nn.silu(c)
    s_cc = jax.nn.silu(c_ctx)
    for i in range(DEPTH):
        ctx_live = any(j % N_MIXERS == 0 for j in range(i + 1, DEPTH))
        mod = s_c @ ada_w[i] + ada_b[i]
        sh1, sc1, g1, sh2, sc2, g2 = [t[:, None, :] for t in jnp.split(mod, 6, axis=-1)]
        modc = s_cc @ ada_w[i] + ada_b[i]
        csh1, csc1, cg1, csh2, csc2, cg2 = jnp.split(modc, 6, axis=-1)
        xm = rmsnorm(h, norm_mix[i]) * (1.0 + sc1) + sh1
        cm = rmsnorm(hc, norm_mix[i]) * (1.0 + csc1) + csh1
        if i % N_MIXERS == 0:
            k = i // N_MIXERS
            prm = (rg_w_in[k], rg_conv_w[k], rg_conv_b[k], rg_wa[k], rg_ba[k], rg_wx[k], rg_bx[k], rg_lam[k])
            zero = jnp.zeros((bsz, D_RNN), jnp.float32)
            hf_c, hb_c, ug_c = rglru_core(cm, zero, zero, *prm)
            hf, hb, ug = rglru_core(xm, hf_c[:, -1], hb_c[:, 0], *prm)
            h = h + g1 * rglru_out(hf + hb, ug, rg_w_out[k])
            if ctx_live:
                hc = hc + cg1 * rglru_out(hf_c + hb_c, ug_c, rg_w_out[k])
        else:
            k = i // N_MIXERS
            prm = (cf_w_pw1[k], cf_b_pw1[k], cf_conv_w[k], cf_conv_b[k], cf_ln_g[k], cf_ln_b[k],
                   cf_w_pw2[k], cf_b_pw2[k])
            h = h + g1 * conformer_conv(xm, *prm)
            if ctx_live:
                hc = hc + cg1 * conformer_conv(cm, *prm)
        xm2 = rmsnorm(h, norm_ffn[i]) * (1.0 + sc2) + sh2
        h = h + g2 * conv_ffn(xm2, ffn_w_up[i], ffn_conv_w[i], ffn_conv_b[i], ffn_w_down[i], ROWS, GRID_W)
        if ctx_live:
            cm2 = rmsnorm(hc, norm_ffn[i]) * (1.0 + csc2) + csh2
            hc = hc + cg2 * conv_ffn(cm2, ffn_w_up[i], ffn_conv_w[i], ffn_conv_b[i], ffn_w_down[i], 1, ctx_len)
    return rmsnorm(h, norm_final)
```

```python
import numpy as np
from contextlib import ExitStack
import concourse.bass as bass
import concourse.mybir as mybir
from concourse.bass_utils import run_bass_kernel_spmd

F32, BF16 = mybir.dt.float32, mybir.dt.bfloat16
AF = mybir.ActivationFunctionType
ALU = mybir.AluOpType

D = 1024
SEQ = 16384
CTX = 256
DR = 1280
DFF = 2816
EPS = 1e-6
NTX = 4483
NTE = 4480
RG_ROWS = [3, 3, 5, 6, 6, 6, 6, 6, 6, 6, 6, 5, 3, 3]
NBL = len(RG_ROWS)
NBO = NBL - 2

def _gate_tiles():
    tiles = []
    for j in range(10):
        heads = set(range((j * 128) // 80, ((j + 1) * 128 - 1) // 80 + 1))
        ins = set()
        for h in heads:
            for ch in (h * 80, h * 80 + 79):
                ins.add(ch // 128)
        for i in sorted(ins):
            tiles.append((j, i))
    return tiles
GT = _gate_tiles()
NGT = len(GT)

_off = {}
def _alloc_cols():
    o = 0
    for name, n in [("cT", 16), ("ada_b", 96), ("norm_mix", 16), ("norm_ffn", 16), ("norm_final", 8),
                    ("rg_conv_w", 40), ("rg_conv_b", 10), ("rg_ba", 20), ("rg_bx", 20), ("rg_lam", 20), ("o_lam", 30), ("o_ba", 30), ("o_bx", 30),
                    ("cf_b1", 16), ("cf_conv_w", 248), ("cf_conv_b", 8), ("cf_ln_g", 8), ("cf_ln_b", 8),
                    ("cf_b2", 8), ("ffn_cw", 792), ("ffn_cb", 88)]:
        _off[name] = o
        o += n
    return o
PC = _alloc_cols()


def fm(v, nch):
    return np.ascontiguousarray(np.asarray(v, np.float32).reshape(nch, 128).T)


class Buf:
    def __init__(self, t, name):
        self.t = t
        self.name = name
        self.w = {}
        self.r = {}
        self.ds = None

    def __getitem__(self, i):
        return self.t[i]


class Ctx:
    def __init__(self, nc, ndsem=64):
        self.nc = nc
        self.E = {"pe": nc.tensor, "act": nc.scalar, "dve": nc.vector, "pool": nc.gpsimd, "sp": nc.sync}
        self.sem = {e: nc.alloc_semaphore("s_" + e) for e in self.E}
        self.cnt = {e: 0 for e in self.E}
        self.known = {e: {} for e in self.E}
        self.dpool = [[nc.alloc_semaphore("d%d" % i), 0, "d%d" % i] for i in range(ndsem)]
        self.dfree = list(range(ndsem))
        self.nops = 0
        self.extra = []

    def dram(self, ap, name):
        return Buf(ap, name)

    def _deps(self, reads, writes, part):
        deps = {}

        def add(d):
            for k, (s, v) in d.items():
                if k not in deps or deps[k][1] < v:
                    deps[k] = (s, v)
        for b in reads:
            add(b.w)
        for b in writes:
            if not part:
                add(b.w)
            add(b.r)
        return deps

    def _wait(self, e, deps):
        for k, (s, v) in deps.items():
            if k == "pe" and e == "pe":
                continue
            if self.known[e].get(k, 0) >= v:
                continue
            self.E[e].wait_ge(s, v)
            self.known[e][k] = v

    def op(self, e, fn, reads=(), writes=(), part=False, inc=True):
        self._wait(e, self._deps(reads, writes, part))
        ins = fn(self.E[e])
        if inc:
            self.cnt[e] += 1
            ins.then_inc(self.sem[e], 1)
            tag = (self.sem[e], self.cnt[e])
        else:
            tag = (self.sem[e], self.cnt[e] + 1)
        for b in reads:
            b.r[e] = tag
        for b in writes:
            b.w[e] = tag
        self.nops += 1
        return ins

    def _dsem(self, b):
        if b.ds is None:
            b.ds = self.dfree.pop(0)
        return self.dpool[b.ds]

    def release(self, bufs):
        for b in bufs:
            if b.ds is not None:
                self.dfree.append(b.ds)
                b.ds = None

    def dma(self, q, out, in_, src, dst, owner, part=False):
        self._wait(q, self._deps([src], [dst], part))
        ent = self._dsem(owner)
        ent[1] += 16
        self.E[q].dma_start(out=out, in_=in_).then_inc(ent[0], 16)
        tag = (ent[0], ent[1])
        src.r[ent[2]] = tag
        dst.w[ent[2]] = tag

    def all_gather(self, src, dst, groups):
        self._wait("pool", self._deps([src], [dst], False))
        sem = self.nc.alloc_semaphore("cc%d" % len(self.extra))
        ins = self.E["pool"].collective_compute("AllGather", mybir.AluOpType.bypass, replica_groups=groups,
                                                ins=[src.t.opt()], outs=[dst.t.opt()])
        ins.then_inc(sem)
        key = "cc%d" % len(self.extra)
        self.extra.append([sem, 1, key])
        src.r[key] = (sem, 1)
        dst.w[key] = (sem, 1)

    def barrier(self):
        for e in self.E:
            deps = {}
            for ent in self.extra:
                deps[ent[2]] = (ent[0], ent[1])
            for e2 in self.E:
                if e2 != e and self.cnt[e2] > 0:
                    deps[e2] = (self.sem[e2], self.cnt[e2])
            for ent in self.dpool:
                if ent[1] > 0:
                    deps[ent[2]] = (ent[0], ent[1])
            for k, (s, v) in deps.items():
                if self.known[e].get(k, 0) >= v:
                    continue
                self.E[e].wait_ge(s, v)
                self.known[e][k] = v

    def final_wait(self, e="sp"):
        for ent in self.dpool:
            if ent[1] > 0 and self.known[e].get(ent[2], 0) < ent[1]:
                self.E[e].wait_ge(ent[0], ent[1])
                self.known[e][ent[2]] = ent[1]


class Stage:
    def __init__(self, K, name):
        self.K = K
        self.name = name
        self.stack = ExitStack()
        self.bufs = []
        self.n = 0

    def sb(self, name, shape, dt):
        self.n += 1
        t = self.stack.enter_context(self.K.nc.sbuf_tensor("%s_%s_%d" % (self.name, name, self.n), list(shape), dt))
        b = Buf(t, name)
        self.bufs.append(b)
        return b

    def close(self):
        self.K.barrier()
        self.K.release(self.bufs)
        self.stack.close()


def build_program(mode, upto="all", debug=False):
    nc = bass.Bass("TRN2", target_bir_lowering=False)
    K = Ctx(nc)

    def din(name, shape, dt=F32):
        return K.dram(nc.dram_tensor(name, list(shape), dt, kind="ExternalInput").ap(), name)

    def dint(name, shape, dt=F32):
        kind = "ExternalOutput" if (debug and name in ("HA", "HB", "HC")) else "Internal"
        return K.dram(nc.dram_tensor(name, list(shape), dt, kind=kind).ap(), name)

    def dout(name, shape, dt=F32):
        return K.dram(nc.dram_tensor(name, list(shape), dt, kind="ExternalOutput").ap(), name)

    xin = din("xin", [NTX, D])
    cin = din("cin", [CTX + 3, D])
    valid = din("valid", [128, NTX])
    cvalid = din("cvalid", [128, CTX + 3])
    prm_d = din("prm", [128, PC])
    ident_d = din("ident", [128, 128])
    ada_w = din("ada_w", [2, D, 6 * D])
    rg_w_in = din("rg_w_in", [D, 2 * DR])
    gw_d = din("gw", [4 * NGT, 128, 128])
    if mode != "sum":
        rg_w_out = din("rg_w_out", [DR, D])
        cf_w1 = din("cf_w1", [D, 2 * D])
        cf_w2 = din("cf_w2", [D, D])
        ffn_up = din("ffn_up", [2, D, 2 * DFF])
        ffn_dn = din("ffn_dn", [2, DFF, D])
        fmask = din("fmask", [128, 2, (3 if mode == "solo" else 8) * NBO])
        out_d = dout("out", [4096, D])
    SUMW = 2 * 2 * 10 * NBO
    NS = 3 if mode == "solo" else 8
    if mode == "solo":
        xoth = din("xoth", [3, 4099, D])
        ovalid = din("ovalid", [3, 128, 4099])
        gwo_d = din("gwo", [3 * 2 * NGT, 128, 128])
    if mode == "sum":
        sum_out = dout("sums", [128, SUMW])
    elif mode == "main":
        sumg_d = din("sumg", [128, 8, SUMW])
    elif mode == "solo":
        pass
    else:
        sum_loc_d = K.dram(nc.dram_tensor("sum_loc", [128, SUMW], F32).ap(), "sum_loc")
        sumg_all = K.dram(nc.dram_tensor("sumg_all", [8 * 128, SUMW], F32).ap(), "sumg_all")

    if mode != "sum":
        SAB = dint("SAB", [10, 128, 4, NTE])
        XM = dint("XM", [128, 8, NTE], BF16)
        HX = dint("HX", [128, 8, NTE])
        HA = dint("HA", [128, 8, NTE])
        HB = dint("HB", [128, 8, 68 * 64])
        HC = dint("HC", [128, 8, 66 * 64])
        WUP = [dint("WUP%d" % i, [44, 128, 8, 128], BF16) for i in range(2)]
        WDN = [dint("WDN%d" % i, [8, 128, 22, 128], BF16) for i in range(2)]
        DG = [dint("DG%d" % i, [44, 128, 9, 128], BF16) for i in range(2)]

    PS = [Buf(nc.alloc_psum_tensor("ps%d" % i, [128, 512], F32), "ps%d" % i) for i in range(8)]

    G = Stage(K, "g")
    prm = G.sb("prm", [128, PC], F32)
    identF = G.sb("identF", [128, 128], F32)
    identB = G.sb("identB", [128, 128], BF16)
    onesB = G.sb("onesB", [128, 128], BF16)
    onesF = G.sb("onesF", [128, 128], F32)
    MOD = G.sb("MOD", [128, 2, 2, 6, 8], F32)
    VEC = G.sb("VEC", [128, 16, 8], F32)
    SUML = G.sb("SUML", [128, 2, 2, 10, NBL], F32)
    HCTX = G.sb("HCTX", [128, 2, 10], F32)
    HIN = G.sb("HIN", [128, 2, 10, NBL + 1], F32)
    SCV = G.sb("SCV", [128, 6, 10], F32)
    HBG = G.sb("HBG", [128, 4, 10], F32)
    SCVO = G.sb("SCVO", [128, 6, 10], F32)
    HBGO = G.sb("HBGO", [128, 6, 10], F32)
    SUMO = G.sb("SUMO", [128, 3, 2, 2, 10, NBO], F32)
    accO = G.sb("accO", [128, 3, 10, NBO], F32)

    def P(name, n=None, i=0):
        o = _off[name] + i
        return prm[:, o:o + (n if n is not None else 1)]

    V_GS1, V_SH1, V_G1, V_GS2, V_SH2, V_G2 = 0, 1, 2, 3, 4, 5
    V_GSC, V_SHC, V_BG = 12, 13, 14

    K.dma("sp", prm[:], prm_d[:], prm_d, prm, prm)
    K.dma("sp", identF[:], ident_d[:], ident_d, identF, identF)
    K.op("dve", lambda e: e.tensor_copy(out=identB[:], in_=identF[:]), [identF], [identB])
    K.op("dve", lambda e: e.memset(onesB[:], 1.0), [], [onesB])
    K.op("dve", lambda e: e.memset(onesF[:], 1.0), [], [onesF])
    K.op("dve", lambda e: e.memset(SUML[:], 0.0), [], [SUML])

    S0 = Stage(K, "p")
    scb = S0.sb("scb", [128, 16], BF16)
    K.op("act", lambda e: e.activation(out=scb[:], in_=P("cT", 16), func=AF.Silu), [prm], [scb])
    adaw = [S0.sb("adaw%d" % i, [128, 8, 1024], BF16) for i in range(2)]
    it = 0
    for li in range(2):
        for q in range(6):
            wt = adaw[it % 2]
            K.dma("pool", wt[:], ada_w[li][:, q * 1024:(q + 1) * 1024].rearrange("(k p) m -> p k m", p=128),
                  ada_w, wt, wt)
            pp = PS[it % 2]
            for m in range(8):
                for k in range(8):
                    K.op("pe", lambda e, m=m, k=k, wt=wt, pp=pp: e.matmul(
                        pp[:, m * 2:m * 2 + 2], lhsT=wt[:, k, m * 128:(m + 1) * 128], rhs=scb[:, k * 2:k * 2 + 2],
                        start=(k == 0), stop=(k == 7)), [wt, scb], [pp], part=(k > 0 or m > 0), inc=(k == 7))
            for j in range(2):
                K.op("dve", lambda e, j=j, li=li, q=q, pp=pp: e.tensor_tensor(
                    out=MOD[:, li, j, q, :], in0=pp[:, j:16:2], in1=P("ada_b", 8, li * 48 + q * 8), op=ALU.add),
                    [pp, prm], [MOD], part=True)
            it += 1
    for li in range(2):
        K.op("dve", lambda e, li=li: e.scalar_tensor_tensor(
            out=VEC[:, V_GS1 + 6 * li, :], in0=MOD[:, li, 0, 1, :], scalar=1.0, in1=P("norm_mix", 8, li * 8),
            op0=ALU.add, op1=ALU.mult), [MOD, prm], [VEC], part=True)
        K.op("dve", lambda e, li=li: e.scalar_tensor_tensor(
            out=VEC[:, V_GS2 + 6 * li, :], in0=MOD[:, li, 0, 4, :], scalar=1.0, in1=P("norm_ffn", 8, li * 8),
            op0=ALU.add, op1=ALU.mult), [MOD, prm], [VEC], part=True)
        for vrow, q in ((V_SH1, 0), (V_G1, 2), (V_SH2, 3), (V_G2, 5)):
            K.op("dve", lambda e, li=li, vrow=vrow, q=q: e.tensor_copy(out=VEC[:, vrow + 6 * li, :], in_=MOD[:, li, 0, q, :]),
                 [MOD], [VEC], part=True)
    K.op("dve", lambda e: e.scalar_tensor_tensor(
        out=VEC[:, V_GSC, :], in0=MOD[:, 0, 1, 1, :], scalar=1.0, in1=P("norm_mix", 8, 0),
        op0=ALU.add, op1=ALU.mult), [MOD, prm], [VEC], part=True)
    K.op("dve", lambda e: e.tensor_copy(out=VEC[:, V_SHC, :], in_=MOD[:, 0, 1, 0, :]), [MOD], [VEC], part=True)
    K.op("dve", lambda e: e.tensor_tensor(out=VEC[:, V_BG, :], in0=P("cf_b2", 8), in1=MOD[:, 1, 0, 2, :], op=ALU.mult),
         [MOD, prm], [VEC], part=True)
    tmp20 = S0.sb("tmp20", [128, 50], F32)
    ev20 = S0.sb("ev20", [128, 50], F32)
    q20 = S0.sb("q20", [128, 50], F32)
    K.op("act", lambda e: e.activation(out=ev20[:], in_=P("rg_lam", 50), func=AF.Exp, scale=-1.0), [prm], [ev20])
    K.op("dve", lambda e: e.tensor_scalar(out=q20[:], in0=ev20[:], scalar1=-1.0 / 7, scalar2=1.0 / 6, op0=ALU.mult, op1=ALU.add),
         [ev20], [q20])
    for cst in (1.0 / 5, 1.0 / 4, 1.0 / 3, 1.0 / 2, 1.0):
        K.op("dve", lambda e: e.tensor_tensor(out=q20[:], in0=q20[:], in1=ev20[:], op=ALU.mult), [q20, ev20], [q20])
        K.op("dve", lambda e, cst=cst: e.tensor_scalar(out=q20[:], in0=q20[:], scalar1=-1.0, scalar2=cst, op0=ALU.mult, op1=ALU.add),
             [q20], [q20])
    K.op("dve", lambda e: e.tensor_tensor(out=tmp20[:], in0=q20[:], in1=ev20[:], op=ALU.mult), [q20, ev20], [tmp20])
    K.op("dve", lambda e: e.tensor_scalar(out=SCV[:, 0:2, :], in0=tmp20[:, 0:20].rearrange("p (d c) -> p d c", d=2),
                                          scalar1=-4.0, scalar2=None, op0=ALU.mult), [tmp20], [SCV], part=True)
    K.op("dve", lambda e: e.tensor_scalar(out=SCV[:, 2:4, :], in0=tmp20[:, 0:20].rearrange("p (d c) -> p d c", d=2),
                                          scalar1=-8.0, scalar2=None, op0=ALU.mult), [tmp20], [SCV], part=True)
    K.op("dve", lambda e: e.tensor_scalar(out=SCVO[:, 0:3, :], in0=tmp20[:, 20:50].rearrange("p (d c) -> p d c", d=3),
                                          scalar1=-4.0, scalar2=None, op0=ALU.mult), [tmp20], [SCVO], part=True)
    K.op("dve", lambda e: e.tensor_scalar(out=SCVO[:, 3:6, :], in0=tmp20[:, 20:50].rearrange("p (d c) -> p d c", d=3),
                                          scalar1=-8.0, scalar2=None, op0=ALU.mult), [tmp20], [SCVO], part=True)
    for o in range(3):
        K.op("dve", lambda e, o=o: e.tensor_scalar(out=HBGO[:, o * 2, :], in0=P("o_ba", 10, o * 10), scalar1=0.5,
                                                     scalar2=None, op0=ALU.mult), [prm], [HBGO], part=True)
        K.op("dve", lambda e, o=o: e.tensor_scalar(out=HBGO[:, o * 2 + 1, :], in0=P("o_bx", 10, o * 10), scalar1=0.5,
                                                     scalar2=None, op0=ALU.mult), [prm], [HBGO], part=True)
    for d in range(2):
        K.op("dve", lambda e, d=d: e.tensor_scalar(out=HBG[:, d * 2, :], in0=P("rg_ba", 10, d * 10), scalar1=0.5,
                                                     scalar2=None, op0=ALU.mult), [prm], [HBG], part=True)
        K.op("dve", lambda e, d=d: e.tensor_scalar(out=HBG[:, d * 2 + 1, :], in0=P("rg_bx", 10, d * 10), scalar1=0.5,
                                                     scalar2=None, op0=ALU.mult), [prm], [HBG], part=True)
    if mode != "sum":
        dgst = [S0.sb("dgst%d" % i, [128, 9, 128], BF16) for i in range(3)]
        for li in range(2):
            cwo_ = _off["ffn_cw"] + li * 396
            for cc in range(44):
                dgs = dgst[cc % 3]
                for t in range(9):
                    K.op("dve", lambda e, t=t, dgs=dgs, cc=cc, cwo_=cwo_: e.tensor_scalar(
                        out=dgs[:, t, :], in0=identB[:], scalar1=prm[:, cwo_ + cc * 9 + t:cwo_ + cc * 9 + t + 1], scalar2=None,
                        op0=ALU.mult), [identB, prm], [dgs], part=(t > 0))
                K.dma("sp", DG[li][cc], dgs[:], dgs, DG[li], dgs, part=True)
    S0.close()

    def norm_mod(S, h, n, xm, gsrow, shrow, sq, ssb, rt, rstd, tmp, out_f32=None):
        K.op("act", lambda e: e.activation(out=sq[:, :, 0:n], in_=h[:, :, 0:n], func=AF.Square), [h], [sq])
        nseg = [(0, min(n, 512))] + ([(512, n)] if n > 512 else [])
        for si, (a, b) in enumerate(nseg):
            for c in range(8):
                K.op("pe", lambda e, c=c, a=a, b=b, si=si: e.matmul(
                    ssb[si][:, 0:b - a], lhsT=onesB[:], rhs=sq[:, c, a:b], start=(c == 0), stop=(c == 7)),
                    [sq], [ssb[si]], part=(c > 0), inc=(c == 7))
            K.op("act", lambda e, a=a, b=b, si=si: e.activation(
                out=rt[:, a:b], in_=ssb[si][:, 0:b - a], func=AF.Sqrt, scale=1.0 / D, bias=epsb[:, 0:1]),
                [ssb[si]], [rt], part=True)
        K.op("dve", lambda e: e.reciprocal(out=rstd[:, 0:n], in_=rt[:, 0:n]), [rt], [rstd])
        for c in range(8):
            tb = tmp[c % 2]
            K.op("dve", lambda e, c=c, tb=tb: e.tensor_tensor(out=tb[:, 0:n], in0=h[:, c, 0:n], in1=rstd[:, 0:n], op=ALU.mult),
                 [h, rstd], [tb])
            if out_f32 is None:
                K.op("act", lambda e, c=c, tb=tb: e.activation(
                    out=xm[:, c, 0:n], in_=tb[:, 0:n], func=AF.Identity, scale=gsrow(c), bias=shrow(c)),
                    [tb], [xm], part=True)
            else:
                K.op("act", lambda e, c=c, tb=tb: e.activation(
                    out=out_f32[:, c, 0:n], in_=tb[:, 0:n], func=AF.Identity, scale=gsrow(c)),
                    [tb], [out_f32], part=True)

    epsb = G.sb("epsb", [128, 2], F32)
    K.op("dve", lambda e: e.memset(epsb[:, 0:1], EPS), [], [epsb], part=True)
    K.op("dve", lambda e: e.memset(epsb[:, 1:2], 1.0), [], [epsb], part=True)

    S1 = Stage(K, "r1")
    w_in = S1.sb("w_in", [128, 8, DR], BF16)
    K.dma("pool", w_in[:], rg_w_in[:, DR:2 * DR].rearrange("(k p) m -> p k m", p=128), rg_w_in, w_in, w_in)
    gwt = S1.sb("gw", [128, 4 * NGT, 128], BF16)
    K.dma("pool", gwt[:], gw_d[:, :, :].rearrange("t p m -> p t m"), gw_d, gwt, gwt)
    dg4 = S1.sb("dg4", [128, 40, 128], BF16)
    for c in range(10):
        for t in range(4):
            K.op("dve", lambda e, c=c, t=t: e.tensor_scalar(
                out=dg4[:, c * 4 + t, :], in0=identB[:], scalar1=P("rg_conv_w", 1, c * 4 + t), scalar2=None, op0=ALU.mult),
                [identB, prm], [dg4], part=True)
    NM = 387
    xt4 = S1.sb("xt4", [128, 4, D], F32)
    hb_ = S1.sb("h", [128, 8, NM], F32)
    xm = S1.sb("xm", [128, 8, NM], BF16)
    rt = S1.sb("rt", [128, NM], F32)
    rstd = S1.sb("rstd", [128, NM], F32)
    tmpn = [S1.sb("tmpn%d" % i, [128, NM], F32) for i in range(2)]
    vslb = [S1.sb("vsl%d" % i, [128, NM], F32) for i in range(2)]
    uxb = S1.sb("uxb", [128, 10, NM], BF16)
    uxcb = [S1.sb("uxc%d" % i, [128, 10, 384], BF16) for i in range(2)]
    ABs = [S1.sb("AB%d" % i, [128, 4, 384], F32) for i in range(4)]
    trb = [S1.sb("tr%d" % i, [128, 384], F32) for i in range(8)]
    tib = [S1.sb("ti%d" % i, [128, 384], F32) for i in range(8)]
    a2b = [S1.sb("a2%d" % i, [128, 384], F32) for i in range(8)]
    hsb = [S1.sb("hs%d" % i, [128, 384], F32) for i in range(2)]
    accb = S1.sb("acc", [128, 2, 10, NBL + 1], F32)

    if debug:
        print("S1 sbuf remaining", nc.sbuf_bytes_remaining)
    GTI = {}
    for ti, (j, i) in enumerate(GT):
        GTI.setdefault(j, []).append((ti, i))
    AX = mybir.AxisListType.X

    def rgA(B):
        src, vsrc, i0, N, par, store, tau0, is_ctx = B["src"], B["vsrc"], B["i0"], B["N"], B["par"], B["store"], B["tau0"], B["blk"] is None
        N3 = N + 3
        ntile = (N3 + 127) // 128
        vsl, uxc = vslb[par], uxcb[par]
        steps = []

        def s_load():
            for t in range(ntile):
                nt = min(128, N3 - 128 * t)
                K.dma("sp", xt4[0:nt, t, :], src[i0 - 2 + 128 * t:i0 - 2 + 128 * t + nt, :], src, xt4, xt4, part=(t > 0))
            K.dma("sp", vsl[:, 0:N3], vsrc[:, i0 - 2:i0 - 2 + N3], vsrc, vsl, vsl)
        steps.append(s_load)

        def s_tr(c):
            pp = PS[c % 2]
            for t in range(ntile):
                nt = min(128, N3 - 128 * t)
                K.op("pe", lambda e, t=t, nt=nt: e.transpose(
                    out=pp[:, 128 * t:128 * t + nt], in_=xt4[0:nt, t, c * 128:(c + 1) * 128], identity=identF[0:nt, 0:nt]),
                    [xt4, identF], [pp], part=(t > 0))
            if c % 2 == 0:
                K.op("act", lambda e: e.copy(out=hb_[:, c, 0:N3], in_=pp[:, 0:N3]), [pp], [hb_], part=True)
            else:
                K.op("dve", lambda e: e.tensor_copy(out=hb_[:, c, 0:N3], in_=pp[:, 0:N3]), [pp], [hb_], part=True)
        for c in range(8):
            steps.append(lambda c=c: s_tr(c))
        if is_ctx:
            gs = lambda c: VEC[:, V_GSC, c:c + 1]
            sh = lambda c: VEC[:, V_SHC, c:c + 1]
        else:
            gs = lambda c: VEC[:, V_GS1, c:c + 1]
            sh = lambda c: VEC[:, V_SH1, c:c + 1]

        def s_stat():
            K.op("act", lambda e: e.activation(out=xm[:, :, 0:N3], in_=hb_[:, :, 0:N3], func=AF.Square), [hb_], [xm])
            for c in range(8):
                K.op("pe", lambda e, c=c: e.matmul(PS[2][:, 0:N3], lhsT=onesB[:], rhs=xm[:, c, 0:N3], start=(c == 0), stop=(c == 7)),
                     [xm], [PS[2]], part=(c > 0), inc=(c == 7))
            K.op("act", lambda e: e.activation(out=rt[:, 0:N3], in_=PS[2][:, 0:N3], func=AF.Sqrt, scale=1.0 / D, bias=epsb[:, 0:1]),
                 [PS[2]], [rt])
            K.op("dve", lambda e: e.reciprocal(out=rstd[:, 0:N3], in_=rt[:, 0:N3]), [rt], [rstd])
        steps.append((s_stat, "stat"))

        def s_mod(c):
            tb = tmpn[c % 2]
            K.op("dve", lambda e: e.tensor_tensor(out=tb[:, 0:N3], in0=hb_[:, c, 0:N3], in1=rstd[:, 0:N3], op=ALU.mult), [hb_, rstd], [tb])
            K.op("act", lambda e: e.activation(out=xm[:, c, 0:N3], in_=tb[:, 0:N3], func=AF.Identity, scale=gs(c), bias=sh(c)),
                 [tb], [xm], part=True)
        for c in range(8):
            steps.append(lambda c=c: s_mod(c))
        if store:
            def s_store():
                K.dma("pool", HX[:, :, tau0:tau0 + N], hb_[:, :, 2:2 + N], hb_, HX, hb_, part=True)
                K.dma("pool", XM[:, :, tau0:tau0 + N], xm[:, :, 2:2 + N], xm, XM, xm, part=True)
            steps.append(s_store)

        def s_win(oc):
            pp = PS[oc % 2]
            for k in range(8):
                K.op("pe", lambda e, k=k: e.matmul(pp[:, 0:N3], lhsT=w_in[:, k, oc * 128:(oc + 1) * 128], rhs=xm[:, k, 0:N3],
                                                   start=(k == 0), stop=(k == 7)), [w_in, xm], [pp], part=(k > 0), inc=(k == 7))
            K.op("dve", lambda e: e.tensor_tensor(out=uxb[:, oc, 0:N3], in0=pp[:, 0:N3], in1=vsl[:, 0:N3], op=ALU.mult),
                 [pp, vsl], [uxb], part=True)
        for oc in range(10):
            steps.append(lambda oc=oc: s_win(oc))

        def s_conv(c):
            pp = PS[2] if c % 2 == 0 else PS[5]
            for t in range(4):
                K.op("pe", lambda e, t=t: e.matmul(pp[:, 0:N], lhsT=dg4[:, c * 4 + t, :], rhs=uxb[:, c, t:t + N], start=(t == 0), stop=(t == 3)),
                     [dg4, uxb], [pp], part=(t > 0), inc=(t == 3))
            K.op("act", lambda e: e.activation(out=uxc[:, c, 0:N], in_=pp[:, 0:N], func=AF.Identity, bias=P("rg_conv_b", 1, c), scale=1.0),
                 [pp], [uxc], part=True)
        for c in range(10):
            steps.append(lambda c=c: s_conv(c))
        return steps

    def rgB(B):
        N, par, blk, edge, store, tau0, other = B["N"], B["par"], B["blk"], B["edge"], B["store"], B["tau0"], B["other"]
        vsl, uxc = vslb[par], uxcb[par]
        if other is None:
            cfgs = []
            for d in range(2):
                cfgs.append(dict(
                    tb=[(d * 2 + g) * NGT for g in range(2)], hb=(lambda g, c, d=d: HBG[:, d * 2 + g, c:c + 1]),
                    s05=(lambda c, d=d: SCV[:, d, c:c + 1]), s1=(lambda c, d=d: SCV[:, 2 + d, c:c + 1]), plane=d, pa=(lambda c, d=d: 2 * d),
                    acc=(lambda c, d=d: accb[:, d, c, NBL:NBL + 1] if blk is None else accb[:, d, c, blk:blk + 1]),
                    scans=[(d == 1, (lambda c, d=d: HCTX[:, d, c:c + 1] if blk is None else SUML[:, d, 1, c, blk:blk + 1]),
                            HCTX if blk is None else SUML)]))
            gsz = 2
        else:
            o = other
            cfgs = [dict(
                tb=[g * NGT for g in range(2)], hb=(lambda g, c: HBGO[:, o * 2 + g, c:c + 1]),
                s05=(lambda c: SCVO[:, o, c:c + 1]), s1=(lambda c: SCVO[:, 3 + o, c:c + 1]), plane=0, pa=(lambda c: 2 * ((c // 4) % 2)),
                acc=(lambda c: accO[:, o, c, blk:blk + 1]),
                scans=[(False, (lambda c: SUMO[:, o, 0, 1, c, blk:blk + 1]), SUMO),
                       (True, (lambda c: SUMO[:, o, 1, 1, c, blk:blk + 1]), SUMO)])]
            gsz = 3
        steps = []

        def s_p1(ui, c, cf):
            for g in range(2):
                pp = PS[3 + g] if ui % 2 == 0 else PS[6 + g]
                tl = GTI[c]
                for n_, (ti, i) in enumerate(tl):
                    K.op("pe", lambda e, ti=ti, i=i, n_=n_: e.matmul(
                        pp[:, 0:N], lhsT=gwt[:, cf["tb"][g] + ti, :], rhs=uxc[:, i, 0:N],
                        start=(n_ == 0), stop=(n_ == len(tl) - 1)), [gwt, uxc], [pp], part=(n_ > 0), inc=(n_ == len(tl) - 1))
                dst = trb[ui] if g == 0 else tib[ui]
                K.op("act", lambda e, g=g, dst=dst: e.activation(
                    out=dst[:, 0:N], in_=pp[:, 0:N], func=AF.Tanh, scale=0.5, bias=cf["hb"](g, c)), [pp], [dst])

        def s_p2(ui, c, cf):
            ab = ABs[c % 4]
            d = cf["plane"]
            if edge:
                K.op("dve", lambda e: e.scalar_tensor_tensor(
                    out=trb[ui][:, 0:N], in0=trb[ui][:, 0:N], scalar=1.0, in1=vsl[:, 2:2 + N], op0=ALU.add, op1=ALU.mult,
                    accum_out=cf["acc"](c)), [trb[ui], vsl], [trb[ui], accb], part=True)
            else:
                K.op("dve", lambda e: e.tensor_scalar(
                    out=trb[ui][:, 0:N], in0=trb[ui][:, 0:N], scalar1=1.0, scalar2=None, op0=ALU.add, op1=ALU.add,
                    accum_out=cf["acc"](c)), [trb[ui]], [trb[ui], accb], part=True)
            pa = cf["pa"](c)
            K.op("act", lambda e: e.activation(out=ab[:, pa, 0:N], in_=trb[ui][:, 0:N], func=AF.Exp, scale=cf["s05"](c)),
                 [trb[ui]], [ab], part=(other is not None or d > 0))
            K.op("act", lambda e: e.activation(out=a2b[ui][:, 0:N], in_=trb[ui][:, 0:N], func=AF.Exp, scale=cf["s1"](c)),
                 [trb[ui]], [a2b[ui]])
            K.op("dve", lambda e: e.scalar_tensor_tensor(
                out=tib[ui][:, 0:N], in0=tib[ui][:, 0:N], scalar=1.0, in1=uxc[:, c, 0:N], op0=ALU.add, op1=ALU.mult),
                [tib[ui], uxc], [tib[ui]])

        def s_p3(uis):
            for ui in uis:
                K.op("act", lambda e, ui=ui: e.activation(
                    out=a2b[ui][:, 0:N], in_=a2b[ui][:, 0:N], func=AF.Sqrt, scale=-1.0, bias=epsb[:, 1:2]), [a2b[ui]], [a2b[ui]])

        def s_p4(ui, c, cf):
            ab = ABs[c % 4]
            d = cf["plane"]
            pa = cf["pa"](c)
            K.op("dve", lambda e: e.scalar_tensor_tensor(
                out=ab[:, pa + 1, 0:N], in0=a2b[ui][:, 0:N], scalar=0.0, in1=tib[ui][:, 0:N], op0=ALU.max, op1=ALU.mult),
                [tib[ui], a2b[ui]], [ab], part=True)
            for si, (rev, dstf, dstbuf) in enumerate(cf["scans"]):
                hs = hsb[(d + si) % 2]
                if not rev:
                    K.op("dve", lambda e, hs=hs: e.tensor_tensor_scan(
                        out=hs[:, 0:N], data0=ab[:, pa, 0:N], data1=ab[:, pa + 1, 0:N], initial=0.0,
                        op0=ALU.mult, op1=ALU.add), [ab], [hs])
                    src_col = hs[:, N - 1:N]
                else:
                    K.op("dve", lambda e, hs=hs: e.tensor_tensor_scan(
                        out=hs[:, 0:N][:, ::-1], data0=ab[:, pa, 0:N][:, ::-1], data1=ab[:, pa + 1, 0:N][:, ::-1],
                        initial=0.0, op0=ALU.mult, op1=ALU.add), [ab], [hs])
                    src_col = hs[:, 0:1]
                K.op("dve", lambda e, src_col=src_col, dstc=dstf(c): e.tensor_copy(out=dstc, in_=src_col), [hs], [dstbuf], part=True)
            if store and d == 1:
                K.dma("pool", SAB[c][:, :, tau0:tau0 + N], ab[:, :, 0:N], ab, SAB, ab, part=True)

        groups = []
        for gi, c0 in enumerate(range(0, 10, gsz)):
            groups.append([((gi % 2) * 4 + ui, c, cf) for ui, (c, cf) in
                           enumerate([(c, cf) for c in range(c0, min(10, c0 + gsz)) for cf in cfgs])])

        def P1(g):
            return [((lambda u=u: s_p1(*u)), "b") for u in groups[g]]

        def P2(g):
            return [((lambda u=u: s_p2(*u)), "b") for u in groups[g]]

        def P3(g):
            return [((lambda uis=[u[0] for u in groups[g]]: s_p3(uis)), "sqrt")]

        def P4(g):
            return [((lambda u=u: s_p4(*u)), "b") for u in groups[g]]
        steps = P1(0) + P2(0)
        for g in range(len(groups)):
            if g + 1 < len(groups):
                steps += P1(g + 1)
            steps += P3(g)
            if g + 1 < len(groups):
                steps += P2(g + 1)
            steps += P4(g)
        return steps

    blist = [dict(src=cin, vsrc=cvalid, i0=2, N=CTX, blk=None, edge=True, store=False, tau0=0, other=None, pre=None)]
    tau = 0
    for bi, nr in enumerate(RG_ROWS):
        blist.append(dict(src=xin, vsrc=valid, i0=tau + 2, N=nr * 64, blk=bi, edge=bi in (0, NBL - 1), store=(mode != "sum"),
                          tau0=tau, other=None, pre=None))
        tau += nr * 64
    if mode == "solo":
        for o in range(3):
            tau = 0
            for j, nr in enumerate(RG_ROWS[1:-1]):
                blist.append(dict(src=Buf(xoth.t[o], "xo"), vsrc=Buf(ovalid.t[o], "vo"), i0=tau + 2, N=nr * 64, blk=j,
                                  edge=j in (0, NBO - 1), store=False, tau0=0, other=o, pre=(o if j == 0 else None)))
                tau += nr * 64
    for i, B in enumerate(blist):
        B["par"] = i % 2
    def astep(st):
        return st if isinstance(st, tuple) else (st, "a")
    for st in rgA(blist[0]):
        astep(st)[0]()
    for i, B in enumerate(blist):
        if B["pre"] is not None:
            o = B["pre"]
            K.dma("pool", gwt[:, 0:2 * NGT, :], gwo_d[o * 2 * NGT:(o + 1) * 2 * NGT, :, :].rearrange("t p m -> p t m"),
                  gwo_d, gwt, gwt)
        sb_ = rgB(B)
        sa_ = [astep(st) for st in rgA(blist[i + 1])] if i + 1 < len(blist) else []
        ia = 0
        for ib, (fb, tagb) in enumerate(sb_):
            fb()
            tgt = ((ib + 1) * len(sa_)) // len(sb_)
            while ia < tgt:
                fa, taga = sa_[ia]
                if taga == "stat" and tagb != "sqrt" and any(t == "sqrt" for (_, t) in sb_[ib + 1:]):
                    break
                fa()
                ia += 1
        while ia < len(sa_):
            sa_[ia][0]()
            ia += 1
    for d in range(2):
        for c in range(10):
            K.op("act", lambda e, d=d, c=c: e.activation(
                out=SUML[:, d, 0, c, :], in_=accb[:, d, c, 0:NBL], func=AF.Exp, scale=SCV[:, d, c:c + 1]),
                [accb], [SUML], part=True)
    if mode == "solo":
        for o in range(3):
            for c in range(10):
                for d in range(2):
                    K.op("act", lambda e, d=d, c=c, o=o: e.activation(
                        out=SUMO[:, o, d, 0, c, :], in_=accO[:, o, c, :], func=AF.Exp, scale=SCVO[:, o, c:c + 1]),
                        [accb], [SUMO], part=True)
    S1.close()

    SX = Stage(K, "x")
    if mode == "sum":
        stg = SX.sb("stg", [128, 2, 2, 10, NBO], F32)
        K.op("dve", lambda e: e.tensor_copy(out=stg[:], in_=SUML[:, :, :, :, 1:NBL - 1]), [SUML], [stg])
        K.dma("sp", sum_out[:], stg[:].rearrange("p a b c j -> p (a b c j)"), stg, sum_out, stg)
        K.final_wait("sp")
        SX.close()
        G.close()
        return nc

    sumg = SUMO if mode == "solo" else SX.sb("sumg", [128, 8, 2, 2, 10, NBO], F32)
    if mode == "solo":
        pass
    elif mode == "main":
        K.dma("sp", sumg[:].rearrange("p r a b c j -> p r (a b c j)"), sumg_d[:], sumg_d, sumg, sumg)
    else:
        stg = SX.sb("stg", [128, 2, 2, 10, NBO], F32)
        K.op("dve", lambda e: e.tensor_copy(out=stg[:], in_=SUML[:, :, :, :, 1:NBL - 1]), [SUML], [stg])
        K.dma("pool", sum_loc_d[:], stg[:].rearrange("p a b c j -> p (a b c j)"), stg, sum_loc_d, stg)
        K.all_gather(sum_loc_d, sumg_all, [list(range(8))])
        K.dma("pool", sumg[:].rearrange("p r a b c j -> p r (a b c j)"), sumg_all[:, :].rearrange("(r p) w -> p r w", p=128),
              sumg_all, sumg, sumg)
    fm_ = SX.sb("fm", [128, 2, NS, NBO], F32)
    K.dma("sp", fm_[:].rearrange("p d r j -> p d (r j)"), fmask[:], fmask, fm_, fm_)
    At = SX.sb("At", [128, NS, NBO], F32)
    Ht = SX.sb("Ht", [128, NS, NBO], F32)
    sco = SX.sb("sco", [128, NS * NBO], F32)
    for d in range(2):
        for c in range(10):
            K.op("dve", lambda e, d=d, c=c: e.scalar_tensor_tensor(
                out=At[:], in0=sumg[:, :, d, 0, c, :], scalar=-1.0, in1=fm_[:, d, :, :], op0=ALU.add, op1=ALU.mult),
                [sumg, fm_], [At])
            K.op("dve", lambda e: e.tensor_scalar(out=At[:], in0=At[:], scalar1=1.0, scalar2=None, op0=ALU.add), [At], [At])
            K.op("dve", lambda e, d=d, c=c: e.tensor_tensor(out=Ht[:], in0=sumg[:, :, d, 1, c, :], in1=fm_[:, d, :, :], op=ALU.mult),
                 [sumg, fm_], [Ht])
            Af = At[:].rearrange("p r j -> p (r j)")
            Hf = Ht[:].rearrange("p r j -> p (r j)")
            if d == 0:
                K.op("dve", lambda e, d=d, c=c, Af=Af, Hf=Hf: e.tensor_tensor_scan(
                    out=sco[:], data0=Af, data1=Hf, initial=HCTX[:, d, c:c + 1], op0=ALU.mult, op1=ALU.add),
                    [At, Ht, HCTX], [sco])
                K.op("dve", lambda e, c=c: e.tensor_copy(out=HIN[:, 0, c, 0:1], in_=sco[:, NS * NBO - 1:NS * NBO]), [sco], [HIN], part=True)
                K.op("dve", lambda e, c=c: e.tensor_tensor_scan(
                    out=HIN[:, 0, c, 1:NBL + 1], data0=SUML[:, 0, 0, c, :], data1=SUML[:, 0, 1, c, :],
                    initial=HIN[:, 0, c, 0:1], op0=ALU.mult, op1=ALU.add), [SUML, HIN], [HIN], part=True)
            else:
                K.op("dve", lambda e, d=d, c=c, Af=Af, Hf=Hf: e.tensor_tensor_scan(
                    out=sco[:][:, ::-1], data0=Af[:, ::-1], data1=Hf[:, ::-1], initial=HCTX[:, d, c:c + 1],
                    op0=ALU.mult, op1=ALU.add), [At, Ht, HCTX], [sco])
                K.op("dve", lambda e, c=c: e.tensor_copy(out=HIN[:, 1, c, NBL:NBL + 1], in_=sco[:, 0:1]), [sco], [HIN], part=True)
                K.op("dve", lambda e, c=c: e.tensor_tensor_scan(
                    out=HIN[:, 1, c, 0:NBL][:, ::-1], data0=SUML[:, 1, 0, c, :][:, ::-1], data1=SUML[:, 1, 1, c, :][:, ::-1],
                    initial=HIN[:, 1, c, NBL:NBL + 1], op0=ALU.mult, op1=ALU.add), [SUML, HIN], [HIN], part=True)
    SX.close()

    S2 = Stage(K, "r2")
    w_out = S2.sb("w_out", [128, 10, D], BF16)
    K.dma("pool", w_out[:], rg_w_out[:, :].rearrange("(c p) m -> p c m", p=128), rg_w_out, w_out, w_out)
    w_ing = S2.sb("w_ing", [128, 8, DR], BF16)
    K.dma("pool", w_ing[:], rg_w_in[:, 0:DR].rearrange("(k p) m -> p k m", p=128), rg_w_in, w_ing, w_ing)
    if mode != "sum":
        for li in range(2):
            for cc in range(44):
                K.dma("pool", WUP[li][cc], ffn_up[li][:, cc * 128:(cc + 1) * 128].rearrange("(k p) m -> p k m", p=128),
                      ffn_up, WUP[li], WUP[li], part=True)
            for m in range(8):
                K.dma("pool", WDN[li][m],
                      ffn_dn[li][:, m * 128:(m + 1) * 128].rearrange("(j p) m -> p j m", p=128),
                      ffn_dn, WDN[li], WDN[li], part=True)
    AB2 = [S2.sb("AB%d" % i, [128, 4, 384], F32) for i in range(3)]
    gel2 = [S2.sb("gel%d" % i, [128, 10, 384], BF16) for i in range(2)]
    xm2 = [S2.sb("xm%d" % i, [128, 8, 384], BF16) for i in range(2)]
    hx2 = [S2.sb("hx%d" % i, [128, 8, 384], F32) for i in range(2)]
    hsf = [S2.sb("hsf%d" % i, [128, 384], F32) for i in range(2)]
    hsr = [S2.sb("hsr%d" % i, [128, 384], F32) for i in range(2)]
    vb = [S2.sb("v%d" % i, [128, 10, 384], BF16) for i in range(2)]
    taus = [0]
    for nr in RG_ROWS:
        taus.append(taus[-1] + nr * 64)

    def p2_load(bi):
        N, tau = RG_ROWS[bi] * 64, taus[bi]
        xmb, hx, g2_ = xm2[bi % 2], hx2[bi % 2], gel2[bi % 2]
        K.dma("sp", xmb[:, :, 0:N], XM[:, :, tau:tau + N], XM, xmb, xmb)
        K.dma("sp", hx[:, :, 0:N], HX[:, :, tau:tau + N], HX, hx, hx)
        for oc in range(10):
            pp = PS[4 + oc % 4]
            for k in range(8):
                K.op("pe", lambda e, k=k: e.matmul(
                    pp[:, 0:N], lhsT=w_ing[:, k, oc * 128:(oc + 1) * 128], rhs=xmb[:, k, 0:N], start=(k == 0), stop=(k == 7)),
                    [w_ing, xmb], [pp], part=(k > 0), inc=(k == 7))
            K.op("act", lambda e: e.activation(out=g2_[:, oc, 0:N], in_=pp[:, 0:N], func=AF.Gelu_apprx_tanh), [pp], [g2_], part=True)

    def p2_scan(bi):
        N, tau = RG_ROWS[bi] * 64, taus[bi]
        g2_, v = gel2[bi % 2], vb[bi % 2]
        for c in range(10):
            ab = AB2[c % 3]
            K.dma("sp", ab[:, :, 0:N], SAB[c][:, :, tau:tau + N], SAB, ab, ab)
            hf_, hr_ = hsf[c % 2], hsr[c % 2]
            K.op("dve", lambda e: e.tensor_tensor_scan(
                out=hf_[:, 0:N], data0=ab[:, 0, 0:N], data1=ab[:, 1, 0:N], initial=HIN[:, 0, c, bi:bi + 1],
                op0=ALU.mult, op1=ALU.add), [ab], [hf_])
            K.op("dve", lambda e: e.tensor_tensor_scan(
                out=hr_[:, 0:N][:, ::-1], data0=ab[:, 2, 0:N][:, ::-1], data1=ab[:, 3, 0:N][:, ::-1],
                initial=HIN[:, 1, c, bi + 1:bi + 2], op0=ALU.mult, op1=ALU.add), [ab], [hr_])
            K.op("dve", lambda e: e.tensor_tensor(out=hf_[:, 0:N], in0=hf_[:, 0:N], in1=hr_[:, 0:N], op=ALU.add), [hf_, hr_], [hf_])
            K.op("dve", lambda e: e.scalar_tensor_tensor(
                out=v[:, c, 0:N], in0=hf_[:, 0:N], scalar=0.5, in1=g2_[:, c, 0:N], op0=ALU.mult, op1=ALU.mult),
                [hf_, g2_], [v], part=True)

    def p2_out_mm(bi):
        N = RG_ROWS[bi] * 64
        v = vb[bi % 2]
        for m in range(8):
            pp = PS[m % 4]
            for c in range(10):
                K.op("pe", lambda e, c=c: e.matmul(
                    pp[:, 0:N], lhsT=w_out[:, c, m * 128:(m + 1) * 128], rhs=v[:, c, 0:N], start=(c == 0), stop=(c == 9)),
                    [w_out, v], [pp], part=(c > 0), inc=(c == 9))
            hx = hx2[bi % 2]
            K.op("dve", lambda e, m=m, pp=pp, hx=hx: e.scalar_tensor_tensor(
                out=hx[:, m, 0:N], in0=pp[:, 0:N], scalar=VEC[:, V_G1, m:m + 1], in1=hx[:, m, 0:N], op0=ALU.mult, op1=ALU.add),
                [pp, hx], [hx], part=True)
        K.dma("sp", HA[:, :, taus[bi]:taus[bi] + N], hx[:, :, 0:N], hx, HA, hx, part=True)

    p2_load(0)
    p2_scan(0)
    for bi in range(NBL):
        if bi + 1 < NBL:
            p2_load(bi + 1)
        p2_out_mm(bi)
        if bi + 1 < NBL:
            p2_scan(bi + 1)
    S2.close()
    if upto == "rg":
        K.final_wait("sp")
        return nc

    def ffn_stage(li, Hin, rin0, Hout, rout0, blocks, final):
        S = Stage(K, "f%d" % li)
        NIM = 640
        hbuf = [S.sb("h%d" % i, [128, 8, NIM], F32) for i in range(2)]
        sqxb = [S.sb("sqx%d" % i, [128, 8, NIM], BF16) for i in range(2)]
        rt_ = S.sb("rt", [128, NIM], F32)
        rstd_ = S.sb("rstd", [128, NIM], F32)
        tmpf = [S.sb("tmp%d" % i, [128, NIM], F32) for i in range(2)]
        vsb = [S.sb("vs%d" % i, [128, NIM], F32) for i in range(2)]
        wup = [S.sb("wup%d" % i, [128, 8, 128], BF16) for i in range(4)]
        wdn = [S.sb("wdn%d" % i, [128, 22, 128], BF16) for i in range(2)]
        usb = [S.sb("usb%d" % i, [128, 10, 64], BF16) for i in range(3)]
        dg9 = [S.sb("dg9%d" % i, [128, 9, 128], BF16) for i in range(4)]
        sgb = [S.sb("sg%d" % i, [128, 512], F32) for i in range(2)]
        hid = S.sb("hid", [128, 22, 512], BF16)
        if final:
            yf = S.sb("yf", [128, 8, 512], F32)
            ost = [S.sb("ost%d" % i, [128, D], F32) for i in range(2)]
        gsr = lambda c: VEC[:, V_GS2 + 6 * li, c:c + 1]
        shr = lambda c: VEC[:, V_SH2 + 6 * li, c:c + 1]
        cwo = _off["ffn_cw"] + li * 396
        cbo = _off["ffn_cb"] + li * 44
        wi = 0

        def ffn_prep_steps(bi):
            r0, r1 = blocks[bi]
            NI = (r1 - r0) * 64 + 128
            h = hbuf[bi % 2]
            xmq = sqxb[bi % 2]
            t_in = (r0 - 1 - rin0) * 64
            segs_ = [(0, min(NI, 512))] + ([(512, NI)] if NI > 512 else [])
            st = []

            def s_load():
                K.dma("sp", h[:, :, 0:NI], Hin[:, :, t_in:t_in + NI], Hin, h, h)
                if (r0 - 1 < 0) or (r1 + 1 > 64):
                    xi = (r0 - 1 + 3) * 64 + 2
                    K.dma("sp", vsb[bi % 2][:, 0:NI], valid[:, xi:xi + NI], valid, vsb[bi % 2], vsb[bi % 2])
            st.append(s_load)
            st.append(lambda: K.op("act", lambda e: e.activation(out=xmq[:, :, 0:NI], in_=h[:, :, 0:NI], func=AF.Square), [h], [xmq]))
            for (a, b) in segs_:
                def s_ss(a=a, b=b):
                    for c in range(8):
                        K.op("pe", lambda e, c=c: e.matmul(PS[7][:, 0:b - a], lhsT=onesB[:], rhs=xmq[:, c, a:b], start=(c == 0), stop=(c == 7)),
                             [xmq], [PS[7]], part=(c > 0), inc=(c == 7))
                st.append(s_ss)
                st.append(lambda a=a, b=b: K.op("act", lambda e: e.activation(
                    out=rt_[:, a:b], in_=PS[7][:, 0:b - a], func=AF.Sqrt, scale=1.0 / D, bias=epsb[:, 0:1]), [PS[7]], [rt_], part=True))
            st.append(lambda: K.op("dve", lambda e: e.reciprocal(out=rstd_[:, 0:NI], in_=rt_[:, 0:NI]), [rt_], [rstd_]))
            for c in range(8):
                def s_mod(c=c):
                    tb = tmpf[c % 2]
                    K.op("dve", lambda e: e.tensor_tensor(out=tb[:, 0:NI], in0=h[:, c, 0:NI], in1=rstd_[:, 0:NI], op=ALU.mult), [h, rstd_], [tb])
                    K.op("act", lambda e: e.activation(out=xmq[:, c, 0:NI], in_=tb[:, 0:NI], func=AF.Identity, scale=gsr(c), bias=shr(c)),
                         [tb], [xmq], part=True)
                st.append(s_mod)
            return st

        for f_ in ffn_prep_steps(0):
            f_()
        for bi, (r0, r1) in enumerate(blocks):
            nr = r1 - r0
            N = nr * 64
            NI = N + 128
            h = hbuf[bi % 2]
            edge = (r0 - 1 < 0) or (r1 + 1 > 64)
            vs_ = vsb[bi % 2]
            sqx = sqxb[bi % 2]
            xmf = sqx
            def mk(i):
                j, half = i // 2, i % 2
                return dict(j=j, half=half, cc=j + 22 * half, w=wup[(wi0 + i) % 4], us=usb[(wi0 + i) % 3], dg=dg9[(wi0 + i) % 4],
                            ua=PS[((wi0 + i) % 2) * 2], ub=PS[((wi0 + i) % 2) * 2 + 1], cp=PS[4 + (wi0 + i) % 3])

            def segs(c):
                return [(pp, a, b) for (pp, a, b) in ((c["ua"], 0, min(NI, 512)), (c["ub"], 512, NI)) if b > a]

            def stU(c):
                w = c["w"]
                K.dma("sp", w[:], WUP[li][c["cc"]], WUP[li], w, w)
                K.dma("sp", c["dg"][:], DG[li][c["cc"]], DG[li], c["dg"], c["dg"])
                for (pp, a, b) in segs(c):
                    for k in range(8):
                        K.op("pe", lambda e, pp=pp, a=a, b=b, k=k: e.matmul(
                            pp[:, 0:b - a], lhsT=w[:, k, :], rhs=xmf[:, k, a:b], start=(k == 0), stop=(k == 7)),
                            [w, xmf], [pp], part=(k > 0), inc=(k == 7))

            def stE(c):
                us = c["us"]
                usf = us[:].rearrange("p r c -> p (r c)")
                for si, (pp, a, b) in enumerate(segs(c)):
                    if edge:
                        K.op("dve", lambda e, pp=pp, a=a, b=b: e.tensor_tensor(
                            out=usf[:, a:b], in0=pp[:, 0:b - a], in1=vs_[:, a:b], op=ALU.mult), [pp, vs_], [us], part=(si > 0))
                    elif si == 0:
                        K.op("act", lambda e, pp=pp, a=a, b=b: e.copy(out=usf[:, a:b], in_=pp[:, 0:b - a]), [pp], [us], part=(si > 0))
                    else:
                        K.op("dve", lambda e, pp=pp, a=a, b=b: e.tensor_copy(out=usf[:, a:b], in_=pp[:, 0:b - a]), [pp], [us], part=(si > 0))

            def stC(c):
                us, dg, cp = c["us"], c["dg"], c["cp"]
                cp3 = cp[:, 0:N].rearrange("p (r c) -> p r c", c=64)
                for n_, t in enumerate([4, 0, 1, 2, 3, 5, 6, 7, 8]):
                    dy, dx = t // 3 - 1, t % 3 - 1
                    oc0, oc1 = max(0, -dx), 64 - max(0, dx)
                    K.op("pe", lambda e, t=t, dy=dy, dx=dx, oc0=oc0, oc1=oc1, n_=n_: e.matmul(
                        cp3[:, :, oc0:oc1], lhsT=dg[:, t, :], rhs=us[:, 1 + dy:1 + dy + nr, oc0 + dx:oc1 + dx],
                        start=(n_ == 0), stop=(n_ == 8)), [dg, us], [cp], part=(n_ > 0), inc=(n_ == 8))

            def stF(cv_, cg_):
                j = cv_["j"]
                sg = sgb[j % 2]
                K.op("act", lambda e: e.activation(
                    out=sg[:, 0:N], in_=cg_["cp"][:, 0:N], func=AF.Silu, bias=prm[:, cbo + 22 + j:cbo + 23 + j], scale=1.0), [cg_["cp"]], [sg])
                K.op("dve", lambda e: e.scalar_tensor_tensor(
                    out=hid[:, j, 0:N], in0=cv_["cp"][:, 0:N], scalar=prm[:, cbo + j:cbo + j + 1], in1=sg[:, 0:N], op0=ALU.add, op1=ALU.mult),
                    [cv_["cp"], sg], [hid], part=True)

            wi0 = wi
            cfg = [mk(i) for i in range(44)]
            wi += 44
            pre_steps = ffn_prep_steps(bi + 1) if bi + 1 < len(blocks) else []
            stU(cfg[0])
            for i in range(44):
                if i + 1 < 44:
                    stU(cfg[i + 1])
                stE(cfg[i])
                stC(cfg[i])
                if i % 2 == 1:
                    stF(cfg[i - 1], cfg[i])
                if bi + 1 < len(blocks) and i >= 4 and i % 2 == 0 and pre_steps:
                    pre_steps.pop(0)()
            while pre_steps:
                pre_steps.pop(0)()
            for m in range(8):
                w = wdn[m % 2]
                K.dma("sp", w[:], WDN[li][m], WDN[li], w, w)
                pp = PS[7] if m % 2 == 0 else PS[6]
                for j in range(22):
                    K.op("pe", lambda e, pp=pp, j=j, w=w: e.matmul(
                        pp[:, 0:N], lhsT=w[:, j, :], rhs=hid[:, j, 0:N], start=(j == 0), stop=(j == 21)), [w, hid], [pp], part=(j > 0), inc=(j == 21))
                K.op("dve", lambda e, pp=pp, m=m, h=h: e.scalar_tensor_tensor(
                    out=h[:, m, 64:64 + N], in0=pp[:, 0:N], scalar=VEC[:, V_G2 + 6 * li, m:m + 1], in1=h[:, m, 64:64 + N],
                    op0=ALU.mult, op1=ALU.add), [pp, h], [h], part=True)
            if not final:
                t_out = (r0 - rout0) * 64
                K.dma("sp", Hout[:, :, t_out:t_out + N], h[:, :, 64:64 + N], h, Hout, h, part=True)
            else:
                K.op("act", lambda e, h=h: e.activation(out=sqx[:, :, 0:N], in_=h[:, :, 64:64 + N], func=AF.Square), [h], [sqx])
                for c in range(8):
                    K.op("pe", lambda e, c=c: e.matmul(PS[7][:, 0:N], lhsT=onesB[:], rhs=sqx[:, c, 0:N], start=(c == 0), stop=(c == 7)),
                         [sqx], [PS[7]], part=(c > 0), inc=(c == 7))
                K.op("act", lambda e: e.activation(out=rt_[:, 0:N], in_=PS[7][:, 0:N], func=AF.Sqrt, scale=1.0 / D, bias=epsb[:, 0:1]),
                     [PS[7]], [rt_])
                K.op("dve", lambda e: e.reciprocal(out=rstd_[:, 0:N], in_=rt_[:, 0:N]), [rt_], [rstd_])
                for c in range(8):
                    K.op("dve", lambda e, c=c, h=h: e.scalar_tensor_tensor(
                        out=yf[:, c, 0:N], in0=h[:, c, 64:64 + N], scalar=P("norm_final", 1, c), in1=rstd_[:, 0:N],
                        op0=ALU.mult, op1=ALU.mult), [h, rstd_], [yf], part=True)
                for tt in range(N // 128):
                    o = ost[tt % 2]
                    for hf in range(2):
                        pp = PS[hf]
                        for c4 in range(4):
                            c = hf * 4 + c4
                            K.op("pe", lambda e, c=c, c4=c4, tt=tt, pp=pp: e.transpose(
                                out=pp[:, c4 * 128:(c4 + 1) * 128], in_=yf[:, c, tt * 128:(tt + 1) * 128], identity=identF[:]),
                                [yf], [pp], part=(c4 > 0))
                        if hf == 0:
                            K.op("act", lambda e, o=o, pp=pp: e.copy(out=o[:, 0:512], in_=pp[:, :]), [pp], [o], part=True)
                        else:
                            K.op("dve", lambda e, o=o, pp=pp: e.tensor_copy(out=o[:, 512:1024], in_=pp[:, :]), [pp], [o], part=True)
                    tok = r0 * 64 + tt * 128
                    K.dma("sp", out_d[tok:tok + 128, :], o[:], o, out_d, o, part=True)
        S.close()

    def conf_stage(Hin, rin0, Hout, rout0, blocks):
        S = Stage(K, "c")
        w1 = S.sb("w1", [128, 8, 2 * D], BF16)
        for hf in range(2):
            K.dma("pool", w1[:, :, hf * D:(hf + 1) * D], cf_w1[:, hf * D:(hf + 1) * D].rearrange("(k p) m -> p k m", p=128),
                  cf_w1, w1, w1, part=(hf > 0))
        w2 = S.sb("w2", [128, 8, D], BF16)
        K.dma("pool", w2[:], cf_w2[:, :].rearrange("(k p) m -> p k m", p=128), cf_w2, w2, w2)
        dg31 = S.sb("dg31", [128, 8 * 31, 128], BF16)
        for c in range(8):
            for t in range(31):
                K.op("dve", lambda e, c=c, t=t: e.tensor_scalar(
                    out=dg31[:, c * 31 + t, :], in0=identB[:], scalar1=P("cf_conv_w", 1, c * 31 + t), scalar2=None, op0=ALU.mult),
                    [identB], [dg31], part=True)
        NIM = 542
        h = S.sb("h", [128, 8, NIM], F32)
        sqx = S.sb("sqx", [128, 8, NIM], BF16)
        rt_ = S.sb("rt", [128, NIM], F32)
        rstd_ = S.sb("rstd", [128, NIM], F32)
        tmpf = [S.sb("tmp%d" % i, [128, NIM], F32) for i in range(2)]
        vs_ = S.sb("vs", [128, NIM], F32)
        sgb = [S.sb("sg%d" % i, [128, NIM], F32) for i in range(2)]
        glu = S.sb("glu", [128, 8, NIM], BF16)
        cv = S.sb("cv", [128, 8, 512], F32)
        act_ = S.sb("act", [128, 8, 512], BF16)
        dsq = act_
        hn = cv
        gsr = lambda c: VEC[:, V_GS1 + 6, c:c + 1]
        shr = lambda c: VEC[:, V_SH1 + 6, c:c + 1]
        for bi, (r0, r1) in enumerate(blocks):
            N = (r1 - r0) * 64
            NI = N + 30
            t_in = (r0 - rin0) * 64 - 15
            K.dma("sp", h[:, :, 0:NI], Hin[:, :, t_in:t_in + NI], Hin, h, h)
            edge = (r0 - 1 < 0) or (r1 + 1 > 64)
            if edge:
                xi = (r0 + 3) * 64 + 2 - 15
                K.dma("sp", vs_[:, 0:NI], valid[:, xi:xi + NI], valid, vs_, vs_)
            norm_mod(S, h, NI, sqx, gsr, shr, sqx, [PS[7], PS[6]], rt_, rstd_, tmpf)
            segs = [(0, min(NI, 512))] + ([(512, NI)] if NI > 512 else [])
            for j in range(8):
                pa = [PS[0], PS[1]]
                pg = [PS[2], PS[3]]
                if j % 2 == 1:
                    pa, pg = [PS[4], PS[5]], [PS[6], PS[7]]
                for (pset, oc) in ((pg, 8 + j), (pa, j)):
                    for si, (a, b) in enumerate(segs):
                        for k in range(8):
                            K.op("pe", lambda e, pset=pset, si=si, a=a, b=b, k=k, oc=oc: e.matmul(
                                pset[si][:, 0:b - a], lhsT=w1[:, k, oc * 128:(oc + 1) * 128], rhs=sqx[:, k, a:b],
                                start=(k == 0), stop=(k == 7)), [w1, sqx], [pset[si]], part=(k > 0), inc=(k == 7))
                sg = sgb[j % 2]
                for si, (a, b) in enumerate(segs):
                    K.op("act", lambda e, si=si, a=a, b=b, sg=sg, pg=pg, j=j: e.activation(
                        out=sg[:, a:b], in_=pg[si][:, 0:b - a], func=AF.Sigmoid, bias=P("cf_b1", 1, 8 + j), scale=1.0),
                        [pg[si]], [sg], part=(si > 0))
                    K.op("dve", lambda e, si=si, a=a, b=b, sg=sg, pa=pa, j=j: e.scalar_tensor_tensor(
                        out=glu[:, j, a:b], in0=pa[si][:, 0:b - a], scalar=P("cf_b1", 1, j), in1=sg[:, a:b], op0=ALU.add, op1=ALU.mult),
                        [pa[si], sg], [glu], part=True)
                if edge:
                    K.op("dve", lambda e, j=j: e.tensor_tensor(out=glu[:, j, 0:NI], in0=glu[:, j, 0:NI], in1=vs_[:, 0:NI], op=ALU.mult),
                         [glu, vs_], [glu], part=True)
            for j in range(8):
                pp = PS[j % 4]
                for t in range(31):
                    K.op("pe", lambda e, j=j, t=t, pp=pp: e.matmul(
                        pp[:, 0:N], lhsT=dg31[:, j * 31 + t, :], rhs=glu[:, j, t:t + N], start=(t == 0), stop=(t == 30)),
                        [dg31, glu], [pp], part=(t > 0), inc=(t == 30))
                K.op("act", lambda e, j=j, pp=pp: e.activation(out=cv[:, j, 0:N], in_=pp[:, 0:N], func=AF.Identity,
                                                              bias=P("cf_conv_b", 1, j), scale=1.0), [pp], [cv], part=True)
            for j in range(8):
                K.op("pe", lambda e, j=j: e.matmul(PS[4][:, 0:N], lhsT=onesF[:], rhs=cv[:, j, 0:N], start=(j == 0), stop=(j == 7)),
                     [cv], [PS[4]], part=(j > 0), inc=(j == 7))
            for j in range(8):
                K.op("dve", lambda e, j=j: e.scalar_tensor_tensor(
                    out=cv[:, j, 0:N], in0=PS[4][:, 0:N], scalar=-1.0 / D, in1=cv[:, j, 0:N], op0=ALU.mult, op1=ALU.add),
                    [PS[4], cv], [cv], part=True)
            K.op("act", lambda e: e.activation(out=dsq[:, :, 0:N], in_=cv[:, :, 0:N], func=AF.Square), [cv], [dsq])
            for j in range(8):
                K.op("pe", lambda e, j=j: e.matmul(PS[5][:, 0:N], lhsT=onesB[:], rhs=dsq[:, j, 0:N], start=(j == 0), stop=(j == 7)),
                     [dsq], [PS[5]], part=(j > 0), inc=(j == 7))
            K.op("act", lambda e: e.activation(out=rt_[:, 0:N], in_=PS[5][:, 0:N], func=AF.Sqrt, scale=1.0 / D, bias=epsb[:, 0:1]),
                 [PS[5]], [rt_])
            K.op("dve", lambda e: e.reciprocal(out=rstd_[:, 0:N], in_=rt_[:, 0:N]), [rt_], [rstd_])
            for j in range(8):
                K.op("dve", lambda e, j=j: e.tensor_tensor(out=cv[:, j, 0:N], in0=cv[:, j, 0:N], in1=rstd_[:, 0:N], op=ALU.mult),
                     [cv, rstd_], [cv], part=True)
                K.op("act", lambda e, j=j: e.activation(out=act_[:, j, 0:N], in_=cv[:, j, 0:N], func=AF.Silu,
                                                       scale=P("cf_ln_g", 1, j), bias=P("cf_ln_b", 1, j)), [cv], [act_], part=True)
            for m in range(8):
                pp = PS[m % 4]
                for j in range(8):
                    K.op("pe", lambda e, m=m, j=j, pp=pp: e.matmul(
                        pp[:, 0:N], lhsT=w2[:, j, m * 128:(m + 1) * 128], rhs=act_[:, j, 0:N], start=(j == 0), stop=(j == 7)),
                        [w2, act_], [pp], part=(j > 0), inc=(j == 7))
                K.op("act", lambda e, m=m, pp=pp: e.activation(out=hn[:, m, 0:N], in_=pp[:, 0:N], func=AF.Identity,
                                                              scale=VEC[:, V_G1 + 6, m:m + 1], bias=VEC[:, V_BG, m:m + 1]),
                     [pp], [hn], part=True)
                K.op("dve", lambda e, m=m: e.tensor_tensor(out=hn[:, m, 0:N], in0=hn[:, m, 0:N], in1=h[:, m, 15:15 + N], op=ALU.add),
                     [hn, h], [hn], part=True)
            t_out = (r0 - rout0) * 64
            K.dma("sp", Hout[:, :, t_out:t_out + N], hn[:, :, 0:N], hn, Hout, hn, part=True)
        S.close()

    own8 = [(8 * i, 8 * i + 8) for i in range(8)]
    ffn_stage(0, HA, -3, HB, -2, [(-2, 0)] + own8 + [(64, 66)], False)
    if upto == "ffn0":
        K.final_wait("sp")
        return nc
    conf_stage(HB, -2, HC, -1, [(-1, 0)] + own8 + [(64, 65)])
    if upto == "conf":
        K.final_wait("sp")
        return nc
    ffn_stage(1, HC, -1, None, 0, own8, True)
    K.final_wait("sp")
    G.close()
    return nc


def _prep_core_inputs(core, inp, gate_tiles):
    b, k = core // 4, core % 4
    T0 = 4096 * k
    x = inp["x"]
    xin = np.zeros((NTX, D), np.float32)
    g0 = T0 - 194
    lo, hi = max(g0, 0), min(g0 + NTX, SEQ)
    xin[lo - g0:hi - g0] = x[b, lo:hi]
    valid = np.zeros((128, NTX), np.float32)
    valid[:, lo - g0:hi - g0] = 1.0
    cin = np.zeros((CTX + 3, D), np.float32)
    cin[2:2 + CTX] = inp["ctx"][b]
    cvalid = np.zeros((128, CTX + 3), np.float32)
    cvalid[:, 2:2 + CTX] = 1.0
    prm = np.zeros((128, PC), np.float32)

    def put(name, arr):
        arr = np.asarray(arr, np.float32).reshape(128, -1)
        prm[:, _off[name]:_off[name] + arr.shape[1]] = arr
    cT = np.stack([fm(inp["c"][b], 8), fm(inp["c_ctx"], 8)], axis=-1)
    put("cT", cT)
    put("ada_b", np.stack([fm(inp["ada_b"][i], 48) for i in range(2)], axis=1))
    put("norm_mix", np.stack([fm(inp["norm_mix"][i], 8) for i in range(2)], axis=1))
    put("norm_ffn", np.stack([fm(inp["norm_ffn"][i], 8) for i in range(2)], axis=1))
    put("norm_final", fm(inp["norm_final"], 8))
    put("rg_conv_w", np.stack([fm(inp["rg_conv_w"][0, t], 10) for t in range(4)], axis=-1))
    put("rg_conv_b", fm(inp["rg_conv_b"][0], 10))
    put("rg_ba", np.stack([fm(inp["rg_ba"][0, d].reshape(-1), 10) for d in range(2)], axis=1))
    put("rg_bx", np.stack([fm(inp["rg_bx"][0, d].reshape(-1), 10) for d in range(2)], axis=1))
    put("rg_lam", np.stack([fm(inp["rg_lam"][0, d], 10) for d in range(2)], axis=1))
    put("cf_b1", fm(inp["cf_b_pw1"][0], 16))
    put("cf_conv_w", np.stack([fm(inp["cf_conv_w"][0, t], 8) for t in range(31)], axis=-1))
    put("cf_conv_b", fm(inp["cf_conv_b"][0], 8))
    put("cf_ln_g", fm(inp["cf_ln_g"][0], 8))
    put("cf_ln_b", fm(inp["cf_ln_b"][0], 8))
    put("cf_b2", fm(inp["cf_b_pw2"][0], 8))
    cw = np.stack([np.stack([fm(inp["ffn_conv_w"][i].reshape(9, -1)[t], 44) for t in range(9)], axis=-1)
                   for i in range(2)], axis=1)
    put("ffn_cw", cw)
    put("ffn_cb", np.stack([fm(inp["ffn_conv_b"][i], 44) for i in range(2)], axis=1))
    m = {"xin": xin, "cin": cin, "valid": valid, "cvalid": cvalid}
    xoth = np.zeros((3, 4099, D), np.float32)
    ovalid = np.zeros((3, 128, 4099), np.float32)
    gwo = np.zeros((3, 2, NGT, 128, 128), np.float32)
    fmk = np.zeros((128, 2, 3, NBO), np.float32)
    o_lam = np.zeros((128, 3, 10), np.float32)
    o_ba = np.zeros((128, 3, 10), np.float32)
    o_bx = np.zeros((128, 3, 10), np.float32)
    for o in range(3):
        ko = (k + 1 + o) % 4
        dsel = 0 if ko < k else 1
        g0 = 4096 * ko - 2
        lo, hi = max(g0, 0), min(g0 + 4099, SEQ)
        xoth[o, lo - g0:hi - g0] = x[b, lo:hi]
        ovalid[o, :, lo - g0:hi - g0] = 1.0
        gwo[o, 0] = gate_tiles[dsel * 2 + 0]
        gwo[o, 1] = gate_tiles[dsel * 2 + 1]
        o_lam[:, o] = fm(inp["rg_lam"][0, dsel], 10)
        o_ba[:, o] = fm(inp["rg_ba"][0, dsel].reshape(-1), 10)
        o_bx[:, o] = fm(inp["rg_bx"][0, dsel].reshape(-1), 10)
        for j in range(NBO):
            if ko < k and (ko < k - 1 or j <= NBO - 2):
                fmk[:, 0, o, j] = 1.0
            if ko > k and (ko > k + 1 or j >= 1):
                fmk[:, 1, o, j] = 1.0
    put("o_lam", o_lam)
    put("o_ba", o_ba)
    put("o_bx", o_bx)
    m["prm"] = prm
    m["xoth"] = xoth
    m["ovalid"] = ovalid
    m["gwo"] = gwo.reshape(3 * 2 * NGT, 128, 128)
    m["fmask"] = fmk.reshape(128, 2, 3 * NBO)
    return m


def _shared_inputs(inp):
    gw = np.zeros((4, NGT, 128, 128), np.float32)
    mats = [inp["rg_wa"][0, 0], inp["rg_wx"][0, 0], inp["rg_wa"][0, 1], inp["rg_wx"][0, 1]]
    for gi, w in enumerate(mats):
        dense = np.zeros((DR, DR), np.float32)
        for h in range(16):
            dense[h * 80:(h + 1) * 80, h * 80:(h + 1) * 80] = w[h]
        for ti, (j, i) in enumerate(GT):
            gw[gi, ti] = dense[i * 128:(i + 1) * 128, j * 128:(j + 1) * 128]
    return gw, {
        "ident": np.eye(128, dtype=np.float32),
        "ada_w": np.ascontiguousarray(inp["ada_w"], np.float32),
        "rg_w_in": np.ascontiguousarray(inp["rg_w_in"][0], np.float32),
        "gw": gw.reshape(4 * NGT, 128, 128),
        "rg_w_out": np.ascontiguousarray(inp["rg_w_out"][0], np.float32),
        "cf_w1": np.ascontiguousarray(inp["cf_w_pw1"][0], np.float32),
        "cf_w2": np.ascontiguousarray(inp["cf_w_pw2"][0], np.float32),
        "ffn_up": np.ascontiguousarray(inp["ffn_w_up"], np.float32),
        "ffn_dn": np.ascontiguousarray(inp["ffn_w_down"], np.float32),
    }


_SUM_KEYS = ["xin", "cin", "valid", "cvalid", "prm", "ident", "ada_w", "rg_w_in", "gw"]
_PROG = {}


def _prog(mode):
    if mode not in _PROG:
        _PROG[mode] = build_program(mode)
    return _PROG[mode]


def kernel(**inputs):
    inp = {k: np.asarray(v) for k, v in inputs.items()}
    gate_tiles, shared = _shared_inputs(inp)
    percore = [_prep_core_inputs(c, inp, gate_tiles) for c in range(8)]
    cores = list(range(8))
    in2 = [{**shared, **percore[c]} for c in cores]
    r2 = run_bass_kernel_spmd(_prog("solo"), in2, core_ids=cores)
    out = np.zeros((2, SEQ, D), np.float32)
    for c in cores:
        b, k = c // 4, c % 4
        out[b, 4096 * k:4096 * (k + 1)] = np.asarray(r2.results[c]["out"])
    return out
```

```python
import numpy as np
from contextlib import ExitStack
import concourse.bass as bass
import concourse.mybir as mybir
from concourse.bass_utils import run_bass_kernel_spmd

F32, BF16 = mybir.dt.float32, mybir.dt.bfloat16
AF = mybir.ActivationFunctionType
ALU = mybir.AluOpType

D = 1024
SEQ = 16384
CTX = 256
DR = 1280
DFF = 2816
EPS = 1e-6
NTX = 4483
NTE = 4480
RG_ROWS = [3, 3, 5, 6, 6, 6, 6, 6, 6, 6, 6, 5, 3, 3]
NBL = len(RG_ROWS)
NBO = NBL - 2

def _gate_tiles():
    tiles = []
    for j in range(10):
        heads = set(range((j * 128) // 80, ((j + 1) * 128 - 1) // 80 + 1))
        ins = set()
        for h in heads:
            for ch in (h * 80, h * 80 + 79):
                ins.add(ch // 128)
        for i in sorted(ins):
            tiles.append((j, i))
    return tiles
GT = _gate_tiles()
NGT = len(GT)

_off = {}
def _alloc_cols():
    o = 0
    for name, n in [("cT", 16), ("ada_b", 96), ("norm_mix", 16), ("norm_ffn", 16), ("norm_final", 8),
                    ("rg_conv_w", 40), ("rg_conv_b", 10), ("rg_ba", 20), ("rg_bx", 20), ("rg_lam", 20), ("o_lam", 30), ("o_ba", 30), ("o_bx", 30),
                    ("cf_b1", 16), ("cf_conv_w", 248), ("cf_conv_b", 8), ("cf_ln_g", 8), ("cf_ln_b", 8),
                    ("cf_b2", 8), ("ffn_cw", 792), ("ffn_cb", 88)]:
        _off[name] = o
        o += n
    return o
PC = _alloc_cols()


def fm(v, nch):
    return np.ascontiguousarray(np.asarray(v, np.float32).reshape(nch, 128).T)


class Buf:
    def __init__(self, t, name):
        self.t = t
        self.name = name
        self.w = {}
        self.r = {}
        self.ds = None

    def __getitem__(self, i):
        return self.t[i]


class Ctx:
    def __init__(self, nc, ndsem=64):
        self.nc = nc
        self.E = {"pe": nc.tensor, "act": nc.scalar, "dve": nc.vector, "pool": nc.gpsimd, "sp": nc.sync}
        self.sem = {e: nc.alloc_semaphore("s_" + e) for e in self.E}
        self.cnt = {e: 0 for e in self.E}
        self.known = {e: {} for e in self.E}
        self.dpool = [[nc.alloc_semaphore("d%d" % i), 0, "d%d" % i] for i in range(ndsem)]
        self.dfree = list(range(ndsem))
        self.nops = 0
        self.extra = []

    def dram(self, ap, name):
        return Buf(ap, name)

    def _deps(self, reads, writes, part):
        deps = {}

        def add(d):
            for k, (s, v) in d.items():
                if k not in deps or deps[k][1] < v:
                    deps[k] = (s, v)
        for b in reads:
            add(b.w)
        for b in writes:
            if not part:
                add(b.w)
            add(b.r)
        return deps

    def _wait(self, e, deps):
        for k, (s, v) in deps.items():
            if k == "pe" and e == "pe":
                continue
            if self.known[e].get(k, 0) >= v:
                continue
            self.E[e].wait_ge(s, v)
            self.known[e][k] = v

    def op(self, e, fn, reads=(), writes=(), part=False, inc=True):
        self._wait(e, self._deps(reads, writes, part))
        ins = fn(self.E[e])
        if inc:
            self.cnt[e] += 1
            ins.then_inc(self.sem[e], 1)
            tag = (self.sem[e], self.cnt[e])
        else:
            tag = (self.sem[e], self.cnt[e] + 1)
        for b in reads:
            b.r[e] = tag
        for b in writes:
            b.w[e] = tag
        self.nops += 1
        return ins

    def _dsem(self, b):
        if b.ds is None:
            b.ds = self.dfree.pop(0)
        return self.dpool[b.ds]

    def release(self, bufs):
        for b in bufs:
            if b.ds is not None:
                self.dfree.append(b.ds)
                b.ds = None

    def dma(self, q, out, in_, src, dst, owner, part=False):
        self._wait(q, self._deps([src], [dst], part))
        ent = self._dsem(owner)
        ent[1] += 16
        self.E[q].dma_start(out=out, in_=in_).then_inc(ent[0], 16)
        tag = (ent[0], ent[1])
        src.r[ent[2]] = tag
        dst.w[ent[2]] = tag

    def all_gather(self, src, dst, groups):
        self._wait("pool", self._deps([src], [dst], False))
        sem = self.nc.alloc_semaphore("cc%d" % len(self.extra))
        ins = self.E["pool"].collective_compute("AllGather", mybir.AluOpType.bypass, replica_groups=groups,
                                                ins=[src.t.opt()], outs=[dst.t.opt()])
        ins.then_inc(sem)
        key = "cc%d" % len(self.extra)
        self.extra.append([sem, 1, key])
        src.r[key] = (sem, 1)
        dst.w[key] = (sem, 1)

    def barrier(self):
        for e in self.E:
            deps = {}
            for ent in self.extra:
                deps[ent[2]] = (ent[0], ent[1])
            for e2 in self.E:
                if e2 != e and self.cnt[e2] > 0:
                    deps[e2] = (self.sem[e2], self.cnt[e2])
            for ent in self.dpool:
                if ent[1] > 0:
                    deps[ent[2]] = (ent[0], ent[1])
            for k, (s, v) in deps.items():
                if self.known[e].get(k, 0) >= v:
                    continue
                self.E[e].wait_ge(s, v)
                self.known[e][k] = v

    def final_wait(self, e="sp"):
        for ent in self.dpool:
            if ent[1] > 0 and self.known[e].get(ent[2], 0) < ent[1]:
                self.E[e].wait_ge(ent[0], ent[1])
                self.known[e][ent[2]] = ent[1]


class Stage:
    def __init__(self, K, name):
        self.K = K
        self.name = name
        self.stack = ExitStack()
        self.bufs = []
        self.n = 0

    def sb(self, name, shape, dt):
        self.n += 1
        t = self.stack.enter_context(self.K.nc.sbuf_tensor("%s_%s_%d" % (self.name, name, self.n), list(shape), dt))
        b = Buf(t, name)
        self.bufs.append(b)
        return b

    def close(self):
        self.K.barrier()
        self.K.release(self.bufs)
        self.stack.close()


def build_program(mode, upto="all", debug=False):
    nc = bass.Bass("TRN2", target_bir_lowering=False)
    K = Ctx(nc)

    def din(name, shape, dt=F32):
        return K.dram(nc.dram_tensor(name, list(shape), dt, kind="ExternalInput").ap(), name)

    def dint(name, shape, dt=F32):
        kind = "ExternalOutput" if (debug and name in ("HA", "HB", "HC")) else "Internal"
        return K.dram(nc.dram_tensor(name, list(shape), dt, kind=kind).ap(), name)

    def dout(name, shape, dt=F32):
        return K.dram(nc.dram_tensor(name, list(shape), dt, kind="ExternalOutput").ap(), name)

    xin = din("xin", [NTX, D])
    cin = din("cin", [CTX + 3, D])
    valid = din("valid", [128, NTX])
    cvalid = din("cvalid", [128, CTX + 3])
    prm_d = din("prm", [128, PC])
    ident_d = din("ident", [128, 128])
    ada_w = din("ada_w", [2, D, 6 * D])
    rg_w_in = din("rg_w_in", [D, 2 * DR])
    gw_d = din("gw", [4 * NGT, 128, 128])
    if mode != "sum":
        rg_w_out = din("rg_w_out", [DR, D])
        cf_w1 = din("cf_w1", [D, 2 * D])
        cf_w2 = din("cf_w2", [D, D])
        ffn_up = din("ffn_up", [2, D, 2 * DFF])
        ffn_dn = din("ffn_dn", [2, DFF, D])
        fmask = din("fmask", [128, 2, (3 if mode == "solo" else 8) * NBO])
        out_d = dout("out", [4096, D])
    SUMW = 2 * 2 * 10 * NBO
    NS = 3 if mode == "solo" else 8
    if mode == "solo":
        xoth = din("xoth", [3, 4099, D])
        ovalid = din("ovalid", [3, 128, 4099])
        gwo_d = din("gwo", [3 * 2 * NGT, 128, 128])
    if mode == "sum":
        sum_out = dout("sums", [128, SUMW])
    elif mode == "main":
        sumg_d = din("sumg", [128, 8, SUMW])
    elif mode == "solo":
        pass
    else:
        sum_loc_d = K.dram(nc.dram_tensor("sum_loc", [128, SUMW], F32).ap(), "sum_loc")
        sumg_all = K.dram(nc.dram_tensor("sumg_all", [8 * 128, SUMW], F32).ap(), "sumg_all")

    if mode != "sum":
        SAB = dint("SAB", [10, 128, 4, NTE])
        XM = dint("XM", [128, 8, NTE], BF16)
        HX = dint("HX", [128, 8, NTE])
        HA = dint("HA", [128, 8, NTE])
        HB = dint("HB", [128, 8, 68 * 64])
        HC = dint("HC", [128, 8, 66 * 64])
        WUP = [dint("WUP%d" % i, [44, 128, 8, 128], BF16) for i in range(2)]
        WDN = [dint("WDN%d" % i, [8, 128, 22, 128], BF16) for i in range(2)]
        DG = [dint("DG%d" % i, [44, 128, 9, 128], BF16) for i in range(2)]
        DG31 = dint("DG31", [8, 128, 31, 128], BF16)

    PS = [Buf(nc.alloc_psum_tensor("ps%d" % i, [128, 512], F32), "ps%d" % i) for i in range(8)]

    G = Stage(K, "g")
    prm = G.sb("prm", [128, PC], F32)
    identF = G.sb("identF", [128, 128], F32)
    identB = G.sb("identB", [128, 128], BF16)
    onesB = G.sb("onesB", [128, 128], BF16)
    onesF = G.sb("onesF", [128, 128], F32)
    MOD = G.sb("MOD", [128, 2, 2, 6, 8], F32)
    VEC = G.sb("VEC", [128, 16, 8], F32)
    SUML = G.sb("SUML", [128, 2, 2, 10, NBL], F32)
    HCTX = G.sb("HCTX", [128, 2, 10], F32)
    HIN = G.sb("HIN", [128, 2, 10, NBL + 1], F32)
    SCV = G.sb("SCV", [128, 6, 10], F32)
    HBG = G.sb("HBG", [128, 4, 10], F32)
    SCVO = G.sb("SCVO", [128, 6, 10], F32)
    HBGO = G.sb("HBGO", [128, 6, 10], F32)
    SUMO = G.sb("SUMO", [128, 3, 2, 2, 10, NBO], F32)
    accO = G.sb("accO", [128, 3, 10, NBO], F32)

    def P(name, n=None, i=0):
        o = _off[name] + i
        return prm[:, o:o + (n if n is not None else 1)]

    V_GS1, V_SH1, V_G1, V_GS2, V_SH2, V_G2 = 0, 1, 2, 3, 4, 5
    V_GSC, V_SHC, V_BG = 12, 13, 14

    K.dma("sp", prm[:], prm_d[:], prm_d, prm, prm)
    K.dma("sp", identF[:], ident_d[:], ident_d, identF, identF)
    K.op("dve", lambda e: e.tensor_copy(out=identB[:], in_=identF[:]), [identF], [identB])
    K.op("dve", lambda e: e.memset(onesB[:], 1.0), [], [onesB])
    K.op("dve", lambda e: e.memset(onesF[:], 1.0), [], [onesF])
    K.op("dve", lambda e: e.memset(SUML[:], 0.0), [], [SUML])

    S0 = Stage(K, "p")
    scb = S0.sb("scb", [128, 16], BF16)
    K.op("act", lambda e: e.activation(out=scb[:], in_=P("cT", 16), func=AF.Silu), [prm], [scb])
    adaw = [S0.sb("adaw%d" % i, [128, 8, 1024], BF16) for i in range(2)]
    it = 0
    for li in range(2):
        for q in range(6):
            wt = adaw[it % 2]
            K.dma("pool", wt[:], ada_w[li][:, q * 1024:(q + 1) * 1024].rearrange("(k p) m -> p k m", p=128),
                  ada_w, wt, wt)
            pp = PS[it % 2]
            for m in range(8):
                for k in range(8):
                    K.op("pe", lambda e, m=m, k=k, wt=wt, pp=pp: e.matmul(
                        pp[:, m * 2:m * 2 + 2], lhsT=wt[:, k, m * 128:(m + 1) * 128], rhs=scb[:, k * 2:k * 2 + 2],
                        start=(k == 0), stop=(k == 7)), [wt, scb], [pp], part=(k > 0 or m > 0), inc=(k == 7))
            for j in range(2):
                K.op("dve", lambda e, j=j, li=li, q=q, pp=pp: e.tensor_tensor(
                    out=MOD[:, li, j, q, :], in0=pp[:, j:16:2], in1=P("ada_b", 8, li * 48 + q * 8), op=ALU.add),
                    [pp, prm], [MOD], part=True)
            it += 1
    for li in range(2):
        K.op("dve", lambda e, li=li: e.scalar_tensor_tensor(
            out=VEC[:, V_GS1 + 6 * li, :], in0=MOD[:, li, 0, 1, :], scalar=1.0, in1=P("norm_mix", 8, li * 8),
            op0=ALU.add, op1=ALU.mult), [MOD, prm], [VEC], part=True)
        K.op("dve", lambda e, li=li: e.scalar_tensor_tensor(
            out=VEC[:, V_GS2 + 6 * li, :], in0=MOD[:, li, 0, 4, :], scalar=1.0, in1=P("norm_ffn", 8, li * 8),
            op0=ALU.add, op1=ALU.mult), [MOD, prm], [VEC], part=True)
        for vrow, q in ((V_SH1, 0), (V_G1, 2), (V_SH2, 3), (V_G2, 5)):
            K.op("dve", lambda e, li=li, vrow=vrow, q=q: e.tensor_copy(out=VEC[:, vrow + 6 * li, :], in_=MOD[:, li, 0, q, :]),
                 [MOD], [VEC], part=True)
    K.op("dve", lambda e: e.scalar_tensor_tensor(
        out=VEC[:, V_GSC, :], in0=MOD[:, 0, 1, 1, :], scalar=1.0, in1=P("norm_mix", 8, 0),
        op0=ALU.add, op1=ALU.mult), [MOD, prm], [VEC], part=True)
    K.op("dve", lambda e: e.tensor_copy(out=VEC[:, V_SHC, :], in_=MOD[:, 0, 1, 0, :]), [MOD], [VEC], part=True)
    K.op("dve", lambda e: e.tensor_tensor(out=VEC[:, V_BG, :], in0=P("cf_b2", 8), in1=MOD[:, 1, 0, 2, :], op=ALU.mult),
         [MOD, prm], [VEC], part=True)
    tmp20 = S0.sb("tmp20", [128, 50], F32)
    ev20 = S0.sb("ev20", [128, 50], F32)
    q20 = S0.sb("q20", [128, 50], F32)
    K.op("act", lambda e: e.activation(out=ev20[:], in_=P("rg_lam", 50), func=AF.Exp, scale=-1.0), [prm], [ev20])
    K.op("dve", lambda e: e.tensor_scalar(out=q20[:], in0=ev20[:], scalar1=-1.0 / 7, scalar2=1.0 / 6, op0=ALU.mult, op1=ALU.add),
         [ev20], [q20])
    for cst in (1.0 / 5, 1.0 / 4, 1.0 / 3, 1.0 / 2, 1.0):
        K.op("dve", lambda e: e.tensor_tensor(out=q20[:], in0=q20[:], in1=ev20[:], op=ALU.mult), [q20, ev20], [q20])
        K.op("dve", lambda e, cst=cst: e.tensor_scalar(out=q20[:], in0=q20[:], scalar1=-1.0, scalar2=cst, op0=ALU.mult, op1=ALU.add),
             [q20], [q20])
    K.op("dve", lambda e: e.tensor_tensor(out=tmp20[:], in0=q20[:], in1=ev20[:], op=ALU.mult), [q20, ev20], [tmp20])
    K.op("dve", lambda e: e.tensor_scalar(out=SCV[:, 0:2, :], in0=tmp20[:, 0:20].rearrange("p (d c) -> p d c", d=2),
                                          scalar1=-4.0, scalar2=None, op0=ALU.mult), [tmp20], [SCV], part=True)
    K.op("dve", lambda e: e.tensor_scalar(out=SCV[:, 2:4, :], in0=tmp20[:, 0:20].rearrange("p (d c) -> p d c", d=2),
                                          scalar1=-8.0, scalar2=None, op0=ALU.mult), [tmp20], [SCV], part=True)
    K.op("dve", lambda e: e.tensor_scalar(out=SCVO[:, 0:3, :], in0=tmp20[:, 20:50].rearrange("p (d c) -> p d c", d=3),
                                          scalar1=-4.0, scalar2=None, op0=ALU.mult), [tmp20], [SCVO], part=True)
    K.op("dve", lambda e: e.tensor_scalar(out=SCVO[:, 3:6, :], in0=tmp20[:, 20:50].rearrange("p (d c) -> p d c", d=3),
                                          scalar1=-8.0, scalar2=None, op0=ALU.mult), [tmp20], [SCVO], part=True)
    for o in range(3):
        K.op("dve", lambda e, o=o: e.tensor_scalar(out=HBGO[:, o * 2, :], in0=P("o_ba", 10, o * 10), scalar1=0.5,
                                                     scalar2=None, op0=ALU.mult), [prm], [HBGO], part=True)
        K.op("dve", lambda e, o=o: e.tensor_scalar(out=HBGO[:, o * 2 + 1, :], in0=P("o_bx", 10, o * 10), scalar1=0.5,
                                                     scalar2=None, op0=ALU.mult), [prm], [HBGO], part=True)
    for d in range(2):
        K.op("dve", lambda e, d=d: e.tensor_scalar(out=HBG[:, d * 2, :], in0=P("rg_ba", 10, d * 10), scalar1=0.5,
                                                     scalar2=None, op0=ALU.mult), [prm], [HBG], part=True)
        K.op("dve", lambda e, d=d: e.tensor_scalar(out=HBG[:, d * 2 + 1, :], in0=P("rg_bx", 10, d * 10), scalar1=0.5,
                                                     scalar2=None, op0=ALU.mult), [prm], [HBG], part=True)
    if mode != "sum":
        dgst = [S0.sb("dgst%d" % i, [128, 9, 128], BF16) for i in range(3)]
        for li in range(2):
            cwo_ = _off["ffn_cw"] + li * 396
            for cc in range(44):
                dgs = dgst[cc % 3]
                for t in range(9):
                    K.op("dve", lambda e, t=t, dgs=dgs, cc=cc, cwo_=cwo_: e.tensor_scalar(
                        out=dgs[:, t, :], in0=identB[:], scalar1=prm[:, cwo_ + cc * 9 + t:cwo_ + cc * 9 + t + 1], scalar2=None,
                        op0=ALU.mult), [identB, prm], [dgs], part=(t > 0))
                K.dma("sp", DG[li][cc], dgs[:], dgs, DG[li], dgs, part=True)
    if mode != "sum":
        dg31s = [S0.sb("dg31s%d" % i, [128, 31, 128], BF16) for i in range(2)]
        for c in range(8):
            dgs = dg31s[c % 2]
            for t in range(31):
                K.op("dve", lambda e, c=c, t=t, dgs=dgs: e.tensor_scalar(
                    out=dgs[:, t, :], in0=identB[:], scalar1=P("cf_conv_w", 1, c * 31 + t), scalar2=None, op0=ALU.mult),
                    [identB, prm], [dgs], part=(t > 0))
            K.dma("sp", DG31[c], dgs[:], dgs, DG31, dgs, part=True)
    S0.close()

    def norm_mod(S, h, n, xm, gsrow, shrow, sq, ssb, rt, rstd, tmp, out_f32=None):
        K.op("act", lambda e: e.activation(out=sq[:, :, 0:n], in_=h[:, :, 0:n], func=AF.Square), [h], [sq])
        nseg = [(0, min(n, 512))] + ([(512, n)] if n > 512 else [])
        for si, (a, b) in enumerate(nseg):
            for c in range(8):
                K.op("pe", lambda e, c=c, a=a, b=b, si=si: e.matmul(
                    ssb[si][:, 0:b - a], lhsT=onesB[:], rhs=sq[:, c, a:b], start=(c == 0), stop=(c == 7)),
                    [sq], [ssb[si]], part=(c > 0), inc=(c == 7))
            K.op("act", lambda e, a=a, b=b, si=si: e.activation(
                out=rt[:, a:b], in_=ssb[si][:, 0:b - a], func=AF.Sqrt, scale=1.0 / D, bias=epsb[:, 0:1]),
                [ssb[si]], [rt], part=True)
        K.op("dve", lambda e: e.reciprocal(out=rstd[:, 0:n], in_=rt[:, 0:n]), [rt], [rstd])
        for c in range(8):
            tb = tmp[c % 2]
            K.op("dve", lambda e, c=c, tb=tb: e.tensor_tensor(out=tb[:, 0:n], in0=h[:, c, 0:n], in1=rstd[:, 0:n], op=ALU.mult),
                 [h, rstd], [tb])
            if out_f32 is None:
                K.op("act", lambda e, c=c, tb=tb: e.activation(
                    out=xm[:, c, 0:n], in_=tb[:, 0:n], func=AF.Identity, scale=gsrow(c), bias=shrow(c)),
                    [tb], [xm], part=True)
            else:
                K.op("act", lambda e, c=c, tb=tb: e.activation(
                    out=out_f32[:, c, 0:n], in_=tb[:, 0:n], func=AF.Identity, scale=gsrow(c)),
                    [tb], [out_f32], part=True)

    epsb = G.sb("epsb", [128, 2], F32)
    K.op("dve", lambda e: e.memset(epsb[:, 0:1], EPS), [], [epsb], part=True)
    K.op("dve", lambda e: e.memset(epsb[:, 1:2], 1.0), [], [epsb], part=True)

    S1 = Stage(K, "r1")
    w_in = S1.sb("w_in", [128, 8, DR], BF16)
    K.dma("pool", w_in[:], rg_w_in[:, DR:2 * DR].rearrange("(k p) m -> p k m", p=128), rg_w_in, w_in, w_in)
    gwt = S1.sb("gw", [128, 4 * NGT, 128], BF16)
    K.dma("pool", gwt[:], gw_d[:, :, :].rearrange("t p m -> p t m"), gw_d, gwt, gwt)
    dg4 = S1.sb("dg4", [128, 40, 128], BF16)
    for c in range(10):
        for t in range(4):
            K.op("dve", lambda e, c=c, t=t: e.tensor_scalar(
                out=dg4[:, c * 4 + t, :], in0=identB[:], scalar1=P("rg_conv_w", 1, c * 4 + t), scalar2=None, op0=ALU.mult),
                [identB, prm], [dg4], part=True)
    NM = 387
    xt4 = S1.sb("xt4", [128, 4, D], F32)
    hb_ = S1.sb("h", [128, 8, NM], F32)
    xm = S1.sb("xm", [128, 8, NM], BF16)
    rt = S1.sb("rt", [128, NM], F32)
    rstd = S1.sb("rstd", [128, NM], F32)
    tmpn = [S1.sb("tmpn%d" % i, [128, NM], F32) for i in range(2)]
    vslb = [S1.sb("vsl%d" % i, [128, NM], F32) for i in range(2)]
    uxb = S1.sb("uxb", [128, 10, NM], BF16)
    uxcb = [S1.sb("uxc%d" % i, [128, 10, 384], BF16) for i in range(2)]
    ABs = [S1.sb("AB%d" % i, [128, 4, 384], F32) for i in range(4)]
    trb = [S1.sb("tr%d" % i, [128, 384], F32) for i in range(8)]
    tib = [S1.sb("ti%d" % i, [128, 384], F32) for i in range(8)]
    a2b = [S1.sb("a2%d" % i, [128, 384], F32) for i in range(8)]
    hsb = [S1.sb("hs%d" % i, [128, 384], F32) for i in range(2)]
    accb = S1.sb("acc", [128, 2, 10, NBL + 1], F32)

    if debug:
        print("S1 sbuf remaining", nc.sbuf_bytes_remaining)
    GTI = {}
    for ti, (j, i) in enumerate(GT):
        GTI.setdefault(j, []).append((ti, i))
    AX = mybir.AxisListType.X

    def rgA(B):
        src, vsrc, i0, N, par, store, tau0, is_ctx = B["src"], B["vsrc"], B["i0"], B["N"], B["par"], B["store"], B["tau0"], B["blk"] is None
        N3 = N + 3
        ntile = (N3 + 127) // 128
        vsl, uxc = vslb[par], uxcb[par]
        steps = []

        def s_load():
            for t in range(ntile):
                nt = min(128, N3 - 128 * t)
                K.dma("sp", xt4[0:nt, t, :], src[i0 - 2 + 128 * t:i0 - 2 + 128 * t + nt, :], src, xt4, xt4, part=(t > 0))
            K.dma("sp", vsl[:, 0:N3], vsrc[:, i0 - 2:i0 - 2 + N3], vsrc, vsl, vsl)
        steps.append(s_load)

        def s_tr(c):
            pp = PS[c % 2]
            for t in range(ntile):
                nt = min(128, N3 - 128 * t)
                K.op("pe", lambda e, t=t, nt=nt: e.transpose(
                    out=pp[:, 128 * t:128 * t + nt], in_=xt4[0:nt, t, c * 128:(c + 1) * 128], identity=identF[0:nt, 0:nt]),
                    [xt4, identF], [pp], part=(t > 0))
            if c % 2 == 0:
                K.op("act", lambda e: e.copy(out=hb_[:, c, 0:N3], in_=pp[:, 0:N3]), [pp], [hb_], part=True)
            else:
                K.op("dve", lambda e: e.tensor_copy(out=hb_[:, c, 0:N3], in_=pp[:, 0:N3]), [pp], [hb_], part=True)
        for c in range(8):
            steps.append(lambda c=c: s_tr(c))
        if is_ctx:
            gs = lambda c: VEC[:, V_GSC, c:c + 1]
            sh = lambda c: VEC[:, V_SHC, c:c + 1]
        else:
            gs = lambda c: VEC[:, V_GS1, c:c + 1]
            sh = lambda c: VEC[:, V_SH1, c:c + 1]

        def s_stat():
            K.op("act", lambda e: e.activation(out=xm[:, :, 0:N3], in_=hb_[:, :, 0:N3], func=AF.Square), [hb_], [xm])
            for c in range(8):
                K.op("pe", lambda e, c=c: e.matmul(PS[2][:, 0:N3], lhsT=onesB[:], rhs=xm[:, c, 0:N3], start=(c == 0), stop=(c == 7)),
                     [xm], [PS[2]], part=(c > 0), inc=(c == 7))
            K.op("act", lambda e: e.activation(out=rt[:, 0:N3], in_=PS[2][:, 0:N3], func=AF.Sqrt, scale=1.0 / D, bias=epsb[:, 0:1]),
                 [PS[2]], [rt])
            K.op("dve", lambda e: e.reciprocal(out=rstd[:, 0:N3], in_=rt[:, 0:N3]), [rt], [rstd])
        steps.append((s_stat, "stat"))

        def s_mod(c):
            tb = tmpn[c % 2]
            K.op("dve", lambda e: e.tensor_tensor(out=tb[:, 0:N3], in0=hb_[:, c, 0:N3], in1=rstd[:, 0:N3], op=ALU.mult), [hb_, rstd], [tb])
            K.op("act", lambda e: e.activation(out=xm[:, c, 0:N3], in_=tb[:, 0:N3], func=AF.Identity, scale=gs(c), bias=sh(c)),
                 [tb], [xm], part=True)
        for c in range(8):
            steps.append(lambda c=c: s_mod(c))
        if store:
            def s_store():
                K.dma("pool", HX[:, :, tau0:tau0 + N], hb_[:, :, 2:2 + N], hb_, HX, hb_, part=True)
                K.dma("pool", XM[:, :, tau0:tau0 + N], xm[:, :, 2:2 + N], xm, XM, xm, part=True)
            steps.append(s_store)

        def s_win(oc):
            pp = PS[oc % 2]
            for k in range(8):
                K.op("pe", lambda e, k=k: e.matmul(pp[:, 0:N3], lhsT=w_in[:, k, oc * 128:(oc + 1) * 128], rhs=xm[:, k, 0:N3],
                                                   start=(k == 0), stop=(k == 7)), [w_in, xm], [pp], part=(k > 0), inc=(k == 7))
            K.op("dve", lambda e: e.tensor_tensor(out=uxb[:, oc, 0:N3], in0=pp[:, 0:N3], in1=vsl[:, 0:N3], op=ALU.mult),
                 [pp, vsl], [uxb], part=True)
        for oc in range(10):
            steps.append(lambda oc=oc: s_win(oc))

        def s_conv(c):
            pp = PS[2] if c % 2 == 0 else PS[5]
            for t in range(4):
                K.op("pe", lambda e, t=t: e.matmul(pp[:, 0:N], lhsT=dg4[:, c * 4 + t, :], rhs=uxb[:, c, t:t + N], start=(t == 0), stop=(t == 3)),
                     [dg4, uxb], [pp], part=(t > 0), inc=(t == 3))
            K.op("act", lambda e: e.activation(out=uxc[:, c, 0:N], in_=pp[:, 0:N], func=AF.Identity, bias=P("rg_conv_b", 1, c), scale=1.0),
                 [pp], [uxc], part=True)
        for c in range(10):
            steps.append(lambda c=c: s_conv(c))
        return steps

    def rgB(B):
        N, par, blk, edge, store, tau0, other = B["N"], B["par"], B["blk"], B["edge"], B["store"], B["tau0"], B["other"]
        vsl, uxc = vslb[par], uxcb[par]
        if other is None:
            cfgs = []
            for d in range(2):
                cfgs.append(dict(
                    tb=[(d * 2 + g) * NGT for g in range(2)], hb=(lambda g, c, d=d: HBG[:, d * 2 + g, c:c + 1]),
                    s05=(lambda c, d=d: SCV[:, d, c:c + 1]), s1=(lambda c, d=d: SCV[:, 2 + d, c:c + 1]), plane=d, pa=(lambda c, d=d: 2 * d),
                    acc=(lambda c, d=d: accb[:, d, c, NBL:NBL + 1] if blk is None else accb[:, d, c, blk:blk + 1]),
                    scans=[(d == 1, (lambda c, d=d: HCTX[:, d, c:c + 1] if blk is None else SUML[:, d, 1, c, blk:blk + 1]),
                            HCTX if blk is None else SUML)]))
            gsz = 2
        else:
            o = other
            cfgs = [dict(
                tb=[g * NGT for g in range(2)], hb=(lambda g, c: HBGO[:, o * 2 + g, c:c + 1]),
                s05=(lambda c: SCVO[:, o, c:c + 1]), s1=(lambda c: SCVO[:, 3 + o, c:c + 1]), plane=0, pa=(lambda c: 2 * ((c // 4) % 2)),
                acc=(lambda c: accO[:, o, c, blk:blk + 1]),
                scans=[(False, (lambda c: SUMO[:, o, 0, 1, c, blk:blk + 1]), SUMO),
                       (True, (lambda c: SUMO[:, o, 1, 1, c, blk:blk + 1]), SUMO)])]
            gsz = 3
        steps = []

        def s_p1(ui, c, cf):
            for g in range(2):
                pp = PS[3 + g] if ui % 2 == 0 else PS[6 + g]
                tl = GTI[c]
                for n_, (ti, i) in enumerate(tl):
                    K.op("pe", lambda e, ti=ti, i=i, n_=n_: e.matmul(
                        pp[:, 0:N], lhsT=gwt[:, cf["tb"][g] + ti, :], rhs=uxc[:, i, 0:N],
                        start=(n_ == 0), stop=(n_ == len(tl) - 1)), [gwt, uxc], [pp], part=(n_ > 0), inc=(n_ == len(tl) - 1))
                dst = trb[ui] if g == 0 else tib[ui]
                K.op("act", lambda e, g=g, dst=dst: e.activation(
                    out=dst[:, 0:N], in_=pp[:, 0:N], func=AF.Tanh, scale=0.5, bias=cf["hb"](g, c)), [pp], [dst])

        def s_p2(ui, c, cf):
            ab = ABs[c % 4]
            d = cf["plane"]
            if edge:
                K.op("dve", lambda e: e.scalar_tensor_tensor(
                    out=trb[ui][:, 0:N], in0=trb[ui][:, 0:N], scalar=1.0, in1=vsl[:, 2:2 + N], op0=ALU.add, op1=ALU.mult,
                    accum_out=cf["acc"](c)), [trb[ui], vsl], [trb[ui], accb], part=True)
            else:
                K.op("dve", lambda e: e.tensor_scalar(
                    out=trb[ui][:, 0:N], in0=trb[ui][:, 0:N], scalar1=1.0, scalar2=None, op0=ALU.add, op1=ALU.add,
                    accum_out=cf["acc"](c)), [trb[ui]], [trb[ui], accb], part=True)
            pa = cf["pa"](c)
            K.op("act", lambda e: e.activation(out=ab[:, pa, 0:N], in_=trb[ui][:, 0:N], func=AF.Exp, scale=cf["s05"](c)),
                 [trb[ui]], [ab], part=(other is not None or d > 0))
            K.op("act", lambda e: e.activation(out=a2b[ui][:, 0:N], in_=trb[ui][:, 0:N], func=AF.Exp, scale=cf["s1"](c)),
                 [trb[ui]], [a2b[ui]])
            K.op("dve", lambda e: e.scalar_tensor_tensor(
                out=tib[ui][:, 0:N], in0=tib[ui][:, 0:N], scalar=1.0, in1=uxc[:, c, 0:N], op0=ALU.add, op1=ALU.mult),
                [tib[ui], uxc], [tib[ui]])

        def s_p3(uis):
            for ui in uis:
                K.op("act", lambda e, ui=ui: e.activation(
                    out=a2b[ui][:, 0:N], in_=a2b[ui][:, 0:N], func=AF.Sqrt, scale=-1.0, bias=epsb[:, 1:2]), [a2b[ui]], [a2b[ui]])

        def s_p4(ui, c, cf):
            ab = ABs[c % 4]
            d = cf["plane"]
            pa = cf["pa"](c)
            K.op("dve", lambda e: e.scalar_tensor_tensor(
                out=ab[:, pa + 1, 0:N], in0=a2b[ui][:, 0:N], scalar=0.0, in1=tib[ui][:, 0:N], op0=ALU.max, op1=ALU.mult),
                [tib[ui], a2b[ui]], [ab], part=True)
            for si, (rev, dstf, dstbuf) in enumerate(cf["scans"]):
                hs = hsb[(d + si) % 2]
                if not rev:
                    K.op("dve", lambda e, hs=hs: e.tensor_tensor_scan(
                        out=hs[:, 0:N], data0=ab[:, pa, 0:N], data1=ab[:, pa + 1, 0:N], initial=0.0,
                        op0=ALU.mult, op1=ALU.add), [ab], [hs])
                    src_col = hs[:, N - 1:N]
                else:
                    K.op("dve", lambda e, hs=hs: e.tensor_tensor_scan(
                        out=hs[:, 0:N][:, ::-1], data0=ab[:, pa, 0:N][:, ::-1], data1=ab[:, pa + 1, 0:N][:, ::-1],
                        initial=0.0, op0=ALU.mult, op1=ALU.add), [ab], [hs])
                    src_col = hs[:, 0:1]
                K.op("dve", lambda e, src_col=src_col, dstc=dstf(c): e.tensor_copy(out=dstc, in_=src_col), [hs], [dstbuf], part=True)
            if store and d == 1:
                K.dma("pool", SAB[c][:, :, tau0:tau0 + N], ab[:, :, 0:N], ab, SAB, ab, part=True)

        groups = []
        for gi, c0 in enumerate(range(0, 10, gsz)):
            groups.append([((gi % 2) * 4 + ui, c, cf) for ui, (c, cf) in
                           enumerate([(c, cf) for c in range(c0, min(10, c0 + gsz)) for cf in cfgs])])

        def P1(g):
            return [((lambda u=u: s_p1(*u)), "b") for u in groups[g]]

        def P2(g):
            return [((lambda u=u: s_p2(*u)), "b") for u in groups[g]]

        def P3(g):
            return [((lambda uis=[u[0] for u in groups[g]]: s_p3(uis)), "sqrt")]

        def P4(g):
            return [((lambda u=u: s_p4(*u)), "b") for u in groups[g]]
        steps = P1(0) + P2(0)
        for g in range(len(groups)):
            if g + 1 < len(groups):
                steps += P1(g + 1)
            steps += P3(g)
            if g + 1 < len(groups):
                steps += P2(g + 1)
            steps += P4(g)
        return steps

    blist = [dict(src=cin, vsrc=cvalid, i0=2, N=CTX, blk=None, edge=True, store=False, tau0=0, other=None, pre=None)]
    tau = 0
    for bi, nr in enumerate(RG_ROWS):
        blist.append(dict(src=xin, vsrc=valid, i0=tau + 2, N=nr * 64, blk=bi, edge=bi in (0, NBL - 1), store=(mode != "sum"),
                          tau0=tau, other=None, pre=None))
        tau += nr * 64
    if mode == "solo":
        for o in range(3):
            tau = 0
            for j, nr in enumerate(RG_ROWS[1:-1]):
                blist.append(dict(src=Buf(xoth.t[o], "xo"), vsrc=Buf(ovalid.t[o], "vo"), i0=tau + 2, N=nr * 64, blk=j,
                                  edge=j in (0, NBO - 1), store=False, tau0=0, other=o, pre=(o if j == 0 else None)))
                tau += nr * 64
    for i, B in enumerate(blist):
        B["par"] = i % 2
    def astep(st):
        return st if isinstance(st, tuple) else (st, "a")
    for st in rgA(blist[0]):
        astep(st)[0]()
    for i, B in enumerate(blist):
        if B["pre"] is not None:
            o = B["pre"]
            K.dma("pool", gwt[:, 0:2 * NGT, :], gwo_d[o * 2 * NGT:(o + 1) * 2 * NGT, :, :].rearrange("t p m -> p t m"),
                  gwo_d, gwt, gwt)
        sb_ = rgB(B)
        sa_ = [astep(st) for st in rgA(blist[i + 1])] if i + 1 < len(blist) else []
        ia = 0
        for ib, (fb, tagb) in enumerate(sb_):
            fb()
            tgt = ((ib + 1) * len(sa_)) // len(sb_)
            while ia < tgt:
                fa, taga = sa_[ia]
                if taga == "stat" and tagb != "sqrt" and any(t == "sqrt" for (_, t) in sb_[ib + 1:]):
                    break
                fa()
                ia += 1
        while ia < len(sa_):
            sa_[ia][0]()
            ia += 1
    for d in range(2):
        for c in range(10):
            K.op("act", lambda e, d=d, c=c: e.activation(
                out=SUML[:, d, 0, c, :], in_=accb[:, d, c, 0:NBL], func=AF.Exp, scale=SCV[:, d, c:c + 1]),
                [accb], [SUML], part=True)
    if mode == "solo":
        for o in range(3):
            for c in range(10):
                for d in range(2):
                    K.op("act", lambda e, d=d, c=c, o=o: e.activation(
                        out=SUMO[:, o, d, 0, c, :], in_=accO[:, o, c, :], func=AF.Exp, scale=SCVO[:, o, c:c + 1]),
                        [accb], [SUMO], part=True)
    S1.close()

    SX = Stage(K, "x")
    if mode == "sum":
        stg = SX.sb("stg", [128, 2, 2, 10, NBO], F32)
        K.op("dve", lambda e: e.tensor_copy(out=stg[:], in_=SUML[:, :, :, :, 1:NBL - 1]), [SUML], [stg])
        K.dma("sp", sum_out[:], stg[:].rearrange("p a b c j -> p (a b c j)"), stg, sum_out, stg)
        K.final_wait("sp")
        SX.close()
        G.close()
        return nc

    sumg = SUMO if mode == "solo" else SX.sb("sumg", [128, 8, 2, 2, 10, NBO], F32)
    if mode == "solo":
        pass
    elif mode == "main":
        K.dma("sp", sumg[:].rearrange("p r a b c j -> p r (a b c j)"), sumg_d[:], sumg_d, sumg, sumg)
    else:
        stg = SX.sb("stg", [128, 2, 2, 10, NBO], F32)
        K.op("dve", lambda e: e.tensor_copy(out=stg[:], in_=SUML[:, :, :, :, 1:NBL - 1]), [SUML], [stg])
        K.dma("pool", sum_loc_d[:], stg[:].rearrange("p a b c j -> p (a b c j)"), stg, sum_loc_d, stg)
        K.all_gather(sum_loc_d, sumg_all, [list(range(8))])
        K.dma("pool", sumg[:].rearrange("p r a b c j -> p r (a b c j)"), sumg_all[:, :].rearrange("(r p) w -> p r w", p=128),
              sumg_all, sumg, sumg)
    fm_ = SX.sb("fm", [128, 2, NS, NBO], F32)
    K.dma("sp", fm_[:].rearrange("p d r j -> p d (r j)"), fmask[:], fmask, fm_, fm_)
    At = SX.sb("At", [128, NS, NBO], F32)
    Ht = SX.sb("Ht", [128, NS, NBO], F32)
    sco = SX.sb("sco", [128, NS * NBO], F32)
    for d in range(2):
        for c in range(10):
            K.op("dve", lambda e, d=d, c=c: e.scalar_tensor_tensor(
                out=At[:], in0=sumg[:, :, d, 0, c, :], scalar=-1.0, in1=fm_[:, d, :, :], op0=ALU.add, op1=ALU.mult),
                [sumg, fm_], [At])
            K.op("dve", lambda e: e.tensor_scalar(out=At[:], in0=At[:], scalar1=1.0, scalar2=None, op0=ALU.add), [At], [At])
            K.op("dve", lambda e, d=d, c=c: e.tensor_tensor(out=Ht[:], in0=sumg[:, :, d, 1, c, :], in1=fm_[:, d, :, :], op=ALU.mult),
                 [sumg, fm_], [Ht])
            Af = At[:].rearrange("p r j -> p (r j)")
            Hf = Ht[:].rearrange("p r j -> p (r j)")
            if d == 0:
                K.op("dve", lambda e, d=d, c=c, Af=Af, Hf=Hf: e.tensor_tensor_scan(
                    out=sco[:], data0=Af, data1=Hf, initial=HCTX[:, d, c:c + 1], op0=ALU.mult, op1=ALU.add),
                    [At, Ht, HCTX], [sco])
                K.op("dve", lambda e, c=c: e.tensor_copy(out=HIN[:, 0, c, 0:1], in_=sco[:, NS * NBO - 1:NS * NBO]), [sco], [HIN], part=True)
                K.op("dve", lambda e, c=c: e.tensor_tensor_scan(
                    out=HIN[:, 0, c, 1:NBL + 1], data0=SUML[:, 0, 0, c, :], data1=SUML[:, 0, 1, c, :],
                    initial=HIN[:, 0, c, 0:1], op0=ALU.mult, op1=ALU.add), [SUML, HIN], [HIN], part=True)
            else:
                K.op("dve", lambda e, d=d, c=c, Af=Af, Hf=Hf: e.tensor_tensor_scan(
                    out=sco[:][:, ::-1], data0=Af[:, ::-1], data1=Hf[:, ::-1], initial=HCTX[:, d, c:c + 1],
                    op0=ALU.mult, op1=ALU.add), [At, Ht, HCTX], [sco])
                K.op("dve", lambda e, c=c: e.tensor_copy(out=HIN[:, 1, c, NBL:NBL + 1], in_=sco[:, 0:1]), [sco], [HIN], part=True)
                K.op("dve", lambda e, c=c: e.tensor_tensor_scan(
                    out=HIN[:, 1, c, 0:NBL][:, ::-1], data0=SUML[:, 1, 0, c, :][:, ::-1], data1=SUML[:, 1, 1, c, :][:, ::-1],
                    initial=HIN[:, 1, c, NBL:NBL + 1], op0=ALU.mult, op1=ALU.add), [SUML, HIN], [HIN], part=True)
    SX.close()

    S2 = Stage(K, "r2")
    w_out = S2.sb("w_out", [128, 10, D], BF16)
    K.dma("pool", w_out[:], rg_w_out[:, :].rearrange("(c p) m -> p c m", p=128), rg_w_out, w_out, w_out)
    w_ing = S2.sb("w_ing", [128, 8, DR], BF16)
    K.dma("pool", w_ing[:], rg_w_in[:, 0:DR].rearrange("(k p) m -> p k m", p=128), rg_w_in, w_ing, w_ing)
    if mode != "sum":
        for li in range(2):
            for cc in range(44):
                K.dma("pool", WUP[li][cc], ffn_up[li][:, cc * 128:(cc + 1) * 128].rearrange("(k p) m -> p k m", p=128),
                      ffn_up, WUP[li], WUP[li], part=True)
            for m in range(8):
                K.dma("pool", WDN[li][m],
                      ffn_dn[li][:, m * 128:(m + 1) * 128].rearrange("(j p) m -> p j m", p=128),
                      ffn_dn, WDN[li], WDN[li], part=True)
    AB2 = [S2.sb("AB%d" % i, [128, 4, 384], F32) for i in range(3)]
    gel2 = [S2.sb("gel%d" % i, [128, 10, 384], BF16) for i in range(2)]
    xm2 = [S2.sb("xm%d" % i, [128, 8, 384], BF16) for i in range(2)]
    hx2 = [S2.sb("hx%d" % i, [128, 8, 384], F32) for i in range(2)]
    hsf = [S2.sb("hsf%d" % i, [128, 384], F32) for i in range(2)]
    hsr = [S2.sb("hsr%d" % i, [128, 384], F32) for i in range(2)]
    vb = [S2.sb("v%d" % i, [128, 10, 384], BF16) for i in range(2)]
    taus = [0]
    for nr in RG_ROWS:
        taus.append(taus[-1] + nr * 64)

    def p2_load(bi):
        N, tau = RG_ROWS[bi] * 64, taus[bi]
        xmb, hx, g2_ = xm2[bi % 2], hx2[bi % 2], gel2[bi % 2]
        K.dma("sp", xmb[:, :, 0:N], XM[:, :, tau:tau + N], XM, xmb, xmb)
        K.dma("sp", hx[:, :, 0:N], HX[:, :, tau:tau + N], HX, hx, hx)
        for oc in range(10):
            pp = PS[4 + oc % 4]
            for k in range(8):
                K.op("pe", lambda e, k=k: e.matmul(
                    pp[:, 0:N], lhsT=w_ing[:, k, oc * 128:(oc + 1) * 128], rhs=xmb[:, k, 0:N], start=(k == 0), stop=(k == 7)),
                    [w_ing, xmb], [pp], part=(k > 0), inc=(k == 7))
            K.op("act", lambda e: e.activation(out=g2_[:, oc, 0:N], in_=pp[:, 0:N], func=AF.Gelu_apprx_tanh), [pp], [g2_], part=True)

    def p2_scan(bi):
        N, tau = RG_ROWS[bi] * 64, taus[bi]
        g2_, v = gel2[bi % 2], vb[bi % 2]
        for c in range(10):
            ab = AB2[c % 3]
            K.dma("sp", ab[:, :, 0:N], SAB[c][:, :, tau:tau + N], SAB, ab, ab)
            hf_, hr_ = hsf[c % 2], hsr[c % 2]
            K.op("dve", lambda e: e.tensor_tensor_scan(
                out=hf_[:, 0:N], data0=ab[:, 0, 0:N], data1=ab[:, 1, 0:N], initial=HIN[:, 0, c, bi:bi + 1],
                op0=ALU.mult, op1=ALU.add), [ab], [hf_])
            K.op("dve", lambda e: e.tensor_tensor_scan(
                out=hr_[:, 0:N][:, ::-1], data0=ab[:, 2, 0:N][:, ::-1], data1=ab[:, 3, 0:N][:, ::-1],
                initial=HIN[:, 1, c, bi + 1:bi + 2], op0=ALU.mult, op1=ALU.add), [ab], [hr_])
            K.op("dve", lambda e: e.tensor_tensor(out=hf_[:, 0:N], in0=hf_[:, 0:N], in1=hr_[:, 0:N], op=ALU.add), [hf_, hr_], [hf_])
            K.op("dve", lambda e: e.scalar_tensor_tensor(
                out=v[:, c, 0:N], in0=hf_[:, 0:N], scalar=0.5, in1=g2_[:, c, 0:N], op0=ALU.mult, op1=ALU.mult),
                [hf_, g2_], [v], part=True)

    def p2_out_mm(bi):
        N = RG_ROWS[bi] * 64
        v = vb[bi % 2]
        for m in range(8):
            pp = PS[m % 4]
            for c in range(10):
                K.op("pe", lambda e, c=c: e.matmul(
                    pp[:, 0:N], lhsT=w_out[:, c, m * 128:(m + 1) * 128], rhs=v[:, c, 0:N], start=(c == 0), stop=(c == 9)),
                    [w_out, v], [pp], part=(c > 0), inc=(c == 9))
            hx = hx2[bi % 2]
            K.op("dve", lambda e, m=m, pp=pp, hx=hx: e.scalar_tensor_tensor(
                out=hx[:, m, 0:N], in0=pp[:, 0:N], scalar=VEC[:, V_G1, m:m + 1], in1=hx[:, m, 0:N], op0=ALU.mult, op1=ALU.add),
                [pp, hx], [hx], part=True)

    p2_load(0)
    p2_scan(0)
    for bi in range(NBL):
        if bi + 1 < NBL:
            p2_load(bi + 1)
        p2_out_mm(bi)
        if bi + 1 < NBL:
            p2_scan(bi + 1)
        N_ = RG_ROWS[bi] * 64
        K.dma("sp", HA[:, :, taus[bi]:taus[bi] + N_], hx2[bi % 2][:, :, 0:N_], hx2[bi % 2], HA, hx2[bi % 2], part=True)
    S2.close()
    if upto == "rg":
        K.final_wait("sp")
        return nc

    def ffn_stage(li, Hin, rin0, Hout, rout0, blocks, final):
        S = Stage(K, "f%d" % li)
        NIM = 640
        hbuf = [S.sb("h%d" % i, [128, 8, NIM], F32) for i in range(2)]
        sqxb = [S.sb("sqx%d" % i, [128, 8, NIM], BF16) for i in range(2)]
        rt_ = S.sb("rt", [128, NIM], F32)
        rstd_ = S.sb("rstd", [128, NIM], F32)
        tmpf = [S.sb("tmp%d" % i, [128, NIM], F32) for i in range(2)]
        vsb = [S.sb("vs%d" % i, [128, NIM], F32) for i in range(2)]
        wup = [S.sb("wup%d" % i, [128, 8, 128], BF16) for i in range(4)]
        wdn = [S.sb("wdn%d" % i, [128, 22, 128], BF16) for i in range(2)]
        usb = [S.sb("usb%d" % i, [128, 10, 64], BF16) for i in range(3)]
        dg9 = [S.sb("dg9%d" % i, [128, 9, 128], BF16) for i in range(4)]
        sgb = [S.sb("sg%d" % i, [128, 512], F32) for i in range(2)]
        hid = S.sb("hid", [128, 22, 512], BF16)
        if final:
            yf = S.sb("yf", [128, 8, 512], F32)
            ost = [S.sb("ost%d" % i, [128, D], F32) for i in range(2)]
        gsr = lambda c: VEC[:, V_GS2 + 6 * li, c:c + 1]
        shr = lambda c: VEC[:, V_SH2 + 6 * li, c:c + 1]
        cwo = _off["ffn_cw"] + li * 396
        cbo = _off["ffn_cb"] + li * 44
        wi = 0
        pend_store = []

        def ffn_prep_steps(bi):
            r0, r1 = blocks[bi]
            NI = (r1 - r0) * 64 + 128
            h = hbuf[bi % 2]
            xmq = sqxb[bi % 2]
            t_in = (r0 - 1 - rin0) * 64
            segs_ = [(0, min(NI, 512))] + ([(512, NI)] if NI > 512 else [])
            st = []

            def s_load():
                K.dma("sp", h[:, :, 0:NI], Hin[:, :, t_in:t_in + NI], Hin, h, h)
                if (r0 - 1 < 0) or (r1 + 1 > 64):
                    xi = (r0 - 1 + 3) * 64 + 2
                    K.dma("sp", vsb[bi % 2][:, 0:NI], valid[:, xi:xi + NI], valid, vsb[bi % 2], vsb[bi % 2])
            st.append(s_load)
            st.append(lambda: K.op("act", lambda e: e.activation(out=xmq[:, :, 0:NI], in_=h[:, :, 0:NI], func=AF.Square), [h], [xmq]))
            for (a, b) in segs_:
                def s_ss(a=a, b=b):
                    for c in range(8):
                        K.op("pe", lambda e, c=c: e.matmul(PS[7][:, 0:b - a], lhsT=onesB[:], rhs=xmq[:, c, a:b], start=(c == 0), stop=(c == 7)),
                             [xmq], [PS[7]], part=(c > 0), inc=(c == 7))
                st.append(s_ss)
                st.append(lambda a=a, b=b: K.op("act", lambda e: e.activation(
                    out=rt_[:, a:b], in_=PS[7][:, 0:b - a], func=AF.Sqrt, scale=1.0 / D, bias=epsb[:, 0:1]), [PS[7]], [rt_], part=True))
            st.append(lambda: K.op("dve", lambda e: e.reciprocal(out=rstd_[:, 0:NI], in_=rt_[:, 0:NI]), [rt_], [rstd_]))
            for c in range(8):
                def s_mod(c=c):
                    tb = tmpf[c % 2]
                    K.op("dve", lambda e: e.tensor_tensor(out=tb[:, 0:NI], in0=h[:, c, 0:NI], in1=rstd_[:, 0:NI], op=ALU.mult), [h, rstd_], [tb])
                    K.op("act", lambda e: e.activation(out=xmq[:, c, 0:NI], in_=tb[:, 0:NI], func=AF.Identity, scale=gsr(c), bias=shr(c)),
                         [tb], [xmq], part=True)
                st.append(s_mod)
            return st

        for f_ in ffn_prep_steps(0):
            f_()
        for bi, (r0, r1) in enumerate(blocks):
            nr = r1 - r0
            N = nr * 64
            NI = N + 128
            h = hbuf[bi % 2]
            edge = (r0 - 1 < 0) or (r1 + 1 > 64)
            vs_ = vsb[bi % 2]
            sqx = sqxb[bi % 2]
            xmf = sqx
            def mk(i):
                j, half = i // 2, i % 2
                return dict(j=j, half=half, cc=j + 22 * half, w=wup[(wi0 + i) % 4], us=usb[(wi0 + i) % 3], dg=dg9[(wi0 + i) % 4],
                            ua=PS[((wi0 + i) % 2) * 2], ub=PS[((wi0 + i) % 2) * 2 + 1], cp=PS[4 + (wi0 + i) % 3])

            def segs(c):
                return [(pp, a, b) for (pp, a, b) in ((c["ua"], 0, min(NI, 512)), (c["ub"], 512, NI)) if b > a]

            def stU(c):
                w = c["w"]
                K.dma("sp", w[:], WUP[li][c["cc"]], WUP[li], w, w)
                K.dma("sp", c["dg"][:], DG[li][c["cc"]], DG[li], c["dg"], c["dg"])
                for (pp, a, b) in segs(c):
                    for k in range(8):
                        K.op("pe", lambda e, pp=pp, a=a, b=b, k=k: e.matmul(
                            pp[:, 0:b - a], lhsT=w[:, k, :], rhs=xmf[:, k, a:b], start=(k == 0), stop=(k == 7)),
                            [w, xmf], [pp], part=(k > 0), inc=(k == 7))

            def stE(c):
                us = c["us"]
                usf = us[:].rearrange("p r c -> p (r c)")
                for si, (pp, a, b) in enumerate(segs(c)):
                    if edge:
                        K.op("dve", lambda e, pp=pp, a=a, b=b: e.tensor_tensor(
                            out=usf[:, a:b], in0=pp[:, 0:b - a], in1=vs_[:, a:b], op=ALU.mult), [pp, vs_], [us], part=(si > 0))
                    elif si == 0:
                        K.op("act", lambda e, pp=pp, a=a, b=b: e.copy(out=usf[:, a:b], in_=pp[:, 0:b - a]), [pp], [us], part=(si > 0))
                    else:
                        K.op("dve", lambda e, pp=pp, a=a, b=b: e.tensor_copy(out=usf[:, a:b], in_=pp[:, 0:b - a]), [pp], [us], part=(si > 0))

            def stC(c):
                us, dg, cp = c["us"], c["dg"], c["cp"]
                cp3 = cp[:, 0:N].rearrange("p (r c) -> p r c", c=64)
                for n_, t in enumerate([4, 0, 1, 2, 3, 5, 6, 7, 8]):
                    dy, dx = t // 3 - 1, t % 3 - 1
                    oc0, oc1 = max(0, -dx), 64 - max(0, dx)
                    K.op("pe", lambda e, t=t, dy=dy, dx=dx, oc0=oc0, oc1=oc1, n_=n_: e.matmul(
                        cp3[:, :, oc0:oc1], lhsT=dg[:, t, :], rhs=us[:, 1 + dy:1 + dy + nr, oc0 + dx:oc1 + dx],
                        start=(n_ == 0), stop=(n_ == 8)), [dg, us], [cp], part=(n_ > 0), inc=(n_ == 8))

            def stF(cv_, cg_):
                j = cv_["j"]
                sg = sgb[j % 2]
                K.op("act", lambda e: e.activation(
                    out=sg[:, 0:N], in_=cg_["cp"][:, 0:N], func=AF.Silu, bias=prm[:, cbo + 22 + j:cbo + 23 + j], scale=1.0), [cg_["cp"]], [sg])
                K.op("dve", lambda e: e.scalar_tensor_tensor(
                    out=hid[:, j, 0:N], in0=cv_["cp"][:, 0:N], scalar=prm[:, cbo + j:cbo + j + 1], in1=sg[:, 0:N], op0=ALU.add, op1=ALU.mult),
                    [cv_["cp"], sg], [hid], part=True)

            wi0 = wi
            cfg = [mk(i) for i in range(44)]
            wi += 44
            pre_steps = ffn_prep_steps(bi + 1) if bi + 1 < len(blocks) else []
            stU(cfg[0])
            for i in range(44):
                if i + 1 < 44:
                    stU(cfg[i + 1])
                if i == 1 and pend_store:
                    pend_store.pop(0)()
                stE(cfg[i])
                stC(cfg[i])
                if i % 2 == 1:
                    stF(cfg[i - 1], cfg[i])
                if bi + 1 < len(blocks) and i >= 4 and i % 2 == 0 and pre_steps:
                    pre_steps.pop(0)()
            while pre_steps:
                pre_steps.pop(0)()
            for m in range(8):
                w = wdn[m % 2]
                K.dma("sp", w[:], WDN[li][m], WDN[li], w, w)
                pp = PS[7] if m % 2 == 0 else PS[6]
                for j in range(22):
                    K.op("pe", lambda e, pp=pp, j=j, w=w: e.matmul(
                        pp[:, 0:N], lhsT=w[:, j, :], rhs=hid[:, j, 0:N], start=(j == 0), stop=(j == 21)), [w, hid], [pp], part=(j > 0), inc=(j == 21))
                K.op("dve", lambda e, pp=pp, m=m, h=h: e.scalar_tensor_tensor(
                    out=h[:, m, 64:64 + N], in0=pp[:, 0:N], scalar=VEC[:, V_G2 + 6 * li, m:m + 1], in1=h[:, m, 64:64 + N],
                    op0=ALU.mult, op1=ALU.add), [pp, h], [h], part=True)
            if not final:
                t_out = (r0 - rout0) * 64
                pend_store.append((lambda h=h, N=N, t_out=t_out: K.dma(
                    "sp", Hout[:, :, t_out:t_out + N], h[:, :, 64:64 + N], h, Hout, h, part=True)))
                if bi + 1 == len(blocks):
                    pend_store.pop(0)()
            else:
                K.op("act", lambda e, h=h: e.activation(out=sqx[:, :, 0:N], in_=h[:, :, 64:64 + N], func=AF.Square), [h], [sqx])
                for c in range(8):
                    K.op("pe", lambda e, c=c: e.matmul(PS[7][:, 0:N], lhsT=onesB[:], rhs=sqx[:, c, 0:N], start=(c == 0), stop=(c == 7)),
                         [sqx], [PS[7]], part=(c > 0), inc=(c == 7))
                K.op("act", lambda e: e.activation(out=rt_[:, 0:N], in_=PS[7][:, 0:N], func=AF.Sqrt, scale=1.0 / D, bias=epsb[:, 0:1]),
                     [PS[7]], [rt_])
                K.op("dve", lambda e: e.reciprocal(out=rstd_[:, 0:N], in_=rt_[:, 0:N]), [rt_], [rstd_])
                for c in range(8):
                    K.op("dve", lambda e, c=c, h=h: e.scalar_tensor_tensor(
                        out=yf[:, c, 0:N], in0=h[:, c, 64:64 + N], scalar=P("norm_final", 1, c), in1=rstd_[:, 0:N],
                        op0=ALU.mult, op1=ALU.mult), [h, rstd_], [yf], part=True)
                for tt in range(N // 128):
                    o = ost[tt % 2]
                    for hf in range(2):
                        pp = PS[hf]
                        for c4 in range(4):
                            c = hf * 4 + c4
                            K.op("pe", lambda e, c=c, c4=c4, tt=tt, pp=pp: e.transpose(
                                out=pp[:, c4 * 128:(c4 + 1) * 128], in_=yf[:, c, tt * 128:(tt + 1) * 128], identity=identF[:]),
                                [yf], [pp], part=(c4 > 0))
                        if hf == 0:
                            K.op("act", lambda e, o=o, pp=pp: e.copy(out=o[:, 0:512], in_=pp[:, :]), [pp], [o], part=True)
                        else:
                            K.op("dve", lambda e, o=o, pp=pp: e.tensor_copy(out=o[:, 512:1024], in_=pp[:, :]), [pp], [o], part=True)
                    tok = r0 * 64 + tt * 128
                    K.dma("sp", out_d[tok:tok + 128, :], o[:], o, out_d, o, part=True)
        S.close()

    def conf_stage(Hin, rin0, Hout, rout0, blocks):
        S = Stage(K, "c")
        w1 = S.sb("w1", [128, 8, 2 * D], BF16)
        for hf in range(2):
            K.dma("pool", w1[:, :, hf * D:(hf + 1) * D], cf_w1[:, hf * D:(hf + 1) * D].rearrange("(k p) m -> p k m", p=128),
                  cf_w1, w1, w1, part=(hf > 0))
        w2 = S.sb("w2", [128, 8, D], BF16)
        K.dma("pool", w2[:], cf_w2[:, :].rearrange("(k p) m -> p k m", p=128), cf_w2, w2, w2)
        dgc = [S.sb("dgc%d" % i, [128, 31, 128], BF16) for i in range(2)]
        NIM = 542
        hb2 = [S.sb("h%d" % i, [128, 8, NIM], F32) for i in range(2)]
        sqb2 = [S.sb("sqx%d" % i, [128, 8, NIM], BF16) for i in range(2)]
        rt_ = S.sb("rt", [128, NIM], F32)
        rstd_ = S.sb("rstd", [128, NIM], F32)
        rtn_ = S.sb("rtn", [128, NIM], F32)
        rstdn_ = S.sb("rstdn", [128, NIM], F32)
        tmpf = [S.sb("tmp%d" % i, [128, NIM], F32) for i in range(2)]
        vsb2 = [S.sb("vs%d" % i, [128, NIM], F32) for i in range(2)]
        sgb = [S.sb("sg%d" % i, [128, NIM], F32) for i in range(2)]
        glu = S.sb("glu", [128, 8, NIM], BF16)
        cv = S.sb("cv", [128, 8, 512], F32)
        act_ = S.sb("act", [128, 8, 512], BF16)
        dsq = act_
        hn = cv
        gsr = lambda c: VEC[:, V_GS1 + 6, c:c + 1]
        shr = lambda c: VEC[:, V_SH1 + 6, c:c + 1]

        def conf_prep_steps(bi):
            r0, r1 = blocks[bi]
            NI = (r1 - r0) * 64 + 30
            h, xmq, vs_ = hb2[bi % 2], sqb2[bi % 2], vsb2[bi % 2]
            t_in = (r0 - rin0) * 64 - 15
            segs_ = [(0, min(NI, 512))] + ([(512, NI)] if NI > 512 else [])
            st = []

            def s_load():
                K.dma("sp", h[:, :, 0:NI], Hin[:, :, t_in:t_in + NI], Hin, h, h)
                if (r0 - 1 < 0) or (r1 + 1 > 64):
                    xi = (r0 + 3) * 64 + 2 - 15
                    K.dma("sp", vs_[:, 0:NI], valid[:, xi:xi + NI], valid, vs_, vs_)
            st.append(s_load)
            st.append(lambda: K.op("act", lambda e: e.activation(out=xmq[:, :, 0:NI], in_=h[:, :, 0:NI], func=AF.Square), [h], [xmq]))
            for (a, b) in segs_:
                def s_ss(a=a, b=b):
                    for c in range(8):
                        K.op("pe", lambda e, c=c: e.matmul(PS[7][:, 0:b - a], lhsT=onesB[:], rhs=xmq[:, c, a:b], start=(c == 0), stop=(c == 7)),
                             [xmq], [PS[7]], part=(c > 0), inc=(c == 7))
                    K.op("act", lambda e: e.activation(
                        out=rtn_[:, a:b], in_=PS[7][:, 0:b - a], func=AF.Sqrt, scale=1.0 / D, bias=epsb[:, 0:1]), [PS[7]], [rtn_], part=True)
                st.append(s_ss)
            st.append(lambda: K.op("dve", lambda e: e.reciprocal(out=rstdn_[:, 0:NI], in_=rtn_[:, 0:NI]), [rtn_], [rstdn_]))
            for c in range(8):
                def s_mod(c=c):
                    tb = tmpf[c % 2]
                    K.op("dve", lambda e: e.tensor_tensor(out=tb[:, 0:NI], in0=h[:, c, 0:NI], in1=rstdn_[:, 0:NI], op=ALU.mult), [h, rstdn_], [tb])
                    K.op("act", lambda e: e.activation(out=xmq[:, c, 0:NI], in_=tb[:, 0:NI], func=AF.Identity, scale=gsr(c), bias=shr(c)),
                         [tb], [xmq], part=True)
                st.append(s_mod)
            return st

        for f_ in conf_prep_steps(0):
            f_()
        for bi, (r0, r1) in enumerate(blocks):
            N = (r1 - r0) * 64
            NI = N + 30
            h, sqx, vs_ = hb2[bi % 2], sqb2[bi % 2], vsb2[bi % 2]
            edge = (r0 - 1 < 0) or (r1 + 1 > 64)
            pre_steps = conf_prep_steps(bi + 1) if bi + 1 < len(blocks) else []
            segs = [(0, min(NI, 512))] + ([(512, NI)] if NI > 512 else [])
            for j in range(8):
                pa = [PS[0], PS[1]]
                pg = [PS[2], PS[3]]
                if j % 2 == 1:
                    pa, pg = [PS[4], PS[5]], [PS[6], PS[7]]
                for (pset, oc) in ((pg, 8 + j), (pa, j)):
                    for si, (a, b) in enumerate(segs):
                        for k in range(8):
                            K.op("pe", lambda e, pset=pset, si=si, a=a, b=b, k=k, oc=oc: e.matmul(
                                pset[si][:, 0:b - a], lhsT=w1[:, k, oc * 128:(oc + 1) * 128], rhs=sqx[:, k, a:b],
                                start=(k == 0), stop=(k == 7)), [w1, sqx], [pset[si]], part=(k > 0), inc=(k == 7))
                sg = sgb[j % 2]
                for si, (a, b) in enumerate(segs):
                    K.op("act", lambda e, si=si, a=a, b=b, sg=sg, pg=pg, j=j: e.activation(
                        out=sg[:, a:b], in_=pg[si][:, 0:b - a], func=AF.Sigmoid, bias=P("cf_b1", 1, 8 + j), scale=1.0),
                        [pg[si]], [sg], part=(si > 0))
                    K.op("dve", lambda e, si=si, a=a, b=b, sg=sg, pa=pa, j=j: e.scalar_tensor_tensor(
                        out=glu[:, j, a:b], in0=pa[si][:, 0:b - a], scalar=P("cf_b1", 1, j), in1=sg[:, a:b], op0=ALU.add, op1=ALU.mult),
                        [pa[si], sg], [glu], part=True)
                if edge:
                    K.op("dve", lambda e, j=j: e.tensor_tensor(out=glu[:, j, 0:NI], in0=glu[:, j, 0:NI], in1=vs_[:, 0:NI], op=ALU.mult),
                         [glu, vs_], [glu], part=True)
                if pre_steps:
                    pre_steps.pop(0)()
            for j in range(8):
                pp = PS[j % 4]
                dgt = dgc[j % 2]
                K.dma("sp", dgt[:], DG31[j], DG31, dgt, dgt)
                for t in range(31):
                    K.op("pe", lambda e, j=j, t=t, pp=pp, dgt=dgt: e.matmul(
                        pp[:, 0:N], lhsT=dgt[:, t, :], rhs=glu[:, j, t:t + N], start=(t == 0), stop=(t == 30)),
                        [dgt, glu], [pp], part=(t > 0), inc=(t == 30))
                K.op("act", lambda e, j=j, pp=pp: e.activation(out=cv[:, j, 0:N], in_=pp[:, 0:N], func=AF.Identity,
                                                              bias=P("cf_conv_b", 1, j), scale=1.0), [pp], [cv], part=True)
                if pre_steps:
                    pre_steps.pop(0)()
            while pre_steps:
                pre_steps.pop(0)()
            for j in range(8):
                K.op("pe", lambda e, j=j: e.matmul(PS[4][:, 0:N], lhsT=onesF[:], rhs=cv[:, j, 0:N], start=(j == 0), stop=(j == 7)),
                     [cv], [PS[4]], part=(j > 0), inc=(j == 7))
            for j in range(8):
                K.op("dve", lambda e, j=j: e.scalar_tensor_tensor(
                    out=cv[:, j, 0:N], in0=PS[4][:, 0:N], scalar=-1.0 / D, in1=cv[:, j, 0:N], op0=ALU.mult, op1=ALU.add),
                    [PS[4], cv], [cv], part=True)
            K.op("act", lambda e: e.activation(out=dsq[:, :, 0:N], in_=cv[:, :, 0:N], func=AF.Square), [cv], [dsq])
            for j in range(8):
                K.op("pe", lambda e, j=j: e.matmul(PS[5][:, 0:N], lhsT=onesB[:], rhs=dsq[:, j, 0:N], start=(j == 0), stop=(j == 7)),
                     [dsq], [PS[5]], part=(j > 0), inc=(j == 7))
            K.op("act", lambda e: e.activation(out=rt_[:, 0:N], in_=PS[5][:, 0:N], func=AF.Sqrt, scale=1.0 / D, bias=epsb[:, 0:1]),
                 [PS[5]], [rt_])
            K.op("dve", lambda e: e.reciprocal(out=rstd_[:, 0:N], in_=rt_[:, 0:N]), [rt_], [rstd_])
            for j in range(8):
                K.op("dve", lambda e, j=j: e.tensor_tensor(out=cv[:, j, 0:N], in0=cv[:, j, 0:N], in1=rstd_[:, 0:N], op=ALU.mult),
                     [cv, rstd_], [cv], part=True)
                K.op("act", lambda e, j=j: e.activation(out=act_[:, j, 0:N], in_=cv[:, j, 0:N], func=AF.Silu,
                                                       scale=P("cf_ln_g", 1, j), bias=P("cf_ln_b", 1, j)), [cv], [act_], part=True)
            for m in range(8):
                pp = PS[m % 4]
                for j in range(8):
                    K.op("pe", lambda e, m=m, j=j, pp=pp: e.matmul(
                        pp[:, 0:N], lhsT=w2[:, j, m * 128:(m + 1) * 128], rhs=act_[:, j, 0:N], start=(j == 0), stop=(j == 7)),
                        [w2, act_], [pp], part=(j > 0), inc=(j == 7))
                K.op("act", lambda e, m=m, pp=pp: e.activation(out=hn[:, m, 0:N], in_=pp[:, 0:N], func=AF.Identity,
                                                              scale=VEC[:, V_G1 + 6, m:m + 1], bias=VEC[:, V_BG, m:m + 1]),
                     [pp], [hn], part=True)
                K.op("dve", lambda e, m=m: e.tensor_tensor(out=hn[:, m, 0:N], in0=hn[:, m, 0:N], in1=h[:, m, 15:15 + N], op=ALU.add),
                     [hn, h], [hn], part=True)
            t_out = (r0 - rout0) * 64
            K.dma("sp", Hout[:, :, t_out:t_out + N], hn[:, :, 0:N], hn, Hout, hn, part=True)
        S.close()

    own8 = [(8 * i, 8 * i + 8) for i in range(8)]
    ffn_stage(0, HA, -3, HB, -2, [(-2, 0)] + own8 + [(64, 66)], False)
    if upto == "ffn0":
        K.final_wait("sp")
        return nc
    conf_stage(HB, -2, HC, -1, [(-1, 0)] + own8 + [(64, 65)])
    if upto == "conf":
        K.final_wait("sp")
        return nc
    ffn_stage(1, HC, -1, None, 0, own8, True)
    K.final_wait("sp")
    G.close()
    return nc


def _prep_core_inputs(core, inp, gate_tiles):
    b, k = core // 4, core % 4
    T0 = 4096 * k
    x = inp["x"]
    xin = np.zeros((NTX, D), np.float32)
    g0 = T0 - 194
    lo, hi = max(g0, 0), min(g0 + NTX, SEQ)
    xin[lo - g0:hi - g0] = x[b, lo:hi]
    valid = np.zeros((128, NTX), np.float32)
    valid[:, lo - g0:hi - g0] = 1.0
    cin = np.zeros((CTX + 3, D), np.float32)
    cin[2:2 + CTX] = inp["ctx"][b]
    cvalid = np.zeros((128, CTX + 3), np.float32)
    cvalid[:, 2:2 + CTX] = 1.0
    prm = np.zeros((128, PC), np.float32)

    def put(name, arr):
        arr = np.asarray(arr, np.float32).reshape(128, -1)
        prm[:, _off[name]:_off[name] + arr.shape[1]] = arr
    cT = np.stack([fm(inp["c"][b], 8), fm(inp["c_ctx"], 8)], axis=-1)
    put("cT", cT)
    put("ada_b", np.stack([fm(inp["ada_b"][i], 48) for i in range(2)], axis=1))
    put("norm_mix", np.stack([fm(inp["norm_mix"][i], 8) for i in range(2)], axis=1))
    put("norm_ffn", np.stack([fm(inp["norm_ffn"][i], 8) for i in range(2)], axis=1))
    put("norm_final", fm(inp["norm_final"], 8))
    put("rg_conv_w", np.stack([fm(inp["rg_conv_w"][0, t], 10) for t in range(4)], axis=-1))
    put("rg_conv_b", fm(inp["rg_conv_b"][0], 10))
    put("rg_ba", np.stack([fm(inp["rg_ba"][0, d].reshape(-1), 10) for d in range(2)], axis=1))
    put("rg_bx", np.stack([fm(inp["rg_bx"][0, d].reshape(-1), 10) for d in range(2)], axis=1))
    put("rg_lam", np.stack([fm(inp["rg_lam"][0, d], 10) for d in range(2)], axis=1))
    put("cf_b1", fm(inp["cf_b_pw1"][0], 16))
    put("cf_conv_w", np.stack([fm(inp["cf_conv_w"][0, t], 8) for t in range(31)], axis=-1))
    put("cf_conv_b", fm(inp["cf_conv_b"][0], 8))
    put("cf_ln_g", fm(inp["cf_ln_g"][0], 8))
    put("cf_ln_b", fm(inp["cf_ln_b"][0], 8))
    put("cf_b2", fm(inp["cf_b_pw2"][0], 8))
    cw = np.stack([np.stack([fm(inp["ffn_conv_w"][i].reshape(9, -1)[t], 44) for t in range(9)], axis=-1)
                   for i in range(2)], axis=1)
    put("ffn_cw", cw)
    put("ffn_cb", np.stack([fm(inp["ffn_conv_b"][i], 44) for i in range(2)], axis=1))
    m = {"xin": xin, "cin": cin, "valid": valid, "cvalid": cvalid}
    xoth = np.zeros((3, 4099, D), np.float32)
    ovalid = np.zeros((3, 128, 4099), np.float32)
    gwo = np.zeros((3, 2, NGT, 128, 128), np.float32)
    fmk = np.zeros((128, 2, 3, NBO), np.float32)
    o_lam = np.zeros((128, 3, 10), np.float32)
    o_ba = np.zeros((128, 3, 10), np.float32)
    o_bx = np.zeros((128, 3, 10), np.float32)
    for o in range(3):
        ko = (k + 1 + o) % 4
        dsel = 0 if ko < k else 1
        g0 = 4096 * ko - 2
        lo, hi = max(g0, 0), min(g0 + 4099, SEQ)
        xoth[o, lo - g0:hi - g0] = x[b, lo:hi]
        ovalid[o, :, lo - g0:hi - g0] = 1.0
        gwo[o, 0] = gate_tiles[dsel * 2 + 0]
        gwo[o, 1] = gate_tiles[dsel * 2 + 1]
        o_lam[:, o] = fm(inp["rg_lam"][0, dsel], 10)
        o_ba[:, o] = fm(inp["rg_ba"][0, dsel].reshape(-1), 10)
        o_bx[:, o] = fm(inp["rg_bx"][0, dsel].reshape(-1), 10)
        for j in range(NBO):
            if ko < k and (ko < k - 1 or j <= NBO - 2):
                fmk[:, 0, o, j] = 1.0
            if ko > k and (ko > k + 1 or j >= 1):
                fmk[:, 1, o, j] = 1.0
    put("o_lam", o_lam)
    put("o_ba", o_ba)
    put("o_bx", o_bx)
    m["prm"] = prm
    m["xoth"] = xoth
    m["ovalid"] = ovalid
    m["gwo"] = gwo.reshape(3 * 2 * NGT, 128, 128)
    m["fmask"] = fmk.reshape(128, 2, 3 * NBO)
    return m


def _shared_inputs(inp):
    gw = np.zeros((4, NGT, 128, 128), np.float32)
    mats = [inp["rg_wa"][0, 0], inp["rg_wx"][0, 0], inp["rg_wa"][0, 1], inp["rg_wx"][0, 1]]
    for gi, w in enumerate(mats):
        dense = np.zeros((DR, DR), np.float32)
        for h in range(16):
            dense[h * 80:(h + 1) * 80, h * 80:(h + 1) * 80] = w[h]
        for ti, (j, i) in enumerate(GT):
            gw[gi, ti] = dense[i * 128:(i + 1) * 128, j * 128:(j + 1) * 128]
    return gw, {
        "ident": np.eye(128, dtype=np.float32),
        "ada_w": np.ascontiguousarray(inp["ada_w"], np.float32),
        "rg_w_in": np.ascontiguousarray(inp["rg_w_in"][0], np.float32),
        "gw": gw.reshape(4 * NGT, 128, 128),
        "rg_w_out": np.ascontiguousarray(inp["rg_w_out"][0], np.float32),
        "cf_w1": np.ascontiguousarray(inp["cf_w_pw1"][0], np.float32),
        "cf_w2": np.ascontiguousarray(inp["cf_w_pw2"][0], np.float32),
        "ffn_up": np.ascontiguousarray(inp["ffn_w_up"], np.float32),
        "ffn_dn": np.ascontiguousarray(inp["ffn_w_down"], np.float32),
    }


_SUM_KEYS = ["xin", "cin", "valid", "cvalid", "prm", "ident", "ada_w", "rg_w_in", "gw"]
_PROG = {}


def _prog(mode):
    if mode not in _PROG:
        _PROG[mode] = build_program(mode)
    return _PROG[mode]


def kernel(**inputs):
    inp = {k: np.asarray(v) for k, v in inputs.items()}
    gate_tiles, shared = _shared_inputs(inp)
    percore = [_prep_core_inputs(c, inp, gate_tiles) for c in range(8)]
    cores = list(range(8))
    in2 = [{**shared, **percore[c]} for c in cores]
    r2 = run_bass_kernel_spmd(_prog("solo"), in2, core_ids=cores)
    out = np.zeros((2, SEQ, D), np.float32)
    for c in cores:
        b, k = c // 4, c % 4
        out[b, 4096 * k:4096 * (k + 1)] = np.asarray(r2.results[c]["out"])
    return out
```

```python
import numpy as np
from contextlib import ExitStack
import concourse.bass as bass
import concourse.mybir as mybir
from concourse.bass_utils import run_bass_kernel_spmd

F32, BF16 = mybir.dt.float32, mybir.dt.bfloat16
AF = mybir.ActivationFunctionType
ALU = mybir.AluOpType

D = 1024
SEQ = 16384
CTX = 256
DR = 1280
DFF = 2816
EPS = 1e-6
NTX = 4483
NTE = 4480
RG_ROWS = [3, 3, 5, 6, 6, 6, 6, 6, 6, 6, 6, 5, 3, 3]
NBL = len(RG_ROWS)
NBO = NBL - 2

def _gate_tiles():
    tiles = []
    for j in range(10):
        heads = set(range((j * 128) // 80, ((j + 1) * 128 - 1) // 80 + 1))
        ins = set()
        for h in heads:
            for ch in (h * 80, h * 80 + 79):
                ins.add(ch // 128)
        for i in sorted(ins):
            tiles.append((j, i))
    return tiles
GT = _gate_tiles()
NGT = len(GT)

_off = {}
def _alloc_cols():
    o = 0
    for name, n in [("cT", 16), ("ada_b", 96), ("norm_mix", 16), ("norm_ffn", 16), ("norm_final", 8),
                    ("rg_conv_w", 40), ("rg_conv_b", 10), ("rg_ba", 20), ("rg_bx", 20), ("rg_lam", 20), ("o_lam", 30), ("o_ba", 30), ("o_bx", 30),
                    ("cf_b1", 16), ("cf_conv_w", 248), ("cf_conv_b", 8), ("cf_ln_g", 8), ("cf_ln_b", 8),
                    ("cf_b2", 8), ("ffn_cw", 792), ("ffn_cb", 88)]:
        _off[name] = o
        o += n
    return o
PC = _alloc_cols()


def fm(v, nch):
    return np.ascontiguousarray(np.asarray(v, np.float32).reshape(nch, 128).T)


class Buf:
    def __init__(self, t, name):
        self.t = t
        self.name = name
        self.w = {}
        self.r = {}
        self.ds = None

    def __getitem__(self, i):
        return self.t[i]


class Ctx:
    def __init__(self, nc, ndsem=64):
        self.nc = nc
        self.E = {"pe": nc.tensor, "act": nc.scalar, "dve": nc.vector, "pool": nc.gpsimd, "sp": nc.sync}
        self.sem = {e: nc.alloc_semaphore("s_" + e) for e in self.E}
        self.cnt = {e: 0 for e in self.E}
        self.known = {e: {} for e in self.E}
        self.dpool = [[nc.alloc_semaphore("d%d" % i), 0, "d%d" % i] for i in range(ndsem)]
        self.dfree = list(range(ndsem))
        self.nops = 0
        self.extra = []

    def dram(self, ap, name):
        return Buf(ap, name)

    def _deps(self, reads, writes, part):
        deps = {}

        def add(d):
            for k, (s, v) in d.items():
                if k not in deps or deps[k][1] < v:
                    deps[k] = (s, v)
        for b in reads:
            add(b.w)
        for b in writes:
            if not part:
                add(b.w)
            add(b.r)
        return deps

    def _wait(self, e, deps):
        for k, (s, v) in deps.items():
            if k == "pe" and e == "pe":
                continue
            if self.known[e].get(k, 0) >= v:
                continue
            self.E[e].wait_ge(s, v)
            self.known[e][k] = v

    def op(self, e, fn, reads=(), writes=(), part=False, inc=True):
        self._wait(e, self._deps(reads, writes, part))
        ins = fn(self.E[e])
        if inc:
            self.cnt[e] += 1
            ins.then_inc(self.sem[e], 1)
            tag = (self.sem[e], self.cnt[e])
        else:
            tag = (self.sem[e], self.cnt[e] + 1)
        for b in reads:
            b.r[e] = tag
        for b in writes:
            b.w[e] = tag
        self.nops += 1
        return ins

    def _dsem(self, b):
        if b.ds is None:
            b.ds = self.dfree.pop(0)
        return self.dpool[b.ds]

    def release(self, bufs):
        for b in bufs:
            if b.ds is not None:
                self.dfree.append(b.ds)
                b.ds = None

    def dma(self, q, out, in_, src, dst, owner, part=False):
        self._wait(q, self._deps([src], [dst], part))
        ent = self._dsem(owner)
        ent[1] += 16
        self.E[q].dma_start(out=out, in_=in_).then_inc(ent[0], 16)
        tag = (ent[0], ent[1])
        src.r[ent[2]] = tag
        dst.w[ent[2]] = tag

    def all_gather(self, src, dst, groups):
        self._wait("pool", self._deps([src], [dst], False))
        sem = self.nc.alloc_semaphore("cc%d" % len(self.extra))
        ins = self.E["pool"].collective_compute("AllGather", mybir.AluOpType.bypass, replica_groups=groups,
                                                ins=[src.t.opt()], outs=[dst.t.opt()])
        ins.then_inc(sem)
        key = "cc%d" % len(self.extra)
        self.extra.append([sem, 1, key])
        src.r[key] = (sem, 1)
        dst.w[key] = (sem, 1)

    def barrier(self):
        for e in self.E:
            deps = {}
            for ent in self.extra:
                deps[ent[2]] = (ent[0], ent[1])
            for e2 in self.E:
                if e2 != e and self.cnt[e2] > 0:
                    deps[e2] = (self.sem[e2], self.cnt[e2])
            for ent in self.dpool:
                if ent[1] > 0:
                    deps[ent[2]] = (ent[0], ent[1])
            for k, (s, v) in deps.items():
                if self.known[e].get(k, 0) >= v:
                    continue
                self.E[e].wait_ge(s, v)
                self.known[e][k] = v

    def final_wait(self, e="sp"):
        for ent in self.dpool:
            if ent[1] > 0 and self.known[e].get(ent[2], 0) < ent[1]:
                self.E[e].wait_ge(ent[0], ent[1])
                self.known[e][ent[2]] = ent[1]


class Stage:
    def __init__(self, K, name):
        self.K = K
        self.name = name
        self.stack = ExitStack()
        self.bufs = []
        self.n = 0

    def sb(self, name, shape, dt):
        self.n += 1
        t = self.stack.enter_context(self.K.nc.sbuf_tensor("%s_%s_%d" % (self.name, name, self.n), list(shape), dt))
        b = Buf(t, name)
        self.bufs.append(b)
        return b

    def close(self):
        self.K.barrier()
        self.K.release(self.bufs)
        self.stack.close()


def build_program(mode, upto="all", debug=False):
    nc = bass.Bass("TRN2", target_bir_lowering=False)
    K = Ctx(nc)

    def din(name, shape, dt=F32):
        return K.dram(nc.dram_tensor(name, list(shape), dt, kind="ExternalInput").ap(), name)

    def dint(name, shape, dt=F32):
        kind = "ExternalOutput" if (debug and name in ("HA", "HB", "HC")) else "Internal"
        return K.dram(nc.dram_tensor(name, list(shape), dt, kind=kind).ap(), name)

    def dout(name, shape, dt=F32):
        return K.dram(nc.dram_tensor(name, list(shape), dt, kind="ExternalOutput").ap(), name)

    xin = din("xin", [NTX, D])
    cin = din("cin", [CTX + 3, D])
    valid = din("valid", [128, NTX])
    cvalid = din("cvalid", [128, CTX + 3])
    prm_d = din("prm", [128, PC])
    ident_d = din("ident", [128, 128])
    ada_w = din("ada_w", [2, D, 6 * D])
    rg_w_in = din("rg_w_in", [D, 2 * DR])
    gw_d = din("gw", [4 * NGT, 128, 128])
    if mode != "sum":
        rg_w_out = din("rg_w_out", [DR, D])
        cf_w1 = din("cf_w1", [D, 2 * D])
        cf_w2 = din("cf_w2", [D, D])
        ffn_up = din("ffn_up", [2, D, 2 * DFF])
        ffn_dn = din("ffn_dn", [2, DFF, D])
        fmask = din("fmask", [128, 2, (3 if mode == "solo" else 8) * NBO])
        out_d = dout("out", [4096, D])
    SUMW = 2 * 2 * 10 * NBO
    NS = 3 if mode == "solo" else 8
    if mode == "solo":
        xoth = din("xoth", [3, 4099, D])
        ovalid = din("ovalid", [3, 128, 4099])
        gwo_d = din("gwo", [3 * 2 * NGT, 128, 128])
    if mode == "sum":
        sum_out = dout("sums", [128, SUMW])
    elif mode == "main":
        sumg_d = din("sumg", [128, 8, SUMW])
    elif mode == "solo":
        pass
    else:
        sum_loc_d = K.dram(nc.dram_tensor("sum_loc", [128, SUMW], F32).ap(), "sum_loc")
        sumg_all = K.dram(nc.dram_tensor("sumg_all", [8 * 128, SUMW], F32).ap(), "sumg_all")

    if mode != "sum":
        SAB = dint("SAB", [10, 128, 4, NTE])
        XM = dint("XM", [128, 8, NTE], BF16)
        HX = dint("HX", [128, 8, NTE])
        HA = dint("HA", [128, 8, NTE])
        HB = dint("HB", [128, 8, 68 * 64])
        HC = dint("HC", [128, 8, 66 * 64])
        WUP = [dint("WUP%d" % i, [44, 128, 8, 128], BF16) for i in range(2)]
        WDN = [dint("WDN%d" % i, [8, 128, 22, 128], BF16) for i in range(2)]
        DG = [dint("DG%d" % i, [44, 128, 9, 128], BF16) for i in range(2)]
        DG31 = dint("DG31", [8, 128, 31, 128], BF16)

    PS = [Buf(nc.alloc_psum_tensor("ps%d" % i, [128, 512], F32), "ps%d" % i) for i in range(8)]

    G = Stage(K, "g")
    prm = G.sb("prm", [128, PC], F32)
    identF = G.sb("identF", [128, 128], F32)
    identB = G.sb("identB", [128, 128], BF16)
    onesB = G.sb("onesB", [128, 128], BF16)
    onesF = G.sb("onesF", [128, 128], F32)
    MOD = G.sb("MOD", [128, 2, 2, 6, 8], F32)
    VEC = G.sb("VEC", [128, 16, 8], F32)
    SUML = G.sb("SUML", [128, 2, 2, 10, NBL], F32)
    HCTX = G.sb("HCTX", [128, 2, 10], F32)
    HIN = G.sb("HIN", [128, 2, 10, NBL + 1], F32)
    SCV = G.sb("SCV", [128, 6, 10], F32)
    HBG = G.sb("HBG", [128, 4, 10], F32)
    SCVO = G.sb("SCVO", [128, 6, 10], F32)
    HBGO = G.sb("HBGO", [128, 6, 10], F32)
    SUMO = G.sb("SUMO", [128, 3, 2, 2, 10, NBO], F32)
    accO = G.sb("accO", [128, 3, 10, NBO], F32)

    def P(name, n=None, i=0):
        o = _off[name] + i
        return prm[:, o:o + (n if n is not None else 1)]

    V_GS1, V_SH1, V_G1, V_GS2, V_SH2, V_G2 = 0, 1, 2, 3, 4, 5
    V_GSC, V_SHC, V_BG = 12, 13, 14

    K.dma("sp", prm[:], prm_d[:], prm_d, prm, prm)
    K.dma("sp", identF[:], ident_d[:], ident_d, identF, identF)
    K.op("dve", lambda e: e.tensor_copy(out=identB[:], in_=identF[:]), [identF], [identB])
    K.op("dve", lambda e: e.memset(onesB[:], 1.0), [], [onesB])
    K.op("dve", lambda e: e.memset(onesF[:], 1.0), [], [onesF])
    K.op("dve", lambda e: e.memset(SUML[:], 0.0), [], [SUML])

    S0 = Stage(K, "p")
    scb = S0.sb("scb", [128, 16], BF16)
    K.op("act", lambda e: e.activation(out=scb[:], in_=P("cT", 16), func=AF.Silu), [prm], [scb])
    adaw = [S0.sb("adaw%d" % i, [128, 8, 1024], BF16) for i in range(2)]
    it = 0
    for li in range(2):
        for q in range(6):
            wt = adaw[it % 2]
            K.dma("pool", wt[:], ada_w[li][:, q * 1024:(q + 1) * 1024].rearrange("(k p) m -> p k m", p=128),
                  ada_w, wt, wt)
            pp = PS[it % 2]
            for m in range(8):
                for k in range(8):
                    K.op("pe", lambda e, m=m, k=k, wt=wt, pp=pp: e.matmul(
                        pp[:, m * 2:m * 2 + 2], lhsT=wt[:, k, m * 128:(m + 1) * 128], rhs=scb[:, k * 2:k * 2 + 2],
                        start=(k == 0), stop=(k == 7)), [wt, scb], [pp], part=(k > 0 or m > 0), inc=(k == 7))
            for j in range(2):
                K.op("dve", lambda e, j=j, li=li, q=q, pp=pp: e.tensor_tensor(
                    out=MOD[:, li, j, q, :], in0=pp[:, j:16:2], in1=P("ada_b", 8, li * 48 + q * 8), op=ALU.add),
                    [pp, prm], [MOD], part=True)
            it += 1
    for li in range(2):
        K.op("dve", lambda e, li=li: e.scalar_tensor_tensor(
            out=VEC[:, V_GS1 + 6 * li, :], in0=MOD[:, li, 0, 1, :], scalar=1.0, in1=P("norm_mix", 8, li * 8),
            op0=ALU.add, op1=ALU.mult), [MOD, prm], [VEC], part=True)
        K.op("dve", lambda e, li=li: e.scalar_tensor_tensor(
            out=VEC[:, V_GS2 + 6 * li, :], in0=MOD[:, li, 0, 4, :], scalar=1.0, in1=P("norm_ffn", 8, li * 8),
            op0=ALU.add, op1=ALU.mult), [MOD, prm], [VEC], part=True)
        for vrow, q in ((V_SH1, 0), (V_G1, 2), (V_SH2, 3), (V_G2, 5)):
            K.op("dve", lambda e, li=li, vrow=vrow, q=q: e.tensor_copy(out=VEC[:, vrow + 6 * li, :], in_=MOD[:, li, 0, q, :]),
                 [MOD], [VEC], part=True)
    K.op("dve", lambda e: e.scalar_tensor_tensor(
        out=VEC[:, V_GSC, :], in0=MOD[:, 0, 1, 1, :], scalar=1.0, in1=P("norm_mix", 8, 0),
        op0=ALU.add, op1=ALU.mult), [MOD, prm], [VEC], part=True)
    K.op("dve", lambda e: e.tensor_copy(out=VEC[:, V_SHC, :], in_=MOD[:, 0, 1, 0, :]), [MOD], [VEC], part=True)
    K.op("dve", lambda e: e.tensor_tensor(out=VEC[:, V_BG, :], in0=P("cf_b2", 8), in1=MOD[:, 1, 0, 2, :], op=ALU.mult),
         [MOD, prm], [VEC], part=True)
    tmp20 = S0.sb("tmp20", [128, 50], F32)
    ev20 = S0.sb("ev20", [128, 50], F32)
    q20 = S0.sb("q20", [128, 50], F32)
    K.op("act", lambda e: e.activation(out=ev20[:], in_=P("rg_lam", 50), func=AF.Exp, scale=-1.0), [prm], [ev20])
    K.op("dve", lambda e: e.tensor_scalar(out=q20[:], in0=ev20[:], scalar1=-1.0 / 7, scalar2=1.0 / 6, op0=ALU.mult, op1=ALU.add),
         [ev20], [q20])
    for cst in (1.0 / 5, 1.0 / 4, 1.0 / 3, 1.0 / 2, 1.0):
        K.op("dve", lambda e: e.tensor_tensor(out=q20[:], in0=q20[:], in1=ev20[:], op=ALU.mult), [q20, ev20], [q20])
        K.op("dve", lambda e, cst=cst: e.tensor_scalar(out=q20[:], in0=q20[:], scalar1=-1.0, scalar2=cst, op0=ALU.mult, op1=ALU.add),
             [q20], [q20])
    K.op("dve", lambda e: e.tensor_tensor(out=tmp20[:], in0=q20[:], in1=ev20[:], op=ALU.mult), [q20, ev20], [tmp20])
    K.op("dve", lambda e: e.tensor_scalar(out=SCV[:, 0:2, :], in0=tmp20[:, 0:20].rearrange("p (d c) -> p d c", d=2),
                                          scalar1=-4.0, scalar2=None, op0=ALU.mult), [tmp20], [SCV], part=True)
    K.op("dve", lambda e: e.tensor_scalar(out=SCV[:, 2:4, :], in0=tmp20[:, 0:20].rearrange("p (d c) -> p d c", d=2),
                                          scalar1=-8.0, scalar2=None, op0=ALU.mult), [tmp20], [SCV], part=True)
    K.op("dve", lambda e: e.tensor_scalar(out=SCVO[:, 0:3, :], in0=tmp20[:, 20:50].rearrange("p (d c) -> p d c", d=3),
                                          scalar1=-4.0, scalar2=None, op0=ALU.mult), [tmp20], [SCVO], part=True)
    K.op("dve", lambda e: e.tensor_scalar(out=SCVO[:, 3:6, :], in0=tmp20[:, 20:50].rearrange("p (d c) -> p d c", d=3),
                                          scalar1=-8.0, scalar2=None, op0=ALU.mult), [tmp20], [SCVO], part=True)
    for o in range(3):
        K.op("dve", lambda e, o=o: e.tensor_scalar(out=HBGO[:, o * 2, :], in0=P("o_ba", 10, o * 10), scalar1=0.5,
                                                     scalar2=None, op0=ALU.mult), [prm], [HBGO], part=True)
        K.op("dve", lambda e, o=o: e.tensor_scalar(out=HBGO[:, o * 2 + 1, :], in0=P("o_bx", 10, o * 10), scalar1=0.5,
                                                     scalar2=None, op0=ALU.mult), [prm], [HBGO], part=True)
    for d in range(2):
        K.op("dve", lambda e, d=d: e.tensor_scalar(out=HBG[:, d * 2, :], in0=P("rg_ba", 10, d * 10), scalar1=0.5,
                                                     scalar2=None, op0=ALU.mult), [prm], [HBG], part=True)
        K.op("dve", lambda e, d=d: e.tensor_scalar(out=HBG[:, d * 2 + 1, :], in0=P("rg_bx", 10, d * 10), scalar1=0.5,
                                                     scalar2=None, op0=ALU.mult), [prm], [HBG], part=True)
    if mode != "sum":
        dgst = [S0.sb("dgst%d" % i, [128, 9, 128], BF16) for i in range(3)]
        for li in range(2):
            cwo_ = _off["ffn_cw"] + li * 396
            for cc in range(44):
                dgs = dgst[cc % 3]
                for t in range(9):
                    K.op("dve", lambda e, t=t, dgs=dgs, cc=cc, cwo_=cwo_: e.tensor_scalar(
                        out=dgs[:, t, :], in0=identB[:], scalar1=prm[:, cwo_ + cc * 9 + t:cwo_ + cc * 9 + t + 1], scalar2=None,
                        op0=ALU.mult), [identB, prm], [dgs], part=(t > 0))
                K.dma("sp", DG[li][cc], dgs[:], dgs, DG[li], dgs, part=True)
    if mode != "sum":
        dg31s = [S0.sb("dg31s%d" % i, [128, 31, 128], BF16) for i in range(2)]
        for c in range(8):
            dgs = dg31s[c % 2]
            for t in range(31):
                K.op("dve", lambda e, c=c, t=t, dgs=dgs: e.tensor_scalar(
                    out=dgs[:, t, :], in0=identB[:], scalar1=P("cf_conv_w", 1, c * 31 + t), scalar2=None, op0=ALU.mult),
                    [identB, prm], [dgs], part=(t > 0))
            K.dma("sp", DG31[c], dgs[:], dgs, DG31, dgs, part=True)
    S0.close()

    def norm_mod(S, h, n, xm, gsrow, shrow, sq, ssb, rt, rstd, tmp, out_f32=None):
        K.op("act", lambda e: e.activation(out=sq[:, :, 0:n], in_=h[:, :, 0:n], func=AF.Square), [h], [sq])
        nseg = [(0, min(n, 512))] + ([(512, n)] if n > 512 else [])
        for si, (a, b) in enumerate(nseg):
            for c in range(8):
                K.op("pe", lambda e, c=c, a=a, b=b, si=si: e.matmul(
                    ssb[si][:, 0:b - a], lhsT=onesB[:], rhs=sq[:, c, a:b], start=(c == 0), stop=(c == 7)),
                    [sq], [ssb[si]], part=(c > 0), inc=(c == 7))
            K.op("act", lambda e, a=a, b=b, si=si: e.activation(
                out=rt[:, a:b], in_=ssb[si][:, 0:b - a], func=AF.Sqrt, scale=1.0 / D, bias=epsb[:, 0:1]),
                [ssb[si]], [rt], part=True)
        K.op("dve", lambda e: e.reciprocal(out=rstd[:, 0:n], in_=rt[:, 0:n]), [rt], [rstd])
        for c in range(8):
            tb = tmp[c % 2]
            K.op("dve", lambda e, c=c, tb=tb: e.tensor_tensor(out=tb[:, 0:n], in0=h[:, c, 0:n], in1=rstd[:, 0:n], op=ALU.mult),
                 [h, rstd], [tb])
            if out_f32 is None:
                K.op("act", lambda e, c=c, tb=tb: e.activation(
                    out=xm[:, c, 0:n], in_=tb[:, 0:n], func=AF.Identity, scale=gsrow(c), bias=shrow(c)),
                    [tb], [xm], part=True)
            else:
                K.op("act", lambda e, c=c, tb=tb: e.activation(
                    out=out_f32[:, c, 0:n], in_=tb[:, 0:n], func=AF.Identity, scale=gsrow(c)),
                    [tb], [out_f32], part=True)

    epsb = G.sb("epsb", [128, 2], F32)
    K.op("dve", lambda e: e.memset(epsb[:, 0:1], EPS), [], [epsb], part=True)
    K.op("dve", lambda e: e.memset(epsb[:, 1:2], 1.0), [], [epsb], part=True)

    S1 = Stage(K, "r1")
    w_in = S1.sb("w_in", [128, 8, DR], BF16)
    K.dma("pool", w_in[:], rg_w_in[:, DR:2 * DR].rearrange("(k p) m -> p k m", p=128), rg_w_in, w_in, w_in)
    gwt = S1.sb("gw", [128, 4 * NGT, 128], BF16)
    K.dma("pool", gwt[:], gw_d[:, :, :].rearrange("t p m -> p t m"), gw_d, gwt, gwt)
    dg4 = S1.sb("dg4", [128, 40, 128], BF16)
    for c in range(10):
        for t in range(4):
            K.op("dve", lambda e, c=c, t=t: e.tensor_scalar(
                out=dg4[:, c * 4 + t, :], in0=identB[:], scalar1=P("rg_conv_w", 1, c * 4 + t), scalar2=None, op0=ALU.mult),
                [identB, prm], [dg4], part=True)
    NM = 387
    xt4 = S1.sb("xt4", [128, 4, D], F32)
    hb_ = S1.sb("h", [128, 8, NM], F32)
    xm = S1.sb("xm", [128, 8, NM], BF16)
    rt = S1.sb("rt", [128, NM], F32)
    rstd = S1.sb("rstd", [128, NM], F32)
    tmpn = [S1.sb("tmpn%d" % i, [128, NM], F32) for i in range(2)]
    vslb = [S1.sb("vsl%d" % i, [128, NM], F32) for i in range(2)]
    uxb = S1.sb("uxb", [128, 10, NM], BF16)
    uxcb = [S1.sb("uxc%d" % i, [128, 10, 384], BF16) for i in range(2)]
    ABs = [S1.sb("AB%d" % i, [128, 4, 384], F32) for i in range(4)]
    trb = [S1.sb("tr%d" % i, [128, 384], F32) for i in range(8)]
    tib = [S1.sb("ti%d" % i, [128, 384], F32) for i in range(8)]
    a2b = [S1.sb("a2%d" % i, [128, 384], F32) for i in range(8)]
    hsb = [S1.sb("hs%d" % i, [128, 384], F32) for i in range(2)]
    accb = S1.sb("acc", [128, 2, 10, NBL + 1], F32)

    if debug:
        print("S1 sbuf remaining", nc.sbuf_bytes_remaining)
    GTI = {}
    for ti, (j, i) in enumerate(GT):
        GTI.setdefault(j, []).append((ti, i))
    AX = mybir.AxisListType.X

    def rgA(B):
        src, vsrc, i0, N, par, store, tau0, is_ctx = B["src"], B["vsrc"], B["i0"], B["N"], B["par"], B["store"], B["tau0"], B["blk"] is None
        N3 = N + 3
        ntile = (N3 + 127) // 128
        vsl, uxc = vslb[par], uxcb[par]
        steps = []

        def s_load():
            for t in range(ntile):
                nt = min(128, N3 - 128 * t)
                K.dma("sp", xt4[0:nt, t, :], src[i0 - 2 + 128 * t:i0 - 2 + 128 * t + nt, :], src, xt4, xt4, part=(t > 0))
            K.dma("sp", vsl[:, 0:N3], vsrc[:, i0 - 2:i0 - 2 + N3], vsrc, vsl, vsl)
        steps.append(s_load)

        def s_tr(c):
            pp = PS[c % 2]
            for t in range(ntile):
                nt = min(128, N3 - 128 * t)
                K.op("pe", lambda e, t=t, nt=nt: e.transpose(
                    out=pp[:, 128 * t:128 * t + nt], in_=xt4[0:nt, t, c * 128:(c + 1) * 128], identity=identF[0:nt, 0:nt]),
                    [xt4, identF], [pp], part=(t > 0))
            if c % 2 == 0:
                K.op("act", lambda e: e.copy(out=hb_[:, c, 0:N3], in_=pp[:, 0:N3]), [pp], [hb_], part=True)
            else:
                K.op("dve", lambda e: e.tensor_copy(out=hb_[:, c, 0:N3], in_=pp[:, 0:N3]), [pp], [hb_], part=True)
        for c in range(8):
            steps.append(lambda c=c: s_tr(c))
        if is_ctx:
            gs = lambda c: VEC[:, V_GSC, c:c + 1]
            sh = lambda c: VEC[:, V_SHC, c:c + 1]
        else:
            gs = lambda c: VEC[:, V_GS1, c:c + 1]
            sh = lambda c: VEC[:, V_SH1, c:c + 1]

        def s_stat():
            K.op("act", lambda e: e.activation(out=xm[:, :, 0:N3], in_=hb_[:, :, 0:N3], func=AF.Square), [hb_], [xm])
            for c in range(8):
                K.op("pe", lambda e, c=c: e.matmul(PS[2][:, 0:N3], lhsT=onesB[:], rhs=xm[:, c, 0:N3], start=(c == 0), stop=(c == 7)),
                     [xm], [PS[2]], part=(c > 0), inc=(c == 7))
            K.op("act", lambda e: e.activation(out=rt[:, 0:N3], in_=PS[2][:, 0:N3], func=AF.Sqrt, scale=1.0 / D, bias=epsb[:, 0:1]),
                 [PS[2]], [rt])
            K.op("dve", lambda e: e.reciprocal(out=rstd[:, 0:N3], in_=rt[:, 0:N3]), [rt], [rstd])
        steps.append((s_stat, "stat"))

        def s_mod(c):
            tb = tmpn[c % 2]
            K.op("dve", lambda e: e.tensor_tensor(out=tb[:, 0:N3], in0=hb_[:, c, 0:N3], in1=rstd[:, 0:N3], op=ALU.mult), [hb_, rstd], [tb])
            K.op("act", lambda e: e.activation(out=xm[:, c, 0:N3], in_=tb[:, 0:N3], func=AF.Identity, scale=gs(c), bias=sh(c)),
                 [tb], [xm], part=True)
        for c in range(8):
            steps.append(lambda c=c: s_mod(c))
        if store:
            def s_store():
                K.dma("pool", HX[:, :, tau0:tau0 + N], hb_[:, :, 2:2 + N], hb_, HX, hb_, part=True)
                K.dma("pool", XM[:, :, tau0:tau0 + N], xm[:, :, 2:2 + N], xm, XM, xm, part=True)
            steps.append(s_store)

        def s_win(oc):
            pp = PS[oc % 2]
            for k in range(8):
                K.op("pe", lambda e, k=k: e.matmul(pp[:, 0:N3], lhsT=w_in[:, k, oc * 128:(oc + 1) * 128], rhs=xm[:, k, 0:N3],
                                                   start=(k == 0), stop=(k == 7)), [w_in, xm], [pp], part=(k > 0), inc=(k == 7))
            K.op("dve", lambda e: e.tensor_tensor(out=uxb[:, oc, 0:N3], in0=pp[:, 0:N3], in1=vsl[:, 0:N3], op=ALU.mult),
                 [pp, vsl], [uxb], part=True)
        for oc in range(10):
            steps.append(lambda oc=oc: s_win(oc))

        def s_conv(c):
            pp = PS[2] if c % 2 == 0 else PS[5]
            for t in range(4):
                K.op("pe", lambda e, t=t: e.matmul(pp[:, 0:N], lhsT=dg4[:, c * 4 + t, :], rhs=uxb[:, c, t:t + N], start=(t == 0), stop=(t == 3)),
                     [dg4, uxb], [pp], part=(t > 0), inc=(t == 3))
            K.op("act", lambda e: e.activation(out=uxc[:, c, 0:N], in_=pp[:, 0:N], func=AF.Identity, bias=P("rg_conv_b", 1, c), scale=1.0),
                 [pp], [uxc], part=True)
        for c in range(10):
            steps.append(lambda c=c: s_conv(c))
        return steps

    def rgB(B):
        N, par, blk, edge, store, tau0, other = B["N"], B["par"], B["blk"], B["edge"], B["store"], B["tau0"], B["other"]
        vsl, uxc = vslb[par], uxcb[par]
        if other is None:
            cfgs = []
            for d in range(2):
                cfgs.append(dict(
                    tb=[(d * 2 + g) * NGT for g in range(2)], hb=(lambda g, c, d=d: HBG[:, d * 2 + g, c:c + 1]),
                    s05=(lambda c, d=d: SCV[:, d, c:c + 1]), s1=(lambda c, d=d: SCV[:, 2 + d, c:c + 1]), plane=d, pa=(lambda c, d=d: 2 * d),
                    acc=(lambda c, d=d: accb[:, d, c, NBL:NBL + 1] if blk is None else accb[:, d, c, blk:blk + 1]),
                    scans=[(d == 1, (lambda c, d=d: HCTX[:, d, c:c + 1] if blk is None else SUML[:, d, 1, c, blk:blk + 1]),
                            HCTX if blk is None else SUML)]))
            gsz = 2
        else:
            o = other
            cfgs = [dict(
                tb=[g * NGT for g in range(2)], hb=(lambda g, c: HBGO[:, o * 2 + g, c:c + 1]),
                s05=(lambda c: SCVO[:, o, c:c + 1]), s1=(lambda c: SCVO[:, 3 + o, c:c + 1]), plane=0, pa=(lambda c: 2 * ((c // 4) % 2)),
                acc=(lambda c: accO[:, o, c, blk:blk + 1]),
                scans=[(False, (lambda c: SUMO[:, o, 0, 1, c, blk:blk + 1]), SUMO),
                       (True, (lambda c: SUMO[:, o, 1, 1, c, blk:blk + 1]), SUMO)])]
            gsz = 3
        steps = []

        def s_p1(ui, c, cf):
            for g in range(2):
                pp = PS[3 + g] if ui % 2 == 0 else PS[6 + g]
                tl = GTI[c]
                for n_, (ti, i) in enumerate(tl):
                    K.op("pe", lambda e, ti=ti, i=i, n_=n_: e.matmul(
                        pp[:, 0:N], lhsT=gwt[:, cf["tb"][g] + ti, :], rhs=uxc[:, i, 0:N],
                        start=(n_ == 0), stop=(n_ == len(tl) - 1)), [gwt, uxc], [pp], part=(n_ > 0), inc=(n_ == len(tl) - 1))
                dst = trb[ui] if g == 0 else tib[ui]
                K.op("act", lambda e, g=g, dst=dst: e.activation(
                    out=dst[:, 0:N], in_=pp[:, 0:N], func=AF.Tanh, scale=0.5, bias=cf["hb"](g, c)), [pp], [dst])

        def s_p2(ui, c, cf):
            ab = ABs[c % 4]
            d = cf["plane"]
            if edge:
                K.op("dve", lambda e: e.scalar_tensor_tensor(
                    out=trb[ui][:, 0:N], in0=trb[ui][:, 0:N], scalar=1.0, in1=vsl[:, 2:2 + N], op0=ALU.add, op1=ALU.mult,
                    accum_out=cf["acc"](c)), [trb[ui], vsl], [trb[ui], accb], part=True)
            else:
                K.op("dve", lambda e: e.tensor_scalar(
                    out=trb[ui][:, 0:N], in0=trb[ui][:, 0:N], scalar1=1.0, scalar2=None, op0=ALU.add, op1=ALU.add,
                    accum_out=cf["acc"](c)), [trb[ui]], [trb[ui], accb], part=True)
            pa = cf["pa"](c)
            K.op("act", lambda e: e.activation(out=ab[:, pa, 0:N], in_=trb[ui][:, 0:N], func=AF.Exp, scale=cf["s05"](c)),
                 [trb[ui]], [ab], part=(other is not None or d > 0))
            K.op("act", lambda e: e.activation(out=a2b[ui][:, 0:N], in_=trb[ui][:, 0:N], func=AF.Exp, scale=cf["s1"](c)),
                 [trb[ui]], [a2b[ui]])
            K.op("dve", lambda e: e.scalar_tensor_tensor(
                out=tib[ui][:, 0:N], in0=tib[ui][:, 0:N], scalar=1.0, in1=uxc[:, c, 0:N], op0=ALU.add, op1=ALU.mult),
                [tib[ui], uxc], [tib[ui]])

        def s_p3(uis):
            for ui in uis:
                K.op("act", lambda e, ui=ui: e.activation(
                    out=a2b[ui][:, 0:N], in_=a2b[ui][:, 0:N], func=AF.Sqrt, scale=-1.0, bias=epsb[:, 1:2]), [a2b[ui]], [a2b[ui]])

        def s_p4(ui, c, cf):
            ab = ABs[c % 4]
            d = cf["plane"]
            pa = cf["pa"](c)
            K.op("dve", lambda e: e.scalar_tensor_tensor(
                out=ab[:, pa + 1, 0:N], in0=a2b[ui][:, 0:N], scalar=0.0, in1=tib[ui][:, 0:N], op0=ALU.max, op1=ALU.mult),
                [tib[ui], a2b[ui]], [ab], part=True)
            for si, (rev, dstf, dstbuf) in enumerate(cf["scans"]):
                hs = hsb[(d + si) % 2]
                if not rev:
                    K.op("dve", lambda e, hs=hs: e.tensor_tensor_scan(
                        out=hs[:, 0:N], data0=ab[:, pa, 0:N], data1=ab[:, pa + 1, 0:N], initial=0.0,
                        op0=ALU.mult, op1=ALU.add), [ab], [hs])
                    src_col = hs[:, N - 1:N]
                else:
                    K.op("dve", lambda e, hs=hs: e.tensor_tensor_scan(
                        out=hs[:, 0:N][:, ::-1], data0=ab[:, pa, 0:N][:, ::-1], data1=ab[:, pa + 1, 0:N][:, ::-1],
                        initial=0.0, op0=ALU.mult, op1=ALU.add), [ab], [hs])
                    src_col = hs[:, 0:1]
                K.op("dve", lambda e, src_col=src_col, dstc=dstf(c): e.tensor_copy(out=dstc, in_=src_col), [hs], [dstbuf], part=True)
            if store and d == 1:
                K.dma("pool", SAB[c][:, :, tau0:tau0 + N], ab[:, :, 0:N], ab, SAB, ab, part=True)

        groups = []
        for gi, c0 in enumerate(range(0, 10, gsz)):
            groups.append([((gi % 2) * 4 + ui, c, cf) for ui, (c, cf) in
                           enumerate([(c, cf) for c in range(c0, min(10, c0 + gsz)) for cf in cfgs])])

        def P1(g):
            return [((lambda u=u: s_p1(*u)), "b") for u in groups[g]]

        def P2(g):
            return [((lambda u=u: s_p2(*u)), "b") for u in groups[g]]

        def P3(g):
            return [((lambda uis=[u[0] for u in groups[g]]: s_p3(uis)), "sqrt")]

        def P4(g):
            return [((lambda u=u: s_p4(*u)), "b") for u in groups[g]]
        steps = P1(0) + P2(0)
        for g in range(len(groups)):
            if g + 1 < len(groups):
                steps += P1(g + 1)
            steps += P3(g)
            if g + 1 < len(groups):
                steps += P2(g + 1)
            steps += P4(g)
        return steps

    blist = [dict(src=cin, vsrc=cvalid, i0=2, N=CTX, blk=None, edge=True, store=False, tau0=0, other=None, pre=None)]
    tau = 0
    for bi, nr in enumerate(RG_ROWS):
        blist.append(dict(src=xin, vsrc=valid, i0=tau + 2, N=nr * 64, blk=bi, edge=bi in (0, NBL - 1), store=(mode != "sum"),
                          tau0=tau, other=None, pre=None))
        tau += nr * 64
    if mode == "solo":
        for o in range(3):
            tau = 0
            for j, nr in enumerate(RG_ROWS[1:-1]):
                blist.append(dict(src=Buf(xoth.t[o], "xo"), vsrc=Buf(ovalid.t[o], "vo"), i0=tau + 2, N=nr * 64, blk=j,
                                  edge=j in (0, NBO - 1), store=False, tau0=0, other=o, pre=(o if j == 0 else None)))
                tau += nr * 64
    for i, B in enumerate(blist):
        B["par"] = i % 2
    def astep(st):
        return st if isinstance(st, tuple) else (st, "a")
    for st in rgA(blist[0]):
        astep(st)[0]()
    for i, B in enumerate(blist):
        if B["pre"] is not None:
            o = B["pre"]
            K.dma("pool", gwt[:, 0:2 * NGT, :], gwo_d[o * 2 * NGT:(o + 1) * 2 * NGT, :, :].rearrange("t p m -> p t m"),
                  gwo_d, gwt, gwt)
        sb_ = rgB(B)
        sa_ = [astep(st) for st in rgA(blist[i + 1])] if i + 1 < len(blist) else []
        ia = 0
        for ib, (fb, tagb) in enumerate(sb_):
            fb()
            tgt = ((ib + 1) * len(sa_)) // len(sb_)
            while ia < tgt:
                fa, taga = sa_[ia]
                if taga == "stat" and tagb != "sqrt" and any(t == "sqrt" for (_, t) in sb_[ib + 1:]):
                    break
                fa()
                ia += 1
        while ia < len(sa_):
            sa_[ia][0]()
            ia += 1
    for d in range(2):
        for c in range(10):
            K.op("act", lambda e, d=d, c=c: e.activation(
                out=SUML[:, d, 0, c, :], in_=accb[:, d, c, 0:NBL], func=AF.Exp, scale=SCV[:, d, c:c + 1]),
                [accb], [SUML], part=True)
    if mode == "solo":
        for o in range(3):
            for c in range(10):
                for d in range(2):
                    K.op("act", lambda e, d=d, c=c, o=o: e.activation(
                        out=SUMO[:, o, d, 0, c, :], in_=accO[:, o, c, :], func=AF.Exp, scale=SCVO[:, o, c:c + 1]),
                        [accb], [SUMO], part=True)
    S1.close()

    SX = Stage(K, "x")
    if mode == "sum":
        stg = SX.sb("stg", [128, 2, 2, 10, NBO], F32)
        K.op("dve", lambda e: e.tensor_copy(out=stg[:], in_=SUML[:, :, :, :, 1:NBL - 1]), [SUML], [stg])
        K.dma("sp", sum_out[:], stg[:].rearrange("p a b c j -> p (a b c j)"), stg, sum_out, stg)
        K.final_wait("sp")
        SX.close()
        G.close()
        return nc

    sumg = SUMO if mode == "solo" else SX.sb("sumg", [128, 8, 2, 2, 10, NBO], F32)
    if mode == "solo":
        pass
    elif mode == "main":
        K.dma("sp", sumg[:].rearrange("p r a b c j -> p r (a b c j)"), sumg_d[:], sumg_d, sumg, sumg)
    else:
        stg = SX.sb("stg", [128, 2, 2, 10, NBO], F32)
        K.op("dve", lambda e: e.tensor_copy(out=stg[:], in_=SUML[:, :, :, :, 1:NBL - 1]), [SUML], [stg])
        K.dma("pool", sum_loc_d[:], stg[:].rearrange("p a b c j -> p (a b c j)"), stg, sum_loc_d, stg)
        K.all_gather(sum_loc_d, sumg_all, [list(range(8))])
        K.dma("pool", sumg[:].rearrange("p r a b c j -> p r (a b c j)"), sumg_all[:, :].rearrange("(r p) w -> p r w", p=128),
              sumg_all, sumg, sumg)
    fm_ = SX.sb("fm", [128, 2, NS, NBO], F32)
    K.dma("sp", fm_[:].rearrange("p d r j -> p d (r j)"), fmask[:], fmask, fm_, fm_)
    At = SX.sb("At", [128, NS, NBO], F32)
    Ht = SX.sb("Ht", [128, NS, NBO], F32)
    sco = SX.sb("sco", [128, NS * NBO], F32)
    for d in range(2):
        for c in range(10):
            K.op("dve", lambda e, d=d, c=c: e.scalar_tensor_tensor(
                out=At[:], in0=sumg[:, :, d, 0, c, :], scalar=-1.0, in1=fm_[:, d, :, :], op0=ALU.add, op1=ALU.mult),
                [sumg, fm_], [At])
            K.op("dve", lambda e: e.tensor_scalar(out=At[:], in0=At[:], scalar1=1.0, scalar2=None, op0=ALU.add), [At], [At])
            K.op("dve", lambda e, d=d, c=c: e.tensor_tensor(out=Ht[:], in0=sumg[:, :, d, 1, c, :], in1=fm_[:, d, :, :], op=ALU.mult),
                 [sumg, fm_], [Ht])
            Af = At[:].rearrange("p r j -> p (r j)")
            Hf = Ht[:].rearrange("p r j -> p (r j)")
            if d == 0:
                K.op("dve", lambda e, d=d, c=c, Af=Af, Hf=Hf: e.tensor_tensor_scan(
                    out=sco[:], data0=Af, data1=Hf, initial=HCTX[:, d, c:c + 1], op0=ALU.mult, op1=ALU.add),
                    [At, Ht, HCTX], [sco])
                K.op("dve", lambda e, c=c: e.tensor_copy(out=HIN[:, 0, c, 0:1], in_=sco[:, NS * NBO - 1:NS * NBO]), [sco], [HIN], part=True)
                K.op("dve", lambda e, c=c: e.tensor_tensor_scan(
                    out=HIN[:, 0, c, 1:NBL + 1], data0=SUML[:, 0, 0, c, :], data1=SUML[:, 0, 1, c, :],
                    initial=HIN[:, 0, c, 0:1], op0=ALU.mult, op1=ALU.add), [SUML, HIN], [HIN], part=True)
            else:
                K.op("dve", lambda e, d=d, c=c, Af=Af, Hf=Hf: e.tensor_tensor_scan(
                    out=sco[:][:, ::-1], data0=Af[:, ::-1], data1=Hf[:, ::-1], initial=HCTX[:, d, c:c + 1],
                    op0=ALU.mult, op1=ALU.add), [At, Ht, HCTX], [sco])
                K.op("dve", lambda e, c=c: e.tensor_copy(out=HIN[:, 1, c, NBL:NBL + 1], in_=sco[:, 0:1]), [sco], [HIN], part=True)
                K.op("dve", lambda e, c=c: e.tensor_tensor_scan(
                    out=HIN[:, 1, c, 0:NBL][:, ::-1], data0=SUML[:, 1, 0, c, :][:, ::-1], data1=SUML[:, 1, 1, c, :][:, ::-1],
                    initial=HIN[:, 1, c, NBL:NBL + 1], op0=ALU.mult, op1=ALU.add), [SUML, HIN], [HIN], part=True)
    SX.close()

    S2 = Stage(K, "r2")
    w_out = S2.sb("w_out", [128, 10, D], BF16)
    K.dma("pool", w_out[:], rg_w_out[:, :].rearrange("(c p) m -> p c m", p=128), rg_w_out, w_out, w_out)
    w_ing = S2.sb("w_ing", [128, 8, DR], BF16)
    K.dma("pool", w_ing[:], rg_w_in[:, 0:DR].rearrange("(k p) m -> p k m", p=128), rg_w_in, w_ing, w_ing)
    if mode != "sum":
        for li in range(2):
            for cc in range(44):
                K.dma("pool", WUP[li][cc], ffn_up[li][:, cc * 128:(cc + 1) * 128].rearrange("(k p) m -> p k m", p=128),
                      ffn_up, WUP[li], WUP[li], part=True)
            for m in range(8):
                K.dma("pool", WDN[li][m],
                      ffn_dn[li][:, m * 128:(m + 1) * 128].rearrange("(j p) m -> p j m", p=128),
                      ffn_dn, WDN[li], WDN[li], part=True)
    AB2 = [S2.sb("AB%d" % i, [128, 4, 384], F32) for i in range(3)]
    gel2 = [S2.sb("gel%d" % i, [128, 10, 384], BF16) for i in range(2)]
    xm2 = [S2.sb("xm%d" % i, [128, 8, 384], BF16) for i in range(2)]
    hx2 = [S2.sb("hx%d" % i, [128, 8, 384], F32) for i in range(2)]
    hsf = [S2.sb("hsf%d" % i, [128, 384], F32) for i in range(2)]
    hsr = [S2.sb("hsr%d" % i, [128, 384], F32) for i in range(2)]
    vb = [S2.sb("v%d" % i, [128, 10, 384], BF16) for i in range(2)]
    taus = [0]
    for nr in RG_ROWS:
        taus.append(taus[-1] + nr * 64)

    def p2_load(bi):
        N, tau = RG_ROWS[bi] * 64, taus[bi]
        xmb, hx, g2_ = xm2[bi % 2], hx2[bi % 2], gel2[bi % 2]
        K.dma("sp", xmb[:, :, 0:N], XM[:, :, tau:tau + N], XM, xmb, xmb)
        K.dma("sp", hx[:, :, 0:N], HX[:, :, tau:tau + N], HX, hx, hx)
        for oc in range(10):
            pp = PS[4 + oc % 4]
            for k in range(8):
                K.op("pe", lambda e, k=k: e.matmul(
                    pp[:, 0:N], lhsT=w_ing[:, k, oc * 128:(oc + 1) * 128], rhs=xmb[:, k, 0:N], start=(k == 0), stop=(k == 7)),
                    [w_ing, xmb], [pp], part=(k > 0), inc=(k == 7))
            K.op("act", lambda e: e.activation(out=g2_[:, oc, 0:N], in_=pp[:, 0:N], func=AF.Gelu_apprx_tanh), [pp], [g2_], part=True)

    def p2_scan(bi):
        N, tau = RG_ROWS[bi] * 64, taus[bi]
        g2_, v = gel2[bi % 2], vb[bi % 2]
        for c in range(10):
            ab = AB2[c % 3]
            K.dma("sp", ab[:, :, 0:N], SAB[c][:, :, tau:tau + N], SAB, ab, ab)
            hf_, hr_ = hsf[c % 2], hsr[c % 2]
            K.op("dve", lambda e: e.tensor_tensor_scan(
                out=hf_[:, 0:N], data0=ab[:, 0, 0:N], data1=ab[:, 1, 0:N], initial=HIN[:, 0, c, bi:bi + 1],
                op0=ALU.mult, op1=ALU.add), [ab], [hf_])
            K.op("dve", lambda e: e.tensor_tensor_scan(
                out=hr_[:, 0:N][:, ::-1], data0=ab[:, 2, 0:N][:, ::-1], data1=ab[:, 3, 0:N][:, ::-1],
                initial=HIN[:, 1, c, bi + 1:bi + 2], op0=ALU.mult, op1=ALU.add), [ab], [hr_])
            K.op("dve", lambda e: e.tensor_tensor(out=hf_[:, 0:N], in0=hf_[:, 0:N], in1=hr_[:, 0:N], op=ALU.add), [hf_, hr_], [hf_])
            K.op("dve", lambda e: e.scalar_tensor_tensor(
                out=v[:, c, 0:N], in0=hf_[:, 0:N], scalar=0.5, in1=g2_[:, c, 0:N], op0=ALU.mult, op1=ALU.mult),
                [hf_, g2_], [v], part=True)

    def p2_out_mm(bi):
        N = RG_ROWS[bi] * 64
        v = vb[bi % 2]
        for m in range(8):
            pp = PS[m % 4]
            for c in range(10):
                K.op("pe", lambda e, c=c: e.matmul(
                    pp[:, 0:N], lhsT=w_out[:, c, m * 128:(m + 1) * 128], rhs=v[:, c, 0:N], start=(c == 0), stop=(c == 9)),
                    [w_out, v], [pp], part=(c > 0), inc=(c == 9))
            hx = hx2[bi % 2]
            K.op("dve", lambda e, m=m, pp=pp, hx=hx: e.scalar_tensor_tensor(
                out=hx[:, m, 0:N], in0=pp[:, 0:N], scalar=VEC[:, V_G1, m:m + 1], in1=hx[:, m, 0:N], op0=ALU.mult, op1=ALU.add),
                [pp, hx], [hx], part=True)

    p2_load(0)
    p2_scan(0)
    for bi in range(NBL):
        if bi + 1 < NBL:
            p2_load(bi + 1)
        p2_out_mm(bi)
        if bi + 1 < NBL:
            p2_scan(bi + 1)
        N_ = RG_ROWS[bi] * 64
        K.dma("sp", HA[:, :, taus[bi]:taus[bi] + N_], hx2[bi % 2][:, :, 0:N_], hx2[bi % 2], HA, hx2[bi % 2], part=True)
    S2.close()
    if upto == "rg":
        K.final_wait("sp")
        return nc

    def ffn_stage(li, Hin, rin0, Hout, rout0, blocks, final):
        S = Stage(K, "f%d" % li)
        NIM = 640
        hbuf = [S.sb("h%d" % i, [128, 8, NIM], F32) for i in range(2)]
        sqxb = [S.sb("sqx%d" % i, [128, 8, NIM], BF16) for i in range(2)]
        rt_ = S.sb("rt", [128, NIM], F32)
        rstd_ = S.sb("rstd", [128, NIM], F32)
        tmpf = [S.sb("tmp%d" % i, [128, NIM], F32) for i in range(2)]
        vsb = [S.sb("vs%d" % i, [128, NIM], F32) for i in range(2)]
        wup = [S.sb("wup%d" % i, [128, 8, 128], BF16) for i in range(4)]
        wdn = [S.sb("wdn%d" % i, [128, 22, 128], BF16) for i in range(2)]
        usb = [S.sb("usb%d" % i, [128, 10, 64], BF16) for i in range(3)]
        dg9 = [S.sb("dg9%d" % i, [128, 9, 128], BF16) for i in range(4)]
        sgb = [S.sb("sg%d" % i, [128, 512], F32) for i in range(2)]
        hid = S.sb("hid", [128, 22, 512], BF16)
        if final:
            yf = S.sb("yf", [128, 8, 512], F32)
            ost = [S.sb("ost%d" % i, [128, D], F32) for i in range(2)]
        gsr = lambda c: VEC[:, V_GS2 + 6 * li, c:c + 1]
        shr = lambda c: VEC[:, V_SH2 + 6 * li, c:c + 1]
        cwo = _off["ffn_cw"] + li * 396
        cbo = _off["ffn_cb"] + li * 44
        wi = 0
        pend_store = []

        def ffn_prep_steps(bi):
            r0, r1 = blocks[bi]
            NI = (r1 - r0) * 64 + 128
            h = hbuf[bi % 2]
            xmq = sqxb[bi % 2]
            t_in = (r0 - 1 - rin0) * 64
            segs_ = [(0, min(NI, 512))] + ([(512, NI)] if NI > 512 else [])
            st = []

            def s_load():
                K.dma("sp", h[:, :, 0:NI], Hin[:, :, t_in:t_in + NI], Hin, h, h)
                if (r0 - 1 < 0) or (r1 + 1 > 64):
                    xi = (r0 - 1 + 3) * 64 + 2
                    K.dma("sp", vsb[bi % 2][:, 0:NI], valid[:, xi:xi + NI], valid, vsb[bi % 2], vsb[bi % 2])
            st.append(s_load)
            st.append(lambda: K.op("act", lambda e: e.activation(out=xmq[:, :, 0:NI], in_=h[:, :, 0:NI], func=AF.Square), [h], [xmq]))
            for (a, b) in segs_:
                def s_ss(a=a, b=b):
                    for c in range(8):
                        K.op("pe", lambda e, c=c: e.matmul(PS[7][:, 0:b - a], lhsT=onesB[:], rhs=xmq[:, c, a:b], start=(c == 0), stop=(c == 7)),
                             [xmq], [PS[7]], part=(c > 0), inc=(c == 7))
                st.append(s_ss)
                st.append(lambda a=a, b=b: K.op("act", lambda e: e.activation(
                    out=rt_[:, a:b], in_=PS[7][:, 0:b - a], func=AF.Sqrt, scale=1.0 / D, bias=epsb[:, 0:1]), [PS[7]], [rt_], part=True))
            st.append(lambda: K.op("dve", lambda e: e.reciprocal(out=rstd_[:, 0:NI], in_=rt_[:, 0:NI]), [rt_], [rstd_]))
            for c in range(8):
                def s_mod(c=c):
                    tb = tmpf[c % 2]
                    K.op("dve", lambda e: e.tensor_tensor(out=tb[:, 0:NI], in0=h[:, c, 0:NI], in1=rstd_[:, 0:NI], op=ALU.mult), [h, rstd_], [tb])
                    K.op("act", lambda e: e.activation(out=xmq[:, c, 0:NI], in_=tb[:, 0:NI], func=AF.Identity, scale=gsr(c), bias=shr(c)),
                         [tb], [xmq], part=True)
                st.append(s_mod)
            return st

        for f_ in ffn_prep_steps(0):
            f_()
        for bi, (r0, r1) in enumerate(blocks):
            nr = r1 - r0
            N = nr * 64
            NI = N + 128
            h = hbuf[bi % 2]
            edge = (r0 - 1 < 0) or (r1 + 1 > 64)
            vs_ = vsb[bi % 2]
            sqx = sqxb[bi % 2]
            xmf = sqx
            def mk(i):
                j, half = i // 2, i % 2
                return dict(j=j, half=half, cc=j + 22 * half, w=wup[(wi0 + i) % 4], us=usb[(wi0 + i) % 3], dg=dg9[(wi0 + i) % 4],
                            ua=PS[((wi0 + i) % 2) * 2], ub=PS[((wi0 + i) % 2) * 2 + 1], cp=PS[4 + (wi0 + i) % 3])

            def segs(c):
                return [(pp, a, b) for (pp, a, b) in ((c["ua"], 0, min(NI, 512)), (c["ub"], 512, NI)) if b > a]

            def stU(c):
                w = c["w"]
                K.dma("sp", w[:], WUP[li][c["cc"]], WUP[li], w, w)
                K.dma("sp", c["dg"][:], DG[li][c["cc"]], DG[li], c["dg"], c["dg"])
                for (pp, a, b) in segs(c):
                    for k in range(8):
                        K.op("pe", lambda e, pp=pp, a=a, b=b, k=k: e.matmul(
                            pp[:, 0:b - a], lhsT=w[:, k, :], rhs=xmf[:, k, a:b], start=(k == 0), stop=(k == 7)),
                            [w, xmf], [pp], part=(k > 0), inc=(k == 7))

            def stE(c):
                us = c["us"]
                usf = us[:].rearrange("p r c -> p (r c)")
                for si, (pp, a, b) in enumerate(segs(c)):
                    if edge:
                        K.op("dve", lambda e, pp=pp, a=a, b=b: e.tensor_tensor(
                            out=usf[:, a:b], in0=pp[:, 0:b - a], in1=vs_[:, a:b], op=ALU.mult), [pp, vs_], [us], part=(si > 0))
                    elif si == 0:
                        K.op("act", lambda e, pp=pp, a=a, b=b: e.copy(out=usf[:, a:b], in_=pp[:, 0:b - a]), [pp], [us], part=(si > 0))
                    else:
                        K.op("dve", lambda e, pp=pp, a=a, b=b: e.tensor_copy(out=usf[:, a:b], in_=pp[:, 0:b - a]), [pp], [us], part=(si > 0))

            def stC(c):
                us, dg, cp = c["us"], c["dg"], c["cp"]
                cp3 = cp[:, 0:N].rearrange("p (r c) -> p r c", c=64)
                for n_, t in enumerate([4, 0, 1, 2, 3, 5, 6, 7, 8]):
                    dy, dx = t // 3 - 1, t % 3 - 1
                    oc0, oc1 = max(0, -dx), 64 - max(0, dx)
                    K.op("pe", lambda e, t=t, dy=dy, dx=dx, oc0=oc0, oc1=oc1, n_=n_: e.matmul(
                        cp3[:, :, oc0:oc1], lhsT=dg[:, t, :], rhs=us[:, 1 + dy:1 + dy + nr, oc0 + dx:oc1 + dx],
                        start=(n_ == 0), stop=(n_ == 8)), [dg, us], [cp], part=(n_ > 0), inc=(n_ == 8))

            def stF(cv_, cg_):
                j = cv_["j"]
                sg = sgb[j % 2]
                K.op("act", lambda e: e.activation(
                    out=sg[:, 0:N], in_=cg_["cp"][:, 0:N], func=AF.Silu, bias=prm[:, cbo + 22 + j:cbo + 23 + j], scale=1.0), [cg_["cp"]], [sg])
                K.op("dve", lambda e: e.scalar_tensor_tensor(
                    out=hid[:, j, 0:N], in0=cv_["cp"][:, 0:N], scalar=prm[:, cbo + j:cbo + j + 1], in1=sg[:, 0:N], op0=ALU.add, op1=ALU.mult),
                    [cv_["cp"], sg], [hid], part=True)

            wi0 = wi
            cfg = [mk(i) for i in range(44)]
            wi += 44
            pre_steps = ffn_prep_steps(bi + 1) if bi + 1 < len(blocks) else []
            stU(cfg[0])
            for i in range(44):
                if i + 1 < 44:
                    stU(cfg[i + 1])
                if i == 1 and pend_store:
                    pend_store.pop(0)()
                stE(cfg[i])
                stC(cfg[i])
                if i % 2 == 1:
                    stF(cfg[i - 1], cfg[i])
                if bi + 1 < len(blocks) and i >= 4 and i % 2 == 0 and pre_steps:
                    pre_steps.pop(0)()
            while pre_steps:
                pre_steps.pop(0)()
            for m in range(8):
                w = wdn[m % 2]
                K.dma("sp", w[:], WDN[li][m], WDN[li], w, w)
                pp = PS[7] if m % 2 == 0 else PS[6]
                for j in range(22):
                    K.op("pe", lambda e, pp=pp, j=j, w=w: e.matmul(
                        pp[:, 0:N], lhsT=w[:, j, :], rhs=hid[:, j, 0:N], start=(j == 0), stop=(j == 21)), [w, hid], [pp], part=(j > 0), inc=(j == 21))
                K.op("dve", lambda e, pp=pp, m=m, h=h: e.scalar_tensor_tensor(
                    out=h[:, m, 64:64 + N], in0=pp[:, 0:N], scalar=VEC[:, V_G2 + 6 * li, m:m + 1], in1=h[:, m, 64:64 + N],
                    op0=ALU.mult, op1=ALU.add), [pp, h], [h], part=True)
            if not final:
                t_out = (r0 - rout0) * 64
                pend_store.append((lambda h=h, N=N, t_out=t_out: K.dma(
                    "sp", Hout[:, :, t_out:t_out + N], h[:, :, 64:64 + N], h, Hout, h, part=True)))
                if bi + 1 == len(blocks):
                    pend_store.pop(0)()
            else:
                K.op("act", lambda e, h=h: e.activation(out=sqx[:, :, 0:N], in_=h[:, :, 64:64 + N], func=AF.Square), [h], [sqx])
                for c in range(8):
                    K.op("pe", lambda e, c=c: e.matmul(PS[7][:, 0:N], lhsT=onesB[:], rhs=sqx[:, c, 0:N], start=(c == 0), stop=(c == 7)),
                         [sqx], [PS[7]], part=(c > 0), inc=(c == 7))
                K.op("act", lambda e: e.activation(out=rt_[:, 0:N], in_=PS[7][:, 0:N], func=AF.Sqrt, scale=1.0 / D, bias=epsb[:, 0:1]),
                     [PS[7]], [rt_])
                K.op("dve", lambda e: e.reciprocal(out=rstd_[:, 0:N], in_=rt_[:, 0:N]), [rt_], [rstd_])
                for c in range(8):
                    K.op("dve", lambda e, c=c, h=h: e.scalar_tensor_tensor(
                        out=yf[:, c, 0:N], in0=h[:, c, 64:64 + N], scalar=P("norm_final", 1, c), in1=rstd_[:, 0:N],
                        op0=ALU.mult, op1=ALU.mult), [h, rstd_], [yf], part=True)
                for tt in range(N // 128):
                    o = ost[tt % 2]
                    for hf in range(2):
                        pp = PS[hf]
                        for c4 in range(4):
                            c = hf * 4 + c4
                            K.op("pe", lambda e, c=c, c4=c4, tt=tt, pp=pp: e.transpose(
                                out=pp[:, c4 * 128:(c4 + 1) * 128], in_=yf[:, c, tt * 128:(tt + 1) * 128], identity=identF[:]),
                                [yf], [pp], part=(c4 > 0))
                        if hf == 0:
                            K.op("act", lambda e, o=o, pp=pp: e.copy(out=o[:, 0:512], in_=pp[:, :]), [pp], [o], part=True)
                        else:
                            K.op("dve", lambda e, o=o, pp=pp: e.tensor_copy(out=o[:, 512:1024], in_=pp[:, :]), [pp], [o], part=True)
                    tok = r0 * 64 + tt * 128
                    K.dma("sp", out_d[tok:tok + 128, :], o[:], o, out_d, o, part=True)
        S.close()

    def conf_stage(Hin, rin0, Hout, rout0, blocks):
        S = Stage(K, "c")
        w1 = S.sb("w1", [128, 8, 2 * D], BF16)
        for hf in range(2):
            K.dma("pool", w1[:, :, hf * D:(hf + 1) * D], cf_w1[:, hf * D:(hf + 1) * D].rearrange("(k p) m -> p k m", p=128),
                  cf_w1, w1, w1, part=(hf > 0))
        w2 = S.sb("w2", [128, 8, D], BF16)
        K.dma("pool", w2[:], cf_w2[:, :].rearrange("(k p) m -> p k m", p=128), cf_w2, w2, w2)
        dgc = [S.sb("dgc%d" % i, [128, 31, 128], BF16) for i in range(2)]
        NIM = 542
        hb2 = [S.sb("h%d" % i, [128, 8, NIM], F32) for i in range(2)]
        sqb2 = [S.sb("sqx%d" % i, [128, 8, NIM], BF16) for i in range(2)]
        rt_ = S.sb("rt", [128, NIM], F32)
        rstd_ = S.sb("rstd", [128, NIM], F32)
        rtn_ = S.sb("rtn", [128, NIM], F32)
        rstdn_ = S.sb("rstdn", [128, NIM], F32)
        tmpf = [S.sb("tmp%d" % i, [128, NIM], F32) for i in range(2)]
        vsb2 = [S.sb("vs%d" % i, [128, NIM], F32) for i in range(2)]
        sgb = [S.sb("sg%d" % i, [128, NIM], F32) for i in range(2)]
        glu = S.sb("glu", [128, 8, NIM], BF16)
        cv = S.sb("cv", [128, 8, 512], F32)
        act_ = S.sb("act", [128, 8, 512], BF16)
        dsq = act_
        hn = cv
        gsr = lambda c: VEC[:, V_GS1 + 6, c:c + 1]
        shr = lambda c: VEC[:, V_SH1 + 6, c:c + 1]

        def conf_prep_steps(bi):
            r0, r1 = blocks[bi]
            NI = (r1 - r0) * 64 + 30
            h, xmq, vs_ = hb2[bi % 2], sqb2[bi % 2], vsb2[bi % 2]
            t_in = (r0 - rin0) * 64 - 15
            segs_ = [(0, min(NI, 512))] + ([(512, NI)] if NI > 512 else [])
            st = []

            def s_load():
                K.dma("sp", h[:, :, 0:NI], Hin[:, :, t_in:t_in + NI], Hin, h, h)
                if (r0 - 1 < 0) or (r1 + 1 > 64):
                    xi = (r0 + 3) * 64 + 2 - 15
                    K.dma("sp", vs_[:, 0:NI], valid[:, xi:xi + NI], valid, vs_, vs_)
            st.append(s_load)
            st.append(lambda: K.op("act", lambda e: e.activation(out=xmq[:, :, 0:NI], in_=h[:, :, 0:NI], func=AF.Square), [h], [xmq]))
            for (a, b) in segs_:
                def s_ss(a=a, b=b):
                    for c in range(8):
                        K.op("pe", lambda e, c=c: e.matmul(PS[7][:, 0:b - a], lhsT=onesB[:], rhs=xmq[:, c, a:b], start=(c == 0), stop=(c == 7)),
                             [xmq], [PS[7]], part=(c > 0), inc=(c == 7))
                    K.op("act", lambda e: e.activation(
                        out=rtn_[:, a:b], in_=PS[7][:, 0:b - a], func=AF.Sqrt, scale=1.0 / D, bias=epsb[:, 0:1]), [PS[7]], [rtn_], part=True)
                st.append(s_ss)
            st.append(lambda: K.op("dve", lambda e: e.reciprocal(out=rstdn_[:, 0:NI], in_=rtn_[:, 0:NI]), [rtn_], [rstdn_]))
            for c in range(8):
                def s_mod(c=c):
                    tb = tmpf[c % 2]
                    K.op("dve", lambda e: e.tensor_tensor(out=tb[:, 0:NI], in0=h[:, c, 0:NI], in1=rstdn_[:, 0:NI], op=ALU.mult), [h, rstdn_], [tb])
                    K.op("act", lambda e: e.activation(out=xmq[:, c, 0:NI], in_=tb[:, 0:NI], func=AF.Identity, scale=gsr(c), bias=shr(c)),
                         [tb], [xmq], part=True)
                st.append(s_mod)
            return st

        for f_ in conf_prep_steps(0):
            f_()
        for bi, (r0, r1) in enumerate(blocks):
            N = (r1 - r0) * 64
            NI = N + 30
            h, sqx, vs_ = hb2[bi % 2], sqb2[bi % 2], vsb2[bi % 2]
            edge = (r0 - 1 < 0) or (r1 + 1 > 64)
            pre_steps = conf_prep_steps(bi + 1) if bi + 1 < len(blocks) else []
            segs = [(0, min(NI, 512))] + ([(512, NI)] if NI > 512 else [])
            for j in range(8):
                pa = [PS[0], PS[1]]
                pg = [PS[2], PS[3]]
                if j % 2 == 1:
                    pa, pg = [PS[4], PS[5]], [PS[6], PS[7]]
                for (pset, oc) in ((pg, 8 + j), (pa, j)):
                    for si, (a, b) in enumerate(segs):
                        for k in range(8):
                            K.op("pe", lambda e, pset=pset, si=si, a=a, b=b, k=k, oc=oc: e.matmul(
                                pset[si][:, 0:b - a], lhsT=w1[:, k, oc * 128:(oc + 1) * 128], rhs=sqx[:, k, a:b],
                                start=(k == 0), stop=(k == 7)), [w1, sqx], [pset[si]], part=(k > 0), inc=(k == 7))
                sg = sgb[j % 2]
                for si, (a, b) in enumerate(segs):
                    K.op("act", lambda e, si=si, a=a, b=b, sg=sg, pg=pg, j=j: e.activation(
                        out=sg[:, a:b], in_=pg[si][:, 0:b - a], func=AF.Sigmoid, bias=P("cf_b1", 1, 8 + j), scale=1.0),
                        [pg[si]], [sg], part=(si > 0))
                    K.op("dve", lambda e, si=si, a=a, b=b, sg=sg, pa=pa, j=j: e.scalar_tensor_tensor(
                        out=glu[:, j, a:b], in0=pa[si][:, 0:b - a], scalar=P("cf_b1", 1, j), in1=sg[:, a:b], op0=ALU.add, op1=ALU.mult),
                        [pa[si], sg], [glu], part=True)
                if edge:
                    K.op("dve", lambda e, j=j: e.tensor_tensor(out=glu[:, j, 0:NI], in0=glu[:, j, 0:NI], in1=vs_[:, 0:NI], op=ALU.mult),
                         [glu, vs_], [glu], part=True)
                if pre_steps:
                    pre_steps.pop(0)()
            for j in range(8):
                pp = PS[j % 4]
                dgt = dgc[j % 2]
                K.dma("sp", dgt[:], DG31[j], DG31, dgt, dgt)
                for t in range(31):
                    K.op("pe", lambda e, j=j, t=t, pp=pp, dgt=dgt: e.matmul(
                        pp[:, 0:N], lhsT=dgt[:, t, :], rhs=glu[:, j, t:t + N], start=(t == 0), stop=(t == 30)),
                        [dgt, glu], [pp], part=(t > 0), inc=(t == 30))
                K.op("act", lambda e, j=j, pp=pp: e.activation(out=cv[:, j, 0:N], in_=pp[:, 0:N], func=AF.Identity,
                                                              bias=P("cf_conv_b", 1, j), scale=1.0), [pp], [cv], part=True)
                if pre_steps:
                    pre_steps.pop(0)()
            while pre_steps:
                pre_steps.pop(0)()
            for j in range(8):
                K.op("pe", lambda e, j=j: e.matmul(PS[4][:, 0:N], lhsT=onesF[:], rhs=cv[:, j, 0:N], start=(j == 0), stop=(j == 7)),
                     [cv], [PS[4]], part=(j > 0), inc=(j == 7))
            for j in range(8):
                K.op("dve", lambda e, j=j: e.scalar_tensor_tensor(
                    out=cv[:, j, 0:N], in0=PS[4][:, 0:N], scalar=-1.0 / D, in1=cv[:, j, 0:N], op0=ALU.mult, op1=ALU.add),
                    [PS[4], cv], [cv], part=True)
            K.op("act", lambda e: e.activation(out=dsq[:, :, 0:N], in_=cv[:, :, 0:N], func=AF.Square), [cv], [dsq])
            for j in range(8):
                K.op("pe", lambda e, j=j: e.matmul(PS[5][:, 0:N], lhsT=onesB[:], rhs=dsq[:, j, 0:N], start=(j == 0), stop=(j == 7)),
                     [dsq], [PS[5]], part=(j > 0), inc=(j == 7))
            K.op("act", lambda e: e.activation(out=rt_[:, 0:N], in_=PS[5][:, 0:N], func=AF.Sqrt, scale=1.0 / D, bias=epsb[:, 0:1]),
                 [PS[5]], [rt_])
            K.op("dve", lambda e: e.reciprocal(out=rstd_[:, 0:N], in_=rt_[:, 0:N]), [rt_], [rstd_])
            for j in range(8):
                K.op("dve", lambda e, j=j: e.tensor_tensor(out=cv[:, j, 0:N], in0=cv[:, j, 0:N], in1=rstd_[:, 0:N], op=ALU.mult),
                     [cv, rstd_], [cv], part=True)
                K.op("act", lambda e, j=j: e.activation(out=act_[:, j, 0:N], in_=cv[:, j, 0:N], func=AF.Silu,
                                                       scale=P("cf_ln_g", 1, j), bias=P("cf_ln_b", 1, j)), [cv], [act_], part=True)
            for m in range(8):
                pp = PS[m % 4]
                for j in range(8):
                    K.op("pe", lambda e, m=m, j=j, pp=pp: e.matmul(
                        pp[:, 0:N], lhsT=w2[:, j, m * 128:(m + 1) * 128], rhs=act_[:, j, 0:N], start=(j == 0), stop=(j == 7)),
                        [w2, act_], [pp], part=(j > 0), inc=(j == 7))
                K.op("act", lambda e, m=m, pp=pp: e.activation(out=hn[:, m, 0:N], in_=pp[:, 0:N], func=AF.Identity,
                                                              scale=VEC[:, V_G1 + 6, m:m + 1], bias=VEC[:, V_BG, m:m + 1]),
                     [pp], [hn], part=True)
                K.op("dve", lambda e, m=m: e.tensor_tensor(out=hn[:, m, 0:N], in0=hn[:, m, 0:N], in1=h[:, m, 15:15 + N], op=ALU.add),
                     [hn, h], [hn], part=True)
            t_out = (r0 - rout0) * 64
            K.dma("sp", Hout[:, :, t_out:t_out + N], hn[:, :, 0:N], hn, Hout, hn, part=True)
        S.close()

    own8 = [(8 * i, 8 * i + 8) for i in range(8)]
    ffn_stage(0, HA, -3, HB, -2, [(-2, 4)] + [(4 + 8 * i, 12 + 8 * i) for i in range(7)] + [(60, 66)], False)
    if upto == "ffn0":
        K.final_wait("sp")
        return nc
    conf_stage(HB, -2, HC, -1, [(-1 + 8 * i, 7 + 8 * i) for i in range(8)] + [(63, 65)])
    if upto == "conf":
        K.final_wait("sp")
        return nc
    ffn_stage(1, HC, -1, None, 0, own8, True)
    K.final_wait("sp")
    G.close()
    return nc


def _prep_core_inputs(core, inp, gate_tiles):
    b, k = core // 4, core % 4
    T0 = 4096 * k
    x = inp["x"]
    xin = np.zeros((NTX, D), np.float32)
    g0 = T0 - 194
    lo, hi = max(g0, 0), min(g0 + NTX, SEQ)
    xin[lo - g0:hi - g0] = x[b, lo:hi]
    valid = np.zeros((128, NTX), np.float32)
    valid[:, lo - g0:hi - g0] = 1.0
    cin = np.zeros((CTX + 3, D), np.float32)
    cin[2:2 + CTX] = inp["ctx"][b]
    cvalid = np.zeros((128, CTX + 3), np.float32)
    cvalid[:, 2:2 + CTX] = 1.0
    prm = np.zeros((128, PC), np.float32)

    def put(name, arr):
        arr = np.asarray(arr, np.float32).reshape(128, -1)
        prm[:, _off[name]:_off[name] + arr.shape[1]] = arr
    cT = np.stack([fm(inp["c"][b], 8), fm(inp["c_ctx"], 8)], axis=-1)
    put("cT", cT)
    put("ada_b", np.stack([fm(inp["ada_b"][i], 48) for i in range(2)], axis=1))
    put("norm_mix", np.stack([fm(inp["norm_mix"][i], 8) for i in range(2)], axis=1))
    put("norm_ffn", np.stack([fm(inp["norm_ffn"][i], 8) for i in range(2)], axis=1))
    put("norm_final", fm(inp["norm_final"], 8))
    put("rg_conv_w", np.stack([fm(inp["rg_conv_w"][0, t], 10) for t in range(4)], axis=-1))
    put("rg_conv_b", fm(inp["rg_conv_b"][0], 10))
    put("rg_ba", np.stack([fm(inp["rg_ba"][0, d].reshape(-1), 10) for d in range(2)], axis=1))
    put("rg_bx", np.stack([fm(inp["rg_bx"][0, d].reshape(-1), 10) for d in range(2)], axis=1))
    put("rg_lam", np.stack([fm(inp["rg_lam"][0, d], 10) for d in range(2)], axis=1))
    put("cf_b1", fm(inp["cf_b_pw1"][0], 16))
    put("cf_conv_w", np.stack([fm(inp["cf_conv_w"][0, t], 8) for t in range(31)], axis=-1))
    put("cf_conv_b", fm(inp["cf_conv_b"][0], 8))
    put("cf_ln_g", fm(inp["cf_ln_g"][0], 8))
    put("cf_ln_b", fm(inp["cf_ln_b"][0], 8))
    put("cf_b2", fm(inp["cf_b_pw2"][0], 8))
    cw = np.stack([np.stack([fm(inp["ffn_conv_w"][i].reshape(9, -1)[t], 44) for t in range(9)], axis=-1)
                   for i in range(2)], axis=1)
    put("ffn_cw", cw)
    put("ffn_cb", np.stack([fm(inp["ffn_conv_b"][i], 44) for i in range(2)], axis=1))
    m = {"xin": xin, "cin": cin, "valid": valid, "cvalid": cvalid}
    xoth = np.zeros((3, 4099, D), np.float32)
    ovalid = np.zeros((3, 128, 4099), np.float32)
    gwo = np.zeros((3, 2, NGT, 128, 128), np.float32)
    fmk = np.zeros((128, 2, 3, NBO), np.float32)
    o_lam = np.zeros((128, 3, 10), np.float32)
    o_ba = np.zeros((128, 3, 10), np.float32)
    o_bx = np.zeros((128, 3, 10), np.float32)
    for o in range(3):
        ko = (k + 1 + o) % 4
        dsel = 0 if ko < k else 1
        g0 = 4096 * ko - 2
        lo, hi = max(g0, 0), min(g0 + 4099, SEQ)
        xoth[o, lo - g0:hi - g0] = x[b, lo:hi]
        ovalid[o, :, lo - g0:hi - g0] = 1.0
        gwo[o, 0] = gate_tiles[dsel * 2 + 0]
        gwo[o, 1] = gate_tiles[dsel * 2 + 1]
        o_lam[:, o] = fm(inp["rg_lam"][0, dsel], 10)
        o_ba[:, o] = fm(inp["rg_ba"][0, dsel].reshape(-1), 10)
        o_bx[:, o] = fm(inp["rg_bx"][0, dsel].reshape(-1), 10)
        for j in range(NBO):
            if ko < k and (ko < k - 1 or j <= NBO - 2):
                fmk[:, 0, o, j] = 1.0
            if ko > k and (ko > k + 1 or j >= 1):
                fmk[:, 1, o, j] = 1.0
    put("o_lam", o_lam)
    put("o_ba", o_ba)
    put("o_bx", o_bx)
    m["prm"] = prm
    m["xoth"] = xoth
    m["ovalid"] = ovalid
    m["gwo"] = gwo.reshape(3 * 2 * NGT, 128, 128)
    m["fmask"] = fmk.reshape(128, 2, 3 * NBO)
    return m


def _shared_inputs(inp):
    gw = np.zeros((4, NGT, 128, 128), np.float32)
    mats = [inp["rg_wa"][0, 0], inp["rg_wx"][0, 0], inp["rg_wa"][0, 1], inp["rg_wx"][0, 1]]
    for gi, w in enumerate(mats):
        dense = np.zeros((DR, DR), np.float32)
        for h in range(16):
            dense[h * 80:(h + 1) * 80, h * 80:(h + 1) * 80] = w[h]
        for ti, (j, i) in enumerate(GT):
            gw[gi, ti] = dense[i * 128:(i + 1) * 128, j * 128:(j + 1) * 128]
    return gw, {
        "ident": np.eye(128, dtype=np.float32),
        "ada_w": np.ascontiguousarray(inp["ada_w"], np.float32),
        "rg_w_in": np.ascontiguousarray(inp["rg_w_in"][0], np.float32),
        "gw": gw.reshape(4 * NGT, 128, 128),
        "rg_w_out": np.ascontiguousarray(inp["rg_w_out"][0], np.float32),
        "cf_w1": np.ascontiguousarray(inp["cf_w_pw1"][0], np.float32),
        "cf_w2": np.ascontiguousarray(inp["cf_w_pw2"][0], np.float32),
        "ffn_up": np.ascontiguousarray(inp["ffn_w_up"], np.float32),
        "ffn_dn": np.ascontiguousarray(inp["ffn_w_down"], np.float32),
    }


_SUM_KEYS = ["xin", "cin", "valid", "cvalid", "prm", "ident", "ada_w", "rg_w_in", "gw"]
_PROG = {}


def _prog(mode):
    if mode not in _PROG:
        _PROG[mode] = build_program(mode)
    return _PROG[mode]


def kernel(**inputs):
    inp = {k: np.asarray(v) for k, v in inputs.items()}
    gate_tiles, shared = _shared_inputs(inp)
    percore = [_prep_core_inputs(c, inp, gate_tiles) for c in range(8)]
    cores = list(range(8))
    in2 = [{**shared, **percore[c]} for c in cores]
    r2 = run_bass_kernel_spmd(_prog("solo"), in2, core_ids=cores)
    out = np.zeros((2, SEQ, D), np.float32)
    for c in cores:
        b, k = c // 4, c % 4
        out[b, 4096 * k:4096 * (k + 1)] = np.asarray(r2.results[c]["out"])
    return out
```

```python
import numpy as np
from contextlib import ExitStack
import concourse.bass as bass
import concourse.mybir as mybir
from concourse.bass_utils import run_bass_kernel_spmd

F32, BF16 = mybir.dt.float32, mybir.dt.bfloat16
AF = mybir.ActivationFunctionType
ALU = mybir.AluOpType

D = 1024
SEQ = 16384
CTX = 256
DR = 1280
DFF = 2816
EPS = 1e-6
NTX = 4483
NTE = 4480
RG_ROWS = [5, 6, 6, 6, 6, 6, 6, 6, 6, 6, 6, 5]
OTH_ROWS = [3, 5, 6, 6, 6, 6, 6, 6, 6, 6, 5, 3]
NBL = len(RG_ROWS)
NBO = len(OTH_ROWS)

def _gate_tiles():
    tiles = []
    for j in range(10):
        heads = set(range((j * 128) // 80, ((j + 1) * 128 - 1) // 80 + 1))
        ins = set()
        for h in heads:
            for ch in (h * 80, h * 80 + 79):
                ins.add(ch // 128)
        for i in sorted(ins):
            tiles.append((j, i))
    return tiles
GT = _gate_tiles()
NGT = len(GT)

_off = {}
def _alloc_cols():
    o = 0
    for name, n in [("cT", 16), ("ada_b", 96), ("norm_mix", 16), ("norm_ffn", 16), ("norm_final", 8),
                    ("rg_conv_w", 40), ("rg_conv_b", 10), ("rg_ba", 20), ("rg_bx", 20), ("rg_lam", 20), ("o_lam", 30), ("o_ba", 30), ("o_bx", 30),
                    ("cf_b1", 16), ("cf_conv_w", 248), ("cf_conv_b", 8), ("cf_ln_g", 8), ("cf_ln_b", 8),
                    ("cf_b2", 8), ("ffn_cw", 792), ("ffn_cb", 88)]:
        _off[name] = o
        o += n
    return o
PC = _alloc_cols()


def fm(v, nch):
    return np.ascontiguousarray(np.asarray(v, np.float32).reshape(nch, 128).T)


class Buf:
    def __init__(self, t, name):
        self.t = t
        self.name = name
        self.w = {}
        self.r = {}
        self.ds = None

    def __getitem__(self, i):
        return self.t[i]


class Ctx:
    def __init__(self, nc, ndsem=64):
        self.nc = nc
        self.E = {"pe": nc.tensor, "act": nc.scalar, "dve": nc.vector, "pool": nc.gpsimd, "sp": nc.sync}
        self.sem = {e: nc.alloc_semaphore("s_" + e) for e in self.E}
        self.cnt = {e: 0 for e in self.E}
        self.known = {e: {} for e in self.E}
        self.dpool = [[nc.alloc_semaphore("d%d" % i), 0, "d%d" % i] for i in range(ndsem)]
        self.dfree = list(range(ndsem))
        self.nops = 0
        self.extra = []

    def dram(self, ap, name):
        return Buf(ap, name)

    def _deps(self, reads, writes, part):
        deps = {}

        def add(d):
            for k, (s, v) in d.items():
                if k not in deps or deps[k][1] < v:
                    deps[k] = (s, v)
        for b in reads:
            add(b.w)
        for b in writes:
            if not part:
                add(b.w)
            add(b.r)
        return deps

    def _wait(self, e, deps):
        for k, (s, v) in deps.items():
            if k == "pe" and e == "pe":
                continue
            if self.known[e].get(k, 0) >= v:
                continue
            self.E[e].wait_ge(s, v)
            self.known[e][k] = v

    def op(self, e, fn, reads=(), writes=(), part=False, inc=True):
        self._wait(e, self._deps(reads, writes, part))
        ins = fn(self.E[e])
        if inc:
            self.cnt[e] += 1
            ins.then_inc(self.sem[e], 1)
            tag = (self.sem[e], self.cnt[e])
        else:
            tag = (self.sem[e], self.cnt[e] + 1)
        for b in reads:
            b.r[e] = tag
        for b in writes:
            b.w[e] = tag
        self.nops += 1
        return ins

    def _dsem(self, b):
        if b.ds is None:
            b.ds = self.dfree.pop(0)
        return self.dpool[b.ds]

    def release(self, bufs):
        for b in bufs:
            if b.ds is not None:
                self.dfree.append(b.ds)
                b.ds = None

    def dma(self, q, out, in_, src, dst, owner, part=False):
        self._wait(q, self._deps([src], [dst], part))
        ent = self._dsem(owner)
        ent[1] += 16
        self.E[q].dma_start(out=out, in_=in_).then_inc(ent[0], 16)
        tag = (ent[0], ent[1])
        src.r[ent[2]] = tag
        dst.w[ent[2]] = tag

    def all_gather(self, src, dst, groups):
        self._wait("pool", self._deps([src], [dst], False))
        sem = self.nc.alloc_semaphore("cc%d" % len(self.extra))
        ins = self.E["pool"].collective_compute("AllGather", mybir.AluOpType.bypass, replica_groups=groups,
                                                ins=[src.t.opt()], outs=[dst.t.opt()])
        ins.then_inc(sem)
        key = "cc%d" % len(self.extra)
        self.extra.append([sem, 1, key])
        src.r[key] = (sem, 1)
        dst.w[key] = (sem, 1)

    def barrier(self):
        for e in self.E:
            deps = {}
            for ent in self.extra:
                deps[ent[2]] = (ent[0], ent[1])
            for e2 in self.E:
                if e2 != e and self.cnt[e2] > 0:
                    deps[e2] = (self.sem[e2], self.cnt[e2])
            for ent in self.dpool:
                if ent[1] > 0:
                    deps[ent[2]] = (ent[0], ent[1])
            for k, (s, v) in deps.items():
                if self.known[e].get(k, 0) >= v:
                    continue
                self.E[e].wait_ge(s, v)
                self.known[e][k] = v

    def final_wait(self, e="sp"):
        for ent in self.dpool:
            if ent[1] > 0 and self.known[e].get(ent[2], 0) < ent[1]:
                self.E[e].wait_ge(ent[0], ent[1])
                self.known[e][ent[2]] = ent[1]


class Stage:
    def __init__(self, K, name):
        self.K = K
        self.name = name
        self.stack = ExitStack()
        self.bufs = []
        self.n = 0

    def sb(self, name, shape, dt):
        self.n += 1
        t = self.stack.enter_context(self.K.nc.sbuf_tensor("%s_%s_%d" % (self.name, name, self.n), list(shape), dt))
        b = Buf(t, name)
        self.bufs.append(b)
        return b

    def close(self):
        self.K.barrier()
        self.K.release(self.bufs)
        self.stack.close()


def build_program(mode, upto="all", debug=False):
    nc = bass.Bass("TRN2", target_bir_lowering=False)
    K = Ctx(nc)

    def din(name, shape, dt=F32):
        return K.dram(nc.dram_tensor(name, list(shape), dt, kind="ExternalInput").ap(), name)

    def dint(name, shape, dt=F32):
        kind = "ExternalOutput" if (debug and name in ("HA", "HB", "HC")) else "Internal"
        return K.dram(nc.dram_tensor(name, list(shape), dt, kind=kind).ap(), name)

    def dout(name, shape, dt=F32):
        return K.dram(nc.dram_tensor(name, list(shape), dt, kind="ExternalOutput").ap(), name)

    xin = din("xin", [NTX, D])
    cin = din("cin", [CTX + 3, D])
    valid = din("valid", [128, NTX])
    cvalid = din("cvalid", [128, CTX + 3])
    prm_d = din("prm", [128, PC])
    ident_d = din("ident", [128, 128])
    ada_w = din("ada_w", [2, D, 6 * D])
    rg_w_in = din("rg_w_in", [D, 2 * DR])
    gw_d = din("gw", [4 * NGT, 128, 128])
    if mode != "sum":
        rg_w_out = din("rg_w_out", [DR, D])
        cf_w1 = din("cf_w1", [D, 2 * D])
        cf_w2 = din("cf_w2", [D, D])
        ffn_up = din("ffn_up", [2, D, 2 * DFF])
        ffn_dn = din("ffn_dn", [2, DFF, D])
        fmask = din("fmask", [128, 2, (3 if mode == "solo" else 8) * NBO])
        out_d = dout("out", [4096, D])
    SUMW = 2 * 2 * 10 * NBO
    NS = 3 if mode == "solo" else 8
    if mode == "solo":
        xoth = din("xoth", [3, 4099, D])
        ovalid = din("ovalid", [3, 128, 4099])
        gwo_d = din("gwo", [3 * 2 * NGT, 128, 128])
    if mode == "sum":
        sum_out = dout("sums", [128, SUMW])
    elif mode == "main":
        sumg_d = din("sumg", [128, 8, SUMW])
    elif mode == "solo":
        pass
    else:
        sum_loc_d = K.dram(nc.dram_tensor("sum_loc", [128, SUMW], F32).ap(), "sum_loc")
        sumg_all = K.dram(nc.dram_tensor("sumg_all", [8 * 128, SUMW], F32).ap(), "sumg_all")

    if mode != "sum":
        SAB = dint("SAB", [10, 128, 4, NTE])
        XM = dint("XM", [128, 8, NTE], BF16)
        HX = dint("HX", [128, 8, NTE])
        HA = dint("HA", [128, 8, NTE])
        HB = dint("HB", [128, 8, 68 * 64])
        HC = dint("HC", [128, 8, 66 * 64])
        WUP = [dint("WUP%d" % i, [44, 128, 8, 128], BF16) for i in range(2)]
        WDN = [dint("WDN%d" % i, [8, 128, 22, 128], BF16) for i in range(2)]
        DG = [dint("DG%d" % i, [44, 128, 9, 128], BF16) for i in range(2)]
        DG31 = dint("DG31", [8, 128, 31, 128], BF16)

    PS = [Buf(nc.alloc_psum_tensor("ps%d" % i, [128, 512], F32), "ps%d" % i) for i in range(8)]

    G = Stage(K, "g")
    prm = G.sb("prm", [128, PC], F32)
    identF = G.sb("identF", [128, 128], F32)
    identB = G.sb("identB", [128, 128], BF16)
    onesB = G.sb("onesB", [128, 128], BF16)
    onesF = G.sb("onesF", [128, 128], F32)
    MOD = G.sb("MOD", [128, 2, 2, 6, 8], F32)
    VEC = G.sb("VEC", [128, 16, 8], F32)
    SUML = G.sb("SUML", [128, 2, 2, 10, NBL], F32)
    HCTX = G.sb("HCTX", [128, 2, 10], F32)
    HIN = G.sb("HIN", [128, 2, 10, NBL + 1], F32)
    SCV = G.sb("SCV", [128, 6, 10], F32)
    HBG = G.sb("HBG", [128, 4, 10], F32)
    SCVO = G.sb("SCVO", [128, 6, 10], F32)
    HBGO = G.sb("HBGO", [128, 6, 10], F32)
    SUMO = G.sb("SUMO", [128, 3, 2, 2, 10, NBO], F32)
    accO = G.sb("accO", [128, 3, 10, NBO], F32)

    def P(name, n=None, i=0):
        o = _off[name] + i
        return prm[:, o:o + (n if n is not None else 1)]

    V_GS1, V_SH1, V_G1, V_GS2, V_SH2, V_G2 = 0, 1, 2, 3, 4, 5
    V_GSC, V_SHC, V_BG = 12, 13, 14

    K.dma("sp", prm[:], prm_d[:], prm_d, prm, prm)
    K.dma("sp", identF[:], ident_d[:], ident_d, identF, identF)
    K.op("dve", lambda e: e.tensor_copy(out=identB[:], in_=identF[:]), [identF], [identB])
    K.op("dve", lambda e: e.memset(onesB[:], 1.0), [], [onesB])
    K.op("dve", lambda e: e.memset(onesF[:], 1.0), [], [onesF])
    K.op("dve", lambda e: e.memset(SUML[:], 0.0), [], [SUML])

    S0 = Stage(K, "p")
    scb = S0.sb("scb", [128, 16], BF16)
    K.op("act", lambda e: e.activation(out=scb[:], in_=P("cT", 16), func=AF.Silu), [prm], [scb])
    adaw = [S0.sb("adaw%d" % i, [128, 8, 1024], BF16) for i in range(2)]
    it = 0
    for li in range(2):
        for q in range(6):
            wt = adaw[it % 2]
            K.dma("pool", wt[:], ada_w[li][:, q * 1024:(q + 1) * 1024].rearrange("(k p) m -> p k m", p=128),
                  ada_w, wt, wt)
            pp = PS[it % 2]
            for m in range(8):
                for k in range(8):
                    K.op("pe", lambda e, m=m, k=k, wt=wt, pp=pp: e.matmul(
                        pp[:, m * 2:m * 2 + 2], lhsT=wt[:, k, m * 128:(m + 1) * 128], rhs=scb[:, k * 2:k * 2 + 2],
                        start=(k == 0), stop=(k == 7)), [wt, scb], [pp], part=(k > 0 or m > 0), inc=(k == 7))
            for j in range(2):
                K.op("dve", lambda e, j=j, li=li, q=q, pp=pp: e.tensor_tensor(
                    out=MOD[:, li, j, q, :], in0=pp[:, j:16:2], in1=P("ada_b", 8, li * 48 + q * 8), op=ALU.add),
                    [pp, prm], [MOD], part=True)
            it += 1
    for li in range(2):
        K.op("dve", lambda e, li=li: e.scalar_tensor_tensor(
            out=VEC[:, V_GS1 + 6 * li, :], in0=MOD[:, li, 0, 1, :], scalar=1.0, in1=P("norm_mix", 8, li * 8),
            op0=ALU.add, op1=ALU.mult), [MOD, prm], [VEC], part=True)
        K.op("dve", lambda e, li=li: e.scalar_tensor_tensor(
            out=VEC[:, V_GS2 + 6 * li, :], in0=MOD[:, li, 0, 4, :], scalar=1.0, in1=P("norm_ffn", 8, li * 8),
            op0=ALU.add, op1=ALU.mult), [MOD, prm], [VEC], part=True)
        for vrow, q in ((V_SH1, 0), (V_G1, 2), (V_SH2, 3), (V_G2, 5)):
            K.op("dve", lambda e, li=li, vrow=vrow, q=q: e.tensor_copy(out=VEC[:, vrow + 6 * li, :], in_=MOD[:, li, 0, q, :]),
                 [MOD], [VEC], part=True)
    K.op("dve", lambda e: e.scalar_tensor_tensor(
        out=VEC[:, V_GSC, :], in0=MOD[:, 0, 1, 1, :], scalar=1.0, in1=P("norm_mix", 8, 0),
        op0=ALU.add, op1=ALU.mult), [MOD, prm], [VEC], part=True)
    K.op("dve", lambda e: e.tensor_copy(out=VEC[:, V_SHC, :], in_=MOD[:, 0, 1, 0, :]), [MOD], [VEC], part=True)
    K.op("dve", lambda e: e.tensor_tensor(out=VEC[:, V_BG, :], in0=P("cf_b2", 8), in1=MOD[:, 1, 0, 2, :], op=ALU.mult),
         [MOD, prm], [VEC], part=True)
    tmp20 = S0.sb("tmp20", [128, 50], F32)
    ev20 = S0.sb("ev20", [128, 50], F32)
    q20 = S0.sb("q20", [128, 50], F32)
    K.op("act", lambda e: e.activation(out=ev20[:], in_=P("rg_lam", 50), func=AF.Exp, scale=-1.0), [prm], [ev20])
    K.op("dve", lambda e: e.tensor_scalar(out=q20[:], in0=ev20[:], scalar1=-1.0 / 7, scalar2=1.0 / 6, op0=ALU.mult, op1=ALU.add),
         [ev20], [q20])
    for cst in (1.0 / 5, 1.0 / 4, 1.0 / 3, 1.0 / 2, 1.0):
        K.op("dve", lambda e: e.tensor_tensor(out=q20[:], in0=q20[:], in1=ev20[:], op=ALU.mult), [q20, ev20], [q20])
        K.op("dve", lambda e, cst=cst: e.tensor_scalar(out=q20[:], in0=q20[:], scalar1=-1.0, scalar2=cst, op0=ALU.mult, op1=ALU.add),
             [q20], [q20])
    K.op("dve", lambda e: e.tensor_tensor(out=tmp20[:], in0=q20[:], in1=ev20[:], op=ALU.mult), [q20, ev20], [tmp20])
    K.op("dve", lambda e: e.tensor_scalar(out=SCV[:, 0:2, :], in0=tmp20[:, 0:20].rearrange("p (d c) -> p d c", d=2),
                                          scalar1=-4.0, scalar2=None, op0=ALU.mult), [tmp20], [SCV], part=True)
    K.op("dve", lambda e: e.tensor_scalar(out=SCV[:, 2:4, :], in0=tmp20[:, 0:20].rearrange("p (d c) -> p d c", d=2),
                                          scalar1=-8.0, scalar2=None, op0=ALU.mult), [tmp20], [SCV], part=True)
    K.op("dve", lambda e: e.tensor_scalar(out=SCVO[:, 0:3, :], in0=tmp20[:, 20:50].rearrange("p (d c) -> p d c", d=3),
                                          scalar1=-4.0, scalar2=None, op0=ALU.mult), [tmp20], [SCVO], part=True)
    K.op("dve", lambda e: e.tensor_scalar(out=SCVO[:, 3:6, :], in0=tmp20[:, 20:50].rearrange("p (d c) -> p d c", d=3),
                                          scalar1=-8.0, scalar2=None, op0=ALU.mult), [tmp20], [SCVO], part=True)
    for o in range(3):
        K.op("dve", lambda e, o=o: e.tensor_scalar(out=HBGO[:, o * 2, :], in0=P("o_ba", 10, o * 10), scalar1=0.5,
                                                     scalar2=None, op0=ALU.mult), [prm], [HBGO], part=True)
        K.op("dve", lambda e, o=o: e.tensor_scalar(out=HBGO[:, o * 2 + 1, :], in0=P("o_bx", 10, o * 10), scalar1=0.5,
                                                     scalar2=None, op0=ALU.mult), [prm], [HBGO], part=True)
    for d in range(2):
        K.op("dve", lambda e, d=d: e.tensor_scalar(out=HBG[:, d * 2, :], in0=P("rg_ba", 10, d * 10), scalar1=0.5,
                                                     scalar2=None, op0=ALU.mult), [prm], [HBG], part=True)
        K.op("dve", lambda e, d=d: e.tensor_scalar(out=HBG[:, d * 2 + 1, :], in0=P("rg_bx", 10, d * 10), scalar1=0.5,
                                                     scalar2=None, op0=ALU.mult), [prm], [HBG], part=True)
    if mode != "sum":
        dgst = [S0.sb("dgst%d" % i, [128, 9, 128], BF16) for i in range(3)]
        for li in range(2):
            cwo_ = _off["ffn_cw"] + li * 396
            for cc in range(44):
                dgs = dgst[cc % 3]
                for t in range(9):
                    K.op("dve", lambda e, t=t, dgs=dgs, cc=cc, cwo_=cwo_: e.tensor_scalar(
                        out=dgs[:, t, :], in0=identB[:], scalar1=prm[:, cwo_ + cc * 9 + t:cwo_ + cc * 9 + t + 1], scalar2=None,
                        op0=ALU.mult), [identB, prm], [dgs], part=(t > 0))
                K.dma("sp", DG[li][cc], dgs[:], dgs, DG[li], dgs, part=True)
    if mode != "sum":
        dg31s = [S0.sb("dg31s%d" % i, [128, 31, 128], BF16) for i in range(2)]
        for c in range(8):
            dgs = dg31s[c % 2]
            for t in range(31):
                K.op("dve", lambda e, c=c, t=t, dgs=dgs: e.tensor_scalar(
                    out=dgs[:, t, :], in0=identB[:], scalar1=P("cf_conv_w", 1, c * 31 + t), scalar2=None, op0=ALU.mult),
                    [identB, prm], [dgs], part=(t > 0))
            K.dma("sp", DG31[c], dgs[:], dgs, DG31, dgs, part=True)
    S0.close()

    def norm_mod(S, h, n, xm, gsrow, shrow, sq, ssb, rt, rstd, tmp, out_f32=None):
        K.op("act", lambda e: e.activation(out=sq[:, :, 0:n], in_=h[:, :, 0:n], func=AF.Square), [h], [sq])
        nseg = [(0, min(n, 512))] + ([(512, n)] if n > 512 else [])
        for si, (a, b) in enumerate(nseg):
            for c in range(8):
                K.op("pe", lambda e, c=c, a=a, b=b, si=si: e.matmul(
                    ssb[si][:, 0:b - a], lhsT=onesB[:], rhs=sq[:, c, a:b], start=(c == 0), stop=(c == 7)),
                    [sq], [ssb[si]], part=(c > 0), inc=(c == 7))
            K.op("act", lambda e, a=a, b=b, si=si: e.activation(
                out=rt[:, a:b], in_=ssb[si][:, 0:b - a], func=AF.Sqrt, scale=1.0 / D, bias=epsb[:, 0:1]),
                [ssb[si]], [rt], part=True)
        K.op("dve", lambda e: e.reciprocal(out=rstd[:, 0:n], in_=rt[:, 0:n]), [rt], [rstd])
        for c in range(8):
            tb = tmp[c % 2]
            K.op("dve", lambda e, c=c, tb=tb: e.tensor_tensor(out=tb[:, 0:n], in0=h[:, c, 0:n], in1=rstd[:, 0:n], op=ALU.mult),
                 [h, rstd], [tb])
            if out_f32 is None:
                K.op("act", lambda e, c=c, tb=tb: e.activation(
                    out=xm[:, c, 0:n], in_=tb[:, 0:n], func=AF.Identity, scale=gsrow(c), bias=shrow(c)),
                    [tb], [xm], part=True)
            else:
                K.op("act", lambda e, c=c, tb=tb: e.activation(
                    out=out_f32[:, c, 0:n], in_=tb[:, 0:n], func=AF.Identity, scale=gsrow(c)),
                    [tb], [out_f32], part=True)

    epsb = G.sb("epsb", [128, 2], F32)
    K.op("dve", lambda e: e.memset(epsb[:, 0:1], EPS), [], [epsb], part=True)
    K.op("dve", lambda e: e.memset(epsb[:, 1:2], 1.0), [], [epsb], part=True)

    S1 = Stage(K, "r1")
    w_in = S1.sb("w_in", [128, 8, DR], BF16)
    K.dma("pool", w_in[:], rg_w_in[:, DR:2 * DR].rearrange("(k p) m -> p k m", p=128), rg_w_in, w_in, w_in)
    gwt = S1.sb("gw", [128, 4 * NGT, 128], BF16)
    K.dma("pool", gwt[:], gw_d[:, :, :].rearrange("t p m -> p t m"), gw_d, gwt, gwt)
    dg4 = S1.sb("dg4", [128, 40, 128], BF16)
    for c in range(10):
        for t in range(4):
            K.op("dve", lambda e, c=c, t=t: e.tensor_scalar(
                out=dg4[:, c * 4 + t, :], in0=identB[:], scalar1=P("rg_conv_w", 1, c * 4 + t), scalar2=None, op0=ALU.mult),
                [identB, prm], [dg4], part=True)
    NM = 387
    xt4 = S1.sb("xt4", [128, 4, D], F32)
    hb_ = S1.sb("h", [128, 8, NM], F32)
    xm = S1.sb("xm", [128, 8, NM], BF16)
    rt = S1.sb("rt", [128, NM], F32)
    rstd = S1.sb("rstd", [128, NM], F32)
    tmpn = [S1.sb("tmpn%d" % i, [128, NM], F32) for i in range(2)]
    vslb = [S1.sb("vsl%d" % i, [128, NM], F32) for i in range(2)]
    uxb = S1.sb("uxb", [128, 10, NM], BF16)
    uxcb = [S1.sb("uxc%d" % i, [128, 10, 384], BF16) for i in range(2)]
    ABs = [S1.sb("AB%d" % i, [128, 4, 384], F32) for i in range(4)]
    trb = [S1.sb("tr%d" % i, [128, 384], F32) for i in range(8)]
    tib = [S1.sb("ti%d" % i, [128, 384], F32) for i in range(8)]
    a2b = [S1.sb("a2%d" % i, [128, 384], F32) for i in range(8)]
    hsb = [S1.sb("hs%d" % i, [128, 384], F32) for i in range(2)]
    accb = S1.sb("acc", [128, 2, 10, NBL + 1], F32)

    if debug:
        print("S1 sbuf remaining", nc.sbuf_bytes_remaining)
    GTI = {}
    for ti, (j, i) in enumerate(GT):
        GTI.setdefault(j, []).append((ti, i))
    AX = mybir.AxisListType.X

    def rgA(B):
        src, vsrc, i0, N, par, store, tau0, is_ctx = B["src"], B["vsrc"], B["i0"], B["N"], B["par"], B["store"], B["tau0"], B["blk"] is None
        N3 = N + 3
        ntile = (N3 + 127) // 128
        vsl, uxc = vslb[par], uxcb[par]
        steps = []

        def s_load():
            for t in range(ntile):
                nt = min(128, N3 - 128 * t)
                K.dma("sp", xt4[0:nt, t, :], src[i0 - 2 + 128 * t:i0 - 2 + 128 * t + nt, :], src, xt4, xt4, part=(t > 0))
            K.dma("sp", vsl[:, 0:N3], vsrc[:, i0 - 2:i0 - 2 + N3], vsrc, vsl, vsl)
        steps.append(s_load)

        def s_tr(c):
            pp = PS[c % 2]
            for t in range(ntile):
                nt = min(128, N3 - 128 * t)
                K.op("pe", lambda e, t=t, nt=nt: e.transpose(
                    out=pp[:, 128 * t:128 * t + nt], in_=xt4[0:nt, t, c * 128:(c + 1) * 128], identity=identF[0:nt, 0:nt]),
                    [xt4, identF], [pp], part=(t > 0))
            if c % 2 == 0:
                K.op("act", lambda e: e.copy(out=hb_[:, c, 0:N3], in_=pp[:, 0:N3]), [pp], [hb_], part=True)
            else:
                K.op("dve", lambda e: e.tensor_copy(out=hb_[:, c, 0:N3], in_=pp[:, 0:N3]), [pp], [hb_], part=True)
        for c in range(8):
            steps.append(lambda c=c: s_tr(c))
        if is_ctx:
            gs = lambda c: VEC[:, V_GSC, c:c + 1]
            sh = lambda c: VEC[:, V_SHC, c:c + 1]
        else:
            gs = lambda c: VEC[:, V_GS1, c:c + 1]
            sh = lambda c: VEC[:, V_SH1, c:c + 1]

        def s_stat():
            K.op("act", lambda e: e.activation(out=xm[:, :, 0:N3], in_=hb_[:, :, 0:N3], func=AF.Square), [hb_], [xm])
            for c in range(8):
                K.op("pe", lambda e, c=c: e.matmul(PS[2][:, 0:N3], lhsT=onesB[:], rhs=xm[:, c, 0:N3], start=(c == 0), stop=(c == 7)),
                     [xm], [PS[2]], part=(c > 0), inc=(c == 7))
            K.op("act", lambda e: e.activation(out=rt[:, 0:N3], in_=PS[2][:, 0:N3], func=AF.Sqrt, scale=1.0 / D, bias=epsb[:, 0:1]),
                 [PS[2]], [rt])
            K.op("dve", lambda e: e.reciprocal(out=rstd[:, 0:N3], in_=rt[:, 0:N3]), [rt], [rstd])
        steps.append((s_stat, "stat"))

        def s_mod(c):
            tb = tmpn[c % 2]
            K.op("dve", lambda e: e.tensor_tensor(out=tb[:, 0:N3], in0=hb_[:, c, 0:N3], in1=rstd[:, 0:N3], op=ALU.mult), [hb_, rstd], [tb])
            K.op("act", lambda e: e.activation(out=xm[:, c, 0:N3], in_=tb[:, 0:N3], func=AF.Identity, scale=gs(c), bias=sh(c)),
                 [tb], [xm], part=True)
        for c in range(8):
            steps.append(lambda c=c: s_mod(c))
        if store:
            def s_store():
                K.dma("pool", HX[:, :, tau0:tau0 + N], hb_[:, :, 2:2 + N], hb_, HX, hb_, part=True)
                K.dma("pool", XM[:, :, tau0:tau0 + N], xm[:, :, 2:2 + N], xm, XM, xm, part=True)
            steps.append(s_store)

        def s_win(oc):
            pp = PS[oc % 2]
            for k in range(8):
                K.op("pe", lambda e, k=k: e.matmul(pp[:, 0:N3], lhsT=w_in[:, k, oc * 128:(oc + 1) * 128], rhs=xm[:, k, 0:N3],
                                                   start=(k == 0), stop=(k == 7)), [w_in, xm], [pp], part=(k > 0), inc=(k == 7))
            K.op("dve", lambda e: e.tensor_tensor(out=uxb[:, oc, 0:N3], in0=pp[:, 0:N3], in1=vsl[:, 0:N3], op=ALU.mult),
                 [pp, vsl], [uxb], part=True)
        for oc in range(10):
            steps.append(lambda oc=oc: s_win(oc))

        def s_conv(c):
            pp = PS[2] if c % 2 == 0 else PS[5]
            for t in range(4):
                K.op("pe", lambda e, t=t: e.matmul(pp[:, 0:N], lhsT=dg4[:, c * 4 + t, :], rhs=uxb[:, c, t:t + N], start=(t == 0), stop=(t == 3)),
                     [dg4, uxb], [pp], part=(t > 0), inc=(t == 3))
            K.op("act", lambda e: e.activation(out=uxc[:, c, 0:N], in_=pp[:, 0:N], func=AF.Identity, bias=P("rg_conv_b", 1, c), scale=1.0),
                 [pp], [uxc], part=True)
        for c in range(10):
            steps.append(lambda c=c: s_conv(c))
        return steps

    def rgB(B):
        N, par, blk, edge, store, tau0, other = B["N"], B["par"], B["blk"], B["edge"], B["store"], B["tau0"], B["other"]
        vsl, uxc = vslb[par], uxcb[par]
        if other is None:
            cfgs = []
            for d in range(2):
                cfgs.append(dict(
                    tb=[(d * 2 + g) * NGT for g in range(2)], hb=(lambda g, c, d=d: HBG[:, d * 2 + g, c:c + 1]),
                    s05=(lambda c, d=d: SCV[:, d, c:c + 1]), s1=(lambda c, d=d: SCV[:, 2 + d, c:c + 1]), plane=d, pa=(lambda c, d=d: 2 * d),
                    acc=(lambda c, d=d: accb[:, d, c, NBL:NBL + 1] if blk is None else accb[:, d, c, blk:blk + 1]),
                    scans=[(d == 1, (lambda c, d=d: HCTX[:, d, c:c + 1] if blk is None else SUML[:, d, 1, c, blk:blk + 1]),
                            HCTX if blk is None else SUML)]))
            gsz = 2
        else:
            o = other
            cfgs = [dict(
                tb=[g * NGT for g in range(2)], hb=(lambda g, c: HBGO[:, o * 2 + g, c:c + 1]),
                s05=(lambda c: SCVO[:, o, c:c + 1]), s1=(lambda c: SCVO[:, 3 + o, c:c + 1]), plane=0, pa=(lambda c: 2 * ((c // 4) % 2)),
                acc=(lambda c: accO[:, o, c, blk:blk + 1]),
                scans=[(False, (lambda c: SUMO[:, o, 0, 1, c, blk:blk + 1]), SUMO),
                       (True, (lambda c: SUMO[:, o, 1, 1, c, blk:blk + 1]), SUMO)])]
            gsz = 3
        steps = []

        def s_p1(ui, c, cf):
            for g in range(2):
                pp = PS[3 + g] if ui % 2 == 0 else PS[6 + g]
                tl = GTI[c]
                for n_, (ti, i) in enumerate(tl):
                    K.op("pe", lambda e, ti=ti, i=i, n_=n_: e.matmul(
                        pp[:, 0:N], lhsT=gwt[:, cf["tb"][g] + ti, :], rhs=uxc[:, i, 0:N],
                        start=(n_ == 0), stop=(n_ == len(tl) - 1)), [gwt, uxc], [pp], part=(n_ > 0), inc=(n_ == len(tl) - 1))
                dst = trb[ui] if g == 0 else tib[ui]
                K.op("act", lambda e, g=g, dst=dst: e.activation(
                    out=dst[:, 0:N], in_=pp[:, 0:N], func=AF.Tanh, scale=0.5, bias=cf["hb"](g, c)), [pp], [dst])

        def s_p2(ui, c, cf):
            ab = ABs[c % 4]
            d = cf["plane"]
            if edge:
                K.op("dve", lambda e: e.scalar_tensor_tensor(
                    out=trb[ui][:, 0:N], in0=trb[ui][:, 0:N], scalar=1.0, in1=vsl[:, 2:2 + N], op0=ALU.add, op1=ALU.mult,
                    accum_out=cf["acc"](c)), [trb[ui], vsl], [trb[ui], accb], part=True)
            else:
                K.op("dve", lambda e: e.tensor_scalar(
                    out=trb[ui][:, 0:N], in0=trb[ui][:, 0:N], scalar1=1.0, scalar2=None, op0=ALU.add, op1=ALU.add,
                    accum_out=cf["acc"](c)), [trb[ui]], [trb[ui], accb], part=True)
            pa = cf["pa"](c)
            K.op("act", lambda e: e.activation(out=ab[:, pa, 0:N], in_=trb[ui][:, 0:N], func=AF.Exp, scale=cf["s05"](c)),
                 [trb[ui]], [ab], part=(other is not None or d > 0))
            K.op("act", lambda e: e.activation(out=a2b[ui][:, 0:N], in_=trb[ui][:, 0:N], func=AF.Exp, scale=cf["s1"](c)),
                 [trb[ui]], [a2b[ui]])
            K.op("dve", lambda e: e.scalar_tensor_tensor(
                out=tib[ui][:, 0:N], in0=tib[ui][:, 0:N], scalar=1.0, in1=uxc[:, c, 0:N], op0=ALU.add, op1=ALU.mult),
                [tib[ui], uxc], [tib[ui]])

        def s_p3(uis):
            for ui in uis:
                K.op("act", lambda e, ui=ui: e.activation(
                    out=a2b[ui][:, 0:N], in_=a2b[ui][:, 0:N], func=AF.Sqrt, scale=-1.0, bias=epsb[:, 1:2]), [a2b[ui]], [a2b[ui]])

        def s_p4(ui, c, cf):
            ab = ABs[c % 4]
            d = cf["plane"]
            pa = cf["pa"](c)
            K.op("dve", lambda e: e.scalar_tensor_tensor(
                out=ab[:, pa + 1, 0:N], in0=a2b[ui][:, 0:N], scalar=0.0, in1=tib[ui][:, 0:N], op0=ALU.max, op1=ALU.mult),
                [tib[ui], a2b[ui]], [ab], part=True)
            for si, (rev, dstf, dstbuf) in enumerate(cf["scans"]):
                hs = hsb[(d + si) % 2]
                if not rev:
                    K.op("dve", lambda e, hs=hs: e.tensor_tensor_scan(
                        out=hs[:, 0:N], data0=ab[:, pa, 0:N], data1=ab[:, pa + 1, 0:N], initial=0.0,
                        op0=ALU.mult, op1=ALU.add), [ab], [hs])
                    src_col = hs[:, N - 1:N]
                else:
                    K.op("dve", lambda e, hs=hs: e.tensor_tensor_scan(
                        out=hs[:, 0:N][:, ::-1], data0=ab[:, pa, 0:N][:, ::-1], data1=ab[:, pa + 1, 0:N][:, ::-1],
                        initial=0.0, op0=ALU.mult, op1=ALU.add), [ab], [hs])
                    src_col = hs[:, 0:1]
                K.op("dve", lambda e, src_col=src_col, dstc=dstf(c): e.tensor_copy(out=dstc, in_=src_col), [hs], [dstbuf], part=True)
            if store and d == 1:
                K.dma("pool", SAB[c][:, :, tau0:tau0 + N], ab[:, :, 0:N], ab, SAB, ab, part=True)

        groups = []
        for gi, c0 in enumerate(range(0, 10, gsz)):
            groups.append([((gi % 2) * 4 + ui, c, cf) for ui, (c, cf) in
                           enumerate([(c, cf) for c in range(c0, min(10, c0 + gsz)) for cf in cfgs])])

        def P1(g):
            return [((lambda u=u: s_p1(*u)), "b") for u in groups[g]]

        def P2(g):
            return [((lambda u=u: s_p2(*u)), "b") for u in groups[g]]

        def P3(g):
            return [((lambda uis=[u[0] for u in groups[g]]: s_p3(uis)), "sqrt")]

        def P4(g):
            return [((lambda u=u: s_p4(*u)), "b") for u in groups[g]]
        steps = P1(0) + P2(0)
        for g in range(len(groups)):
            if g + 1 < len(groups):
                steps += P1(g + 1)
            steps += P3(g)
            if g + 1 < len(groups):
                steps += P2(g + 1)
            steps += P4(g)
        return steps

    blist = [dict(src=cin, vsrc=cvalid, i0=2, N=CTX, blk=None, edge=True, store=False, tau0=0, other=None, pre=None)]
    tau = 0
    for bi, nr in enumerate(RG_ROWS):
        blist.append(dict(src=xin, vsrc=valid, i0=tau + 2, N=nr * 64, blk=bi, edge=bi in (0, NBL - 1), store=(mode != "sum"),
                          tau0=tau, other=None, pre=None))
        tau += nr * 64
    if mode == "solo":
        for o in range(3):
            tau = 0
            for j, nr in enumerate(OTH_ROWS):
                blist.append(dict(src=Buf(xoth.t[o], "xo"), vsrc=Buf(ovalid.t[o], "vo"), i0=tau + 2, N=nr * 64, blk=j,
                                  edge=j in (0, NBO - 1), store=False, tau0=0, other=o, pre=(o if j == 0 else None)))
                tau += nr * 64
    for i, B in enumerate(blist):
        B["par"] = i % 2
    def astep(st):
        return st if isinstance(st, tuple) else (st, "a")
    for st in rgA(blist[0]):
        astep(st)[0]()
    for i, B in enumerate(blist):
        if B["pre"] is not None:
            o = B["pre"]
            K.dma("pool", gwt[:, 0:2 * NGT, :], gwo_d[o * 2 * NGT:(o + 1) * 2 * NGT, :, :].rearrange("t p m -> p t m"),
                  gwo_d, gwt, gwt)
        sb_ = rgB(B)
        sa_ = [astep(st) for st in rgA(blist[i + 1])] if i + 1 < len(blist) else []
        ia = 0
        for ib, (fb, tagb) in enumerate(sb_):
            fb()
            tgt = ((ib + 1) * len(sa_)) // len(sb_)
            while ia < tgt:
                fa, taga = sa_[ia]
                if taga == "stat" and tagb != "sqrt" and any(t == "sqrt" for (_, t) in sb_[ib + 1:]):
                    break
                fa()
                ia += 1
        while ia < len(sa_):
            sa_[ia][0]()
            ia += 1
    for d in range(2):
        for c in range(10):
            K.op("act", lambda e, d=d, c=c: e.activation(
                out=SUML[:, d, 0, c, :], in_=accb[:, d, c, 0:NBL], func=AF.Exp, scale=SCV[:, d, c:c + 1]),
                [accb], [SUML], part=True)
    if mode == "solo":
        for o in range(3):
            for c in range(10):
                for d in range(2):
                    K.op("act", lambda e, d=d, c=c, o=o: e.activation(
                        out=SUMO[:, o, d, 0, c, :], in_=accO[:, o, c, :], func=AF.Exp, scale=SCVO[:, o, c:c + 1]),
                        [accb], [SUMO], part=True)
    S1.close()

    SX = Stage(K, "x")
    if mode == "sum":
        stg = SX.sb("stg", [128, 2, 2, 10, NBO], F32)
        K.op("dve", lambda e: e.tensor_copy(out=stg[:], in_=SUML[:, :, :, :, 1:NBL - 1]), [SUML], [stg])
        K.dma("sp", sum_out[:], stg[:].rearrange("p a b c j -> p (a b c j)"), stg, sum_out, stg)
        K.final_wait("sp")
        SX.close()
        G.close()
        return nc

    sumg = SUMO if mode == "solo" else SX.sb("sumg", [128, 8, 2, 2, 10, NBO], F32)
    if mode == "solo":
        pass
    elif mode == "main":
        K.dma("sp", sumg[:].rearrange("p r a b c j -> p r (a b c j)"), sumg_d[:], sumg_d, sumg, sumg)
    else:
        stg = SX.sb("stg", [128, 2, 2, 10, NBO], F32)
        K.op("dve", lambda e: e.tensor_copy(out=stg[:], in_=SUML[:, :, :, :, 1:NBL - 1]), [SUML], [stg])
        K.dma("pool", sum_loc_d[:], stg[:].rearrange("p a b c j -> p (a b c j)"), stg, sum_loc_d, stg)
        K.all_gather(sum_loc_d, sumg_all, [list(range(8))])
        K.dma("pool", sumg[:].rearrange("p r a b c j -> p r (a b c j)"), sumg_all[:, :].rearrange("(r p) w -> p r w", p=128),
              sumg_all, sumg, sumg)
    fm_ = SX.sb("fm", [128, 2, NS, NBO], F32)
    K.dma("sp", fm_[:].rearrange("p d r j -> p d (r j)"), fmask[:], fmask, fm_, fm_)
    At = SX.sb("At", [128, NS, NBO], F32)
    Ht = SX.sb("Ht", [128, NS, NBO], F32)
    sco = SX.sb("sco", [128, NS * NBO], F32)
    for d in range(2):
        for c in range(10):
            K.op("dve", lambda e, d=d, c=c: e.scalar_tensor_tensor(
                out=At[:], in0=sumg[:, :, d, 0, c, :], scalar=-1.0, in1=fm_[:, d, :, :], op0=ALU.add, op1=ALU.mult),
                [sumg, fm_], [At])
            K.op("dve", lambda e: e.tensor_scalar(out=At[:], in0=At[:], scalar1=1.0, scalar2=None, op0=ALU.add), [At], [At])
            K.op("dve", lambda e, d=d, c=c: e.tensor_tensor(out=Ht[:], in0=sumg[:, :, d, 1, c, :], in1=fm_[:, d, :, :], op=ALU.mult),
                 [sumg, fm_], [Ht])
            Af = At[:].rearrange("p r j -> p (r j)")
            Hf = Ht[:].rearrange("p r j -> p (r j)")
            if d == 0:
                K.op("dve", lambda e, d=d, c=c, Af=Af, Hf=Hf: e.tensor_tensor_scan(
                    out=sco[:], data0=Af, data1=Hf, initial=HCTX[:, d, c:c + 1], op0=ALU.mult, op1=ALU.add),
                    [At, Ht, HCTX], [sco])
                K.op("dve", lambda e, c=c: e.tensor_copy(out=HIN[:, 0, c, 0:1], in_=sco[:, NS * NBO - 1:NS * NBO]), [sco], [HIN], part=True)
                K.op("dve", lambda e, c=c: e.tensor_tensor_scan(
                    out=HIN[:, 0, c, 1:NBL + 1], data0=SUML[:, 0, 0, c, :], data1=SUML[:, 0, 1, c, :],
                    initial=HIN[:, 0, c, 0:1], op0=ALU.mult, op1=ALU.add), [SUML, HIN], [HIN], part=True)
            else:
                K.op("dve", lambda e, d=d, c=c, Af=Af, Hf=Hf: e.tensor_tensor_scan(
                    out=sco[:][:, ::-1], data0=Af[:, ::-1], data1=Hf[:, ::-1], initial=HCTX[:, d, c:c + 1],
                    op0=ALU.mult, op1=ALU.add), [At, Ht, HCTX], [sco])
                K.op("dve", lambda e, c=c: e.tensor_copy(out=HIN[:, 1, c, NBL:NBL + 1], in_=sco[:, 0:1]), [sco], [HIN], part=True)
                K.op("dve", lambda e, c=c: e.tensor_tensor_scan(
                    out=HIN[:, 1, c, 0:NBL][:, ::-1], data0=SUML[:, 1, 0, c, :][:, ::-1], data1=SUML[:, 1, 1, c, :][:, ::-1],
                    initial=HIN[:, 1, c, NBL:NBL + 1], op0=ALU.mult, op1=ALU.add), [SUML, HIN], [HIN], part=True)
    SX.close()

    S2 = Stage(K, "r2")
    w_out = S2.sb("w_out", [128, 10, D], BF16)
    K.dma("pool", w_out[:], rg_w_out[:, :].rearrange("(c p) m -> p c m", p=128), rg_w_out, w_out, w_out)
    w_ing = S2.sb("w_ing", [128, 8, DR], BF16)
    K.dma("pool", w_ing[:], rg_w_in[:, 0:DR].rearrange("(k p) m -> p k m", p=128), rg_w_in, w_ing, w_ing)
    if mode != "sum":
        for li in range(2):
            for cc in range(44):
                K.dma("pool", WUP[li][cc], ffn_up[li][:, cc * 128:(cc + 1) * 128].rearrange("(k p) m -> p k m", p=128),
                      ffn_up, WUP[li], WUP[li], part=True)
            for m in range(8):
                K.dma("pool", WDN[li][m],
                      ffn_dn[li][:, m * 128:(m + 1) * 128].rearrange("(j p) m -> p j m", p=128),
                      ffn_dn, WDN[li], WDN[li], part=True)
    AB2 = [S2.sb("AB%d" % i, [128, 4, 384], F32) for i in range(3)]
    gel2 = [S2.sb("gel%d" % i, [128, 10, 384], BF16) for i in range(2)]
    xm2 = [S2.sb("xm%d" % i, [128, 8, 384], BF16) for i in range(2)]
    hx2 = [S2.sb("hx%d" % i, [128, 8, 384], F32) for i in range(2)]
    hsf = [S2.sb("hsf%d" % i, [128, 384], F32) for i in range(2)]
    hsr = [S2.sb("hsr%d" % i, [128, 384], F32) for i in range(2)]
    vb = [S2.sb("v%d" % i, [128, 10, 384], BF16) for i in range(2)]
    taus = [0]
    for nr in RG_ROWS:
        taus.append(taus[-1] + nr * 64)

    def p2_load(bi):
        N, tau = RG_ROWS[bi] * 64, taus[bi]
        xmb, hx, g2_ = xm2[bi % 2], hx2[bi % 2], gel2[bi % 2]
        K.dma("sp", xmb[:, :, 0:N], XM[:, :, tau:tau + N], XM, xmb, xmb)
        K.dma("sp", hx[:, :, 0:N], HX[:, :, tau:tau + N], HX, hx, hx)
        for oc in range(10):
            pp = PS[4 + oc % 4]
            for k in range(8):
                K.op("pe", lambda e, k=k: e.matmul(
                    pp[:, 0:N], lhsT=w_ing[:, k, oc * 128:(oc + 1) * 128], rhs=xmb[:, k, 0:N], start=(k == 0), stop=(k == 7)),
                    [w_ing, xmb], [pp], part=(k > 0), inc=(k == 7))
            K.op("act", lambda e: e.activation(out=g2_[:, oc, 0:N], in_=pp[:, 0:N], func=AF.Gelu_apprx_tanh), [pp], [g2_], part=True)

    def p2_scan(bi):
        N, tau = RG_ROWS[bi] * 64, taus[bi]
        g2_, v = gel2[bi % 2], vb[bi % 2]
        for c in range(10):
            ab = AB2[c % 3]
            K.dma("sp", ab[:, :, 0:N], SAB[c][:, :, tau:tau + N], SAB, ab, ab)
            hf_, hr_ = hsf[c % 2], hsr[c % 2]
            K.op("dve", lambda e: e.tensor_tensor_scan(
                out=hf_[:, 0:N], data0=ab[:, 0, 0:N], data1=ab[:, 1, 0:N], initial=HIN[:, 0, c, bi:bi + 1],
                op0=ALU.mult, op1=ALU.add), [ab], [hf_])
            K.op("dve", lambda e: e.tensor_tensor_scan(
                out=hr_[:, 0:N][:, ::-1], data0=ab[:, 2, 0:N][:, ::-1], data1=ab[:, 3, 0:N][:, ::-1],
                initial=HIN[:, 1, c, bi + 1:bi + 2], op0=ALU.mult, op1=ALU.add), [ab], [hr_])
            K.op("dve", lambda e: e.tensor_tensor(out=hf_[:, 0:N], in0=hf_[:, 0:N], in1=hr_[:, 0:N], op=ALU.add), [hf_, hr_], [hf_])
            K.op("dve", lambda e: e.scalar_tensor_tensor(
                out=v[:, c, 0:N], in0=hf_[:, 0:N], scalar=0.5, in1=g2_[:, c, 0:N], op0=ALU.mult, op1=ALU.mult),
                [hf_, g2_], [v], part=True)

    def p2_out_mm(bi):
        N = RG_ROWS[bi] * 64
        v = vb[bi % 2]
        for m in range(8):
            pp = PS[m % 4]
            for c in range(10):
                K.op("pe", lambda e, c=c: e.matmul(
                    pp[:, 0:N], lhsT=w_out[:, c, m * 128:(m + 1) * 128], rhs=v[:, c, 0:N], start=(c == 0), stop=(c == 9)),
                    [w_out, v], [pp], part=(c > 0), inc=(c == 9))
            hx = hx2[bi % 2]
            K.op("dve", lambda e, m=m, pp=pp, hx=hx: e.scalar_tensor_tensor(
                out=hx[:, m, 0:N], in0=pp[:, 0:N], scalar=VEC[:, V_G1, m:m + 1], in1=hx[:, m, 0:N], op0=ALU.mult, op1=ALU.add),
                [pp, hx], [hx], part=True)

    p2_load(0)
    p2_scan(0)
    for bi in range(NBL):
        if bi + 1 < NBL:
            p2_load(bi + 1)
        p2_out_mm(bi)
        if bi + 1 < NBL:
            p2_scan(bi + 1)
        N_ = RG_ROWS[bi] * 64
        K.dma("sp", HA[:, :, taus[bi]:taus[bi] + N_], hx2[bi % 2][:, :, 0:N_], hx2[bi % 2], HA, hx2[bi % 2], part=True)
    S2.close()
    if upto == "rg":
        K.final_wait("sp")
        return nc

    def ffn_stage(li, Hin, rin0, Hout, rout0, blocks, final):
        S = Stage(K, "f%d" % li)
        NIM = 640
        hbuf = [S.sb("h%d" % i, [128, 8, NIM], F32) for i in range(2)]
        sqxb = [S.sb("sqx%d" % i, [128, 8, NIM], BF16) for i in range(2)]
        rt_ = S.sb("rt", [128, NIM], F32)
        rstd_ = S.sb("rstd", [128, NIM], F32)
        tmpf = [S.sb("tmp%d" % i, [128, NIM], F32) for i in range(2)]
        vsb = [S.sb("vs%d" % i, [128, NIM], F32) for i in range(2)]
        wup = [S.sb("wup%d" % i, [128, 8, 128], BF16) for i in range(4)]
        wdn = [S.sb("wdn%d" % i, [128, 22, 128], BF16) for i in range(2)]
        usb = [S.sb("usb%d" % i, [128, 10, 64], BF16) for i in range(3)]
        dg9 = [S.sb("dg9%d" % i, [128, 9, 128], BF16) for i in range(4)]
        sgb = [S.sb("sg%d" % i, [128, 512], F32) for i in range(2)]
        hid = S.sb("hid", [128, 22, 512], BF16)
        if final:
            yf = S.sb("yf", [128, 8, 512], F32)
            ost = [S.sb("ost%d" % i, [128, D], F32) for i in range(2)]
        gsr = lambda c: VEC[:, V_GS2 + 6 * li, c:c + 1]
        shr = lambda c: VEC[:, V_SH2 + 6 * li, c:c + 1]
        cwo = _off["ffn_cw"] + li * 396
        cbo = _off["ffn_cb"] + li * 44
        wi = 0
        pend_store = []

        def ffn_prep_steps(bi):
            r0, r1 = blocks[bi]
            NI = (r1 - r0) * 64 + 128
            h = hbuf[bi % 2]
            xmq = sqxb[bi % 2]
            t_in = (r0 - 1 - rin0) * 64
            segs_ = [(0, min(NI, 512))] + ([(512, NI)] if NI > 512 else [])
            st = []

            def s_load():
                K.dma("sp", h[:, :, 0:NI], Hin[:, :, t_in:t_in + NI], Hin, h, h)
                if (r0 - 1 < 0) or (r1 + 1 > 64):
                    xi = (r0 - 1 + 3) * 64 + 2
                    K.dma("sp", vsb[bi % 2][:, 0:NI], valid[:, xi:xi + NI], valid, vsb[bi % 2], vsb[bi % 2])
            st.append(s_load)
            st.append(lambda: K.op("act", lambda e: e.activation(out=xmq[:, :, 0:NI], in_=h[:, :, 0:NI], func=AF.Square), [h], [xmq]))
            for (a, b) in segs_:
                def s_ss(a=a, b=b):
                    for c in range(8):
                        K.op("pe", lambda e, c=c: e.matmul(PS[7][:, 0:b - a], lhsT=onesB[:], rhs=xmq[:, c, a:b], start=(c == 0), stop=(c == 7)),
                             [xmq], [PS[7]], part=(c > 0), inc=(c == 7))
                st.append(s_ss)
                st.append(lambda a=a, b=b: K.op("act", lambda e: e.activation(
                    out=rt_[:, a:b], in_=PS[7][:, 0:b - a], func=AF.Sqrt, scale=1.0 / D, bias=epsb[:, 0:1]), [PS[7]], [rt_], part=True))
            st.append(lambda: K.op("dve", lambda e: e.reciprocal(out=rstd_[:, 0:NI], in_=rt_[:, 0:NI]), [rt_], [rstd_]))
            for c in range(8):
                def s_mod(c=c):
                    tb = tmpf[c % 2]
                    K.op("dve", lambda e: e.tensor_tensor(out=tb[:, 0:NI], in0=h[:, c, 0:NI], in1=rstd_[:, 0:NI], op=ALU.mult), [h, rstd_], [tb])
                    K.op("act", lambda e: e.activation(out=xmq[:, c, 0:NI], in_=tb[:, 0:NI], func=AF.Identity, scale=gsr(c), bias=shr(c)),
                         [tb], [xmq], part=True)
                st.append(s_mod)
            return st

        for f_ in ffn_prep_steps(0):
            f_()
        for bi, (r0, r1) in enumerate(blocks):
            nr = r1 - r0
            N = nr * 64
            NI = N + 128
            h = hbuf[bi % 2]
            edge = (r0 - 1 < 0) or (r1 + 1 > 64)
            vs_ = vsb[bi % 2]
            sqx = sqxb[bi % 2]
            xmf = sqx
            def mk(i):
                j, half = i // 2, i % 2
                return dict(j=j, half=half, cc=j + 22 * half, w=wup[(wi0 + i) % 4], us=usb[(wi0 + i) % 3], dg=dg9[(wi0 + i) % 4],
                            ua=PS[((wi0 + i) % 2) * 2], ub=PS[((wi0 + i) % 2) * 2 + 1], cp=PS[4 + (wi0 + i) % 3])

            def segs(c):
                return [(pp, a, b) for (pp, a, b) in ((c["ua"], 0, min(NI, 512)), (c["ub"], 512, NI)) if b > a]

            def stU(c):
                w = c["w"]
                K.dma("sp", w[:], WUP[li][c["cc"]], WUP[li], w, w)
                K.dma("sp", c["dg"][:], DG[li][c["cc"]], DG[li], c["dg"], c["dg"])
                for (pp, a, b) in segs(c):
                    for k in range(8):
                        K.op("pe", lambda e, pp=pp, a=a, b=b, k=k: e.matmul(
                            pp[:, 0:b - a], lhsT=w[:, k, :], rhs=xmf[:, k, a:b], start=(k == 0), stop=(k == 7)),
                            [w, xmf], [pp], part=(k > 0), inc=(k == 7))

            def stE(c):
                us = c["us"]
                usf = us[:].rearrange("p r c -> p (r c)")
                for si, (pp, a, b) in enumerate(segs(c)):
                    if edge:
                        K.op("dve", lambda e, pp=pp, a=a, b=b: e.tensor_tensor(
                            out=usf[:, a:b], in0=pp[:, 0:b - a], in1=vs_[:, a:b], op=ALU.mult), [pp, vs_], [us], part=(si > 0))
                    elif si == 0:
                        K.op("act", lambda e, pp=pp, a=a, b=b: e.copy(out=usf[:, a:b], in_=pp[:, 0:b - a]), [pp], [us], part=(si > 0))
                    else:
                        K.op("dve", lambda e, pp=pp, a=a, b=b: e.tensor_copy(out=usf[:, a:b], in_=pp[:, 0:b - a]), [pp], [us], part=(si > 0))

            def stC(c):
                us, dg, cp = c["us"], c["dg"], c["cp"]
                cp3 = cp[:, 0:N].rearrange("p (r c) -> p r c", c=64)
                for n_, t in enumerate([4, 0, 1, 2, 3, 5, 6, 7, 8]):
                    dy, dx = t // 3 - 1, t % 3 - 1
                    oc0, oc1 = max(0, -dx), 64 - max(0, dx)
                    K.op("pe", lambda e, t=t, dy=dy, dx=dx, oc0=oc0, oc1=oc1, n_=n_: e.matmul(
                        cp3[:, :, oc0:oc1], lhsT=dg[:, t, :], rhs=us[:, 1 + dy:1 + dy + nr, oc0 + dx:oc1 + dx],
                        start=(n_ == 0), stop=(n_ == 8)), [dg, us], [cp], part=(n_ > 0), inc=(n_ == 8))

            def stF(cv_, cg_):
                j = cv_["j"]
                sg = sgb[j % 2]
                K.op("act", lambda e: e.activation(
                    out=sg[:, 0:N], in_=cg_["cp"][:, 0:N], func=AF.Silu, bias=prm[:, cbo + 22 + j:cbo + 23 + j], scale=1.0), [cg_["cp"]], [sg])
                K.op("dve", lambda e: e.scalar_tensor_tensor(
                    out=hid[:, j, 0:N], in0=cv_["cp"][:, 0:N], scalar=prm[:, cbo + j:cbo + j + 1], in1=sg[:, 0:N], op0=ALU.add, op1=ALU.mult),
                    [cv_["cp"], sg], [hid], part=True)

            wi0 = wi
            cfg = [mk(i) for i in range(44)]
            wi += 44
            pre_steps = ffn_prep_steps(bi + 1) if bi + 1 < len(blocks) else []
            stU(cfg[0])
            for i in range(44):
                if i + 1 < 44:
                    stU(cfg[i + 1])
                if i == 1 and pend_store:
                    pend_store.pop(0)()
                stE(cfg[i])
                stC(cfg[i])
                if i % 2 == 1:
                    stF(cfg[i - 1], cfg[i])
                if bi + 1 < len(blocks) and i >= 4 and i % 2 == 0 and pre_steps:
                    pre_steps.pop(0)()
            while pre_steps:
                pre_steps.pop(0)()
            for m in range(8):
                w = wdn[m % 2]
                K.dma("sp", w[:], WDN[li][m], WDN[li], w, w)
                pp = PS[7] if m % 2 == 0 else PS[6]
                for j in range(22):
                    K.op("pe", lambda e, pp=pp, j=j, w=w: e.matmul(
                        pp[:, 0:N], lhsT=w[:, j, :], rhs=hid[:, j, 0:N], start=(j == 0), stop=(j == 21)), [w, hid], [pp], part=(j > 0), inc=(j == 21))
                K.op("dve", lambda e, pp=pp, m=m, h=h: e.scalar_tensor_tensor(
                    out=h[:, m, 64:64 + N], in0=pp[:, 0:N], scalar=VEC[:, V_G2 + 6 * li, m:m + 1], in1=h[:, m, 64:64 + N],
                    op0=ALU.mult, op1=ALU.add), [pp, h], [h], part=True)
            if not final:
                t_out = (r0 - rout0) * 64
                pend_store.append((lambda h=h, N=N, t_out=t_out: K.dma(
                    "sp", Hout[:, :, t_out:t_out + N], h[:, :, 64:64 + N], h, Hout, h, part=True)))
                if bi + 1 == len(blocks):
                    pend_store.pop(0)()
            else:
                K.op("act", lambda e, h=h: e.activation(out=sqx[:, :, 0:N], in_=h[:, :, 64:64 + N], func=AF.Square), [h], [sqx])
                for c in range(8):
                    K.op("pe", lambda e, c=c: e.matmul(PS[7][:, 0:N], lhsT=onesB[:], rhs=sqx[:, c, 0:N], start=(c == 0), stop=(c == 7)),
                         [sqx], [PS[7]], part=(c > 0), inc=(c == 7))
                K.op("act", lambda e: e.activation(out=rt_[:, 0:N], in_=PS[7][:, 0:N], func=AF.Sqrt, scale=1.0 / D, bias=epsb[:, 0:1]),
                     [PS[7]], [rt_])
                K.op("dve", lambda e: e.reciprocal(out=rstd_[:, 0:N], in_=rt_[:, 0:N]), [rt_], [rstd_])
                for c in range(8):
                    K.op("dve", lambda e, c=c, h=h: e.scalar_tensor_tensor(
                        out=yf[:, c, 0:N], in0=h[:, c, 64:64 + N], scalar=P("norm_final", 1, c), in1=rstd_[:, 0:N],
                        op0=ALU.mult, op1=ALU.mult), [h, rstd_], [yf], part=True)
                for tt in range(N // 128):
                    o = ost[tt % 2]
                    for hf in range(2):
                        pp = PS[hf]
                        for c4 in range(4):
                            c = hf * 4 + c4
                            K.op("pe", lambda e, c=c, c4=c4, tt=tt, pp=pp: e.transpose(
                                out=pp[:, c4 * 128:(c4 + 1) * 128], in_=yf[:, c, tt * 128:(tt + 1) * 128], identity=identF[:]),
                                [yf], [pp], part=(c4 > 0))
                        if hf == 0:
                            K.op("act", lambda e, o=o, pp=pp: e.copy(out=o[:, 0:512], in_=pp[:, :]), [pp], [o], part=True)
                        else:
                            K.op("dve", lambda e, o=o, pp=pp: e.tensor_copy(out=o[:, 512:1024], in_=pp[:, :]), [pp], [o], part=True)
                    tok = r0 * 64 + tt * 128
                    K.dma("sp", out_d[tok:tok + 128, :], o[:], o, out_d, o, part=True)
        S.close()

    def conf_stage(Hin, rin0, Hout, rout0, blocks):
        S = Stage(K, "c")
        w1 = S.sb("w1", [128, 8, 2 * D], BF16)
        for hf in range(2):
            K.dma("pool", w1[:, :, hf * D:(hf + 1) * D], cf_w1[:, hf * D:(hf + 1) * D].rearrange("(k p) m -> p k m", p=128),
                  cf_w1, w1, w1, part=(hf > 0))
        w2 = S.sb("w2", [128, 8, D], BF16)
        K.dma("pool", w2[:], cf_w2[:, :].rearrange("(k p) m -> p k m", p=128), cf_w2, w2, w2)
        dgc = [S.sb("dgc%d" % i, [128, 31, 128], BF16) for i in range(2)]
        NIM = 542
        hb2 = [S.sb("h%d" % i, [128, 8, NIM], F32) for i in range(2)]
        sqb2 = [S.sb("sqx%d" % i, [128, 8, NIM], BF16) for i in range(2)]
        rt_ = S.sb("rt", [128, NIM], F32)
        rstd_ = S.sb("rstd", [128, NIM], F32)
        rtn_ = S.sb("rtn", [128, NIM], F32)
        rstdn_ = S.sb("rstdn", [128, NIM], F32)
        tmpf = [S.sb("tmp%d" % i, [128, NIM], F32) for i in range(2)]
        vsb2 = [S.sb("vs%d" % i, [128, NIM], F32) for i in range(2)]
        sgb = [S.sb("sg%d" % i, [128, NIM], F32) for i in range(2)]
        glu = S.sb("glu", [128, 8, NIM], BF16)
        cv = S.sb("cv", [128, 8, 512], F32)
        act_ = S.sb("act", [128, 8, 512], BF16)
        dsq = act_
        hn = cv
        gsr = lambda c: VEC[:, V_GS1 + 6, c:c + 1]
        shr = lambda c: VEC[:, V_SH1 + 6, c:c + 1]

        def conf_prep_steps(bi):
            r0, r1 = blocks[bi]
            NI = (r1 - r0) * 64 + 30
            h, xmq, vs_ = hb2[bi % 2], sqb2[bi % 2], vsb2[bi % 2]
            t_in = (r0 - rin0) * 64 - 15
            segs_ = [(0, min(NI, 512))] + ([(512, NI)] if NI > 512 else [])
            st = []

            def s_load():
                K.dma("sp", h[:, :, 0:NI], Hin[:, :, t_in:t_in + NI], Hin, h, h)
                if (r0 - 1 < 0) or (r1 + 1 > 64):
                    xi = (r0 + 3) * 64 + 2 - 15
                    K.dma("sp", vs_[:, 0:NI], valid[:, xi:xi + NI], valid, vs_, vs_)
            st.append(s_load)
            st.append(lambda: K.op("act", lambda e: e.activation(out=xmq[:, :, 0:NI], in_=h[:, :, 0:NI], func=AF.Square), [h], [xmq]))
            for (a, b) in segs_:
                def s_ss(a=a, b=b):
                    for c in range(8):
                        K.op("pe", lambda e, c=c: e.matmul(PS[7][:, 0:b - a], lhsT=onesB[:], rhs=xmq[:, c, a:b], start=(c == 0), stop=(c == 7)),
                             [xmq], [PS[7]], part=(c > 0), inc=(c == 7))
                    K.op("act", lambda e: e.activation(
                        out=rtn_[:, a:b], in_=PS[7][:, 0:b - a], func=AF.Sqrt, scale=1.0 / D, bias=epsb[:, 0:1]), [PS[7]], [rtn_], part=True)
                st.append(s_ss)
            st.append(lambda: K.op("dve", lambda e: e.reciprocal(out=rstdn_[:, 0:NI], in_=rtn_[:, 0:NI]), [rtn_], [rstdn_]))
            for c in range(8):
                def s_mod(c=c):
                    tb = tmpf[c % 2]
                    K.op("dve", lambda e: e.tensor_tensor(out=tb[:, 0:NI], in0=h[:, c, 0:NI], in1=rstdn_[:, 0:NI], op=ALU.mult), [h, rstdn_], [tb])
                    K.op("act", lambda e: e.activation(out=xmq[:, c, 0:NI], in_=tb[:, 0:NI], func=AF.Identity, scale=gsr(c), bias=shr(c)),
                         [tb], [xmq], part=True)
                st.append(s_mod)
            return st

        for f_ in conf_prep_steps(0):
            f_()
        for bi, (r0, r1) in enumerate(blocks):
            N = (r1 - r0) * 64
            NI = N + 30
            h, sqx, vs_ = hb2[bi % 2], sqb2[bi % 2], vsb2[bi % 2]
            edge = (r0 - 1 < 0) or (r1 + 1 > 64)
            pre_steps = conf_prep_steps(bi + 1) if bi + 1 < len(blocks) else []
            segs = [(0, min(NI, 512))] + ([(512, NI)] if NI > 512 else [])
            for j in range(8):
                pa = [PS[0], PS[1]]
                pg = [PS[2], PS[3]]
                if j % 2 == 1:
                    pa, pg = [PS[4], PS[5]], [PS[6], PS[7]]
                for (pset, oc) in ((pg, 8 + j), (pa, j)):
                    for si, (a, b) in enumerate(segs):
                        for k in range(8):
                            K.op("pe", lambda e, pset=pset, si=si, a=a, b=b, k=k, oc=oc: e.matmul(
                                pset[si][:, 0:b - a], lhsT=w1[:, k, oc * 128:(oc + 1) * 128], rhs=sqx[:, k, a:b],
                                start=(k == 0), stop=(k == 7)), [w1, sqx], [pset[si]], part=(k > 0), inc=(k == 7))
                sg = sgb[j % 2]
                for si, (a, b) in enumerate(segs):
                    K.op("act", lambda e, si=si, a=a, b=b, sg=sg, pg=pg, j=j: e.activation(
                        out=sg[:, a:b], in_=pg[si][:, 0:b - a], func=AF.Sigmoid, bias=P("cf_b1", 1, 8 + j), scale=1.0),
                        [pg[si]], [sg], part=(si > 0))
                    K.op("dve", lambda e, si=si, a=a, b=b, sg=sg, pa=pa, j=j: e.scalar_tensor_tensor(
                        out=glu[:, j, a:b], in0=pa[si][:, 0:b - a], scalar=P("cf_b1", 1, j), in1=sg[:, a:b], op0=ALU.add, op1=ALU.mult),
                        [pa[si], sg], [glu], part=True)
                if edge:
                    K.op("dve", lambda e, j=j: e.tensor_tensor(out=glu[:, j, 0:NI], in0=glu[:, j, 0:NI], in1=vs_[:, 0:NI], op=ALU.mult),
                         [glu, vs_], [glu], part=True)
                if pre_steps:
                    pre_steps.pop(0)()
            for j in range(8):
                pp = PS[j % 4]
                dgt = dgc[j % 2]
                K.dma("sp", dgt[:], DG31[j], DG31, dgt, dgt)
                for t in range(31):
                    K.op("pe", lambda e, j=j, t=t, pp=pp, dgt=dgt: e.matmul(
                        pp[:, 0:N], lhsT=dgt[:, t, :], rhs=glu[:, j, t:t + N], start=(t == 0), stop=(t == 30)),
                        [dgt, glu], [pp], part=(t > 0), inc=(t == 30))
                K.op("act", lambda e, j=j, pp=pp: e.activation(out=cv[:, j, 0:N], in_=pp[:, 0:N], func=AF.Identity,
                                                              bias=P("cf_conv_b", 1, j), scale=1.0), [pp], [cv], part=True)
                if pre_steps:
                    pre_steps.pop(0)()
            while pre_steps:
                pre_steps.pop(0)()
            for j in range(8):
                K.op("pe", lambda e, j=j: e.matmul(PS[4][:, 0:N], lhsT=onesF[:], rhs=cv[:, j, 0:N], start=(j == 0), stop=(j == 7)),
                     [cv], [PS[4]], part=(j > 0), inc=(j == 7))
            for j in range(8):
                K.op("dve", lambda e, j=j: e.scalar_tensor_tensor(
                    out=cv[:, j, 0:N], in0=PS[4][:, 0:N], scalar=-1.0 / D, in1=cv[:, j, 0:N], op0=ALU.mult, op1=ALU.add),
                    [PS[4], cv], [cv], part=True)
            K.op("act", lambda e: e.activation(out=dsq[:, :, 0:N], in_=cv[:, :, 0:N], func=AF.Square), [cv], [dsq])
            for j in range(8):
                K.op("pe", lambda e, j=j: e.matmul(PS[5][:, 0:N], lhsT=onesB[:], rhs=dsq[:, j, 0:N], start=(j == 0), stop=(j == 7)),
                     [dsq], [PS[5]], part=(j > 0), inc=(j == 7))
            K.op("act", lambda e: e.activation(out=rt_[:, 0:N], in_=PS[5][:, 0:N], func=AF.Sqrt, scale=1.0 / D, bias=epsb[:, 0:1]),
                 [PS[5]], [rt_])
            K.op("dve", lambda e: e.reciprocal(out=rstd_[:, 0:N], in_=rt_[:, 0:N]), [rt_], [rstd_])
            for j in range(8):
                K.op("dve", lambda e, j=j: e.tensor_tensor(out=cv[:, j, 0:N], in0=cv[:, j, 0:N], in1=rstd_[:, 0:N], op=ALU.mult),
                     [cv, rstd_], [cv], part=True)
                K.op("act", lambda e, j=j: e.activation(out=act_[:, j, 0:N], in_=cv[:, j, 0:N], func=AF.Silu,
                                                       scale=P("cf_ln_g", 1, j), bias=P("cf_ln_b", 1, j)), [cv], [act_], part=True)
            for m in range(8):
                pp = PS[m % 4]
                for j in range(8):
                    K.op("pe", lambda e, m=m, j=j, pp=pp: e.matmul(
                        pp[:, 0:N], lhsT=w2[:, j, m * 128:(m + 1) * 128], rhs=act_[:, j, 0:N], start=(j == 0), stop=(j == 7)),
                        [w2, act_], [pp], part=(j > 0), inc=(j == 7))
                K.op("act", lambda e, m=m, pp=pp: e.activation(out=hn[:, m, 0:N], in_=pp[:, 0:N], func=AF.Identity,
                                                              scale=VEC[:, V_G1 + 6, m:m + 1], bias=VEC[:, V_BG, m:m + 1]),
                     [pp], [hn], part=True)
                K.op("dve", lambda e, m=m: e.tensor_tensor(out=hn[:, m, 0:N], in0=hn[:, m, 0:N], in1=h[:, m, 15:15 + N], op=ALU.add),
                     [hn, h], [hn], part=True)
            t_out = (r0 - rout0) * 64
            K.dma("sp", Hout[:, :, t_out:t_out + N], hn[:, :, 0:N], hn, Hout, hn, part=True)
        S.close()

    own8 = [(8 * i, 8 * i + 8) for i in range(8)]
    ffn_stage(0, HA, -3, HB, -2, [(-2, 4)] + [(4 + 8 * i, 12 + 8 * i) for i in range(7)] + [(60, 66)], False)
    if upto == "ffn0":
        K.final_wait("sp")
        return nc
    conf_stage(HB, -2, HC, -1, [(-1 + 8 * i, 7 + 8 * i) for i in range(8)] + [(63, 65)])
    if upto == "conf":
        K.final_wait("sp")
        return nc
    ffn_stage(1, HC, -1, None, 0, own8, True)
    K.final_wait("sp")
    G.close()
    return nc


def _prep_core_inputs(core, inp, gate_tiles):
    b, k = core // 4, core % 4
    T0 = 4096 * k
    x = inp["x"]
    xin = np.zeros((NTX, D), np.float32)
    g0 = T0 - 194
    lo, hi = max(g0, 0), min(g0 + NTX, SEQ)
    xin[lo - g0:hi - g0] = x[b, lo:hi]
    valid = np.zeros((128, NTX), np.float32)
    valid[:, lo - g0:hi - g0] = 1.0
    cin = np.zeros((CTX + 3, D), np.float32)
    cin[2:2 + CTX] = inp["ctx"][b]
    cvalid = np.zeros((128, CTX + 3), np.float32)
    cvalid[:, 2:2 + CTX] = 1.0
    prm = np.zeros((128, PC), np.float32)

    def put(name, arr):
        arr = np.asarray(arr, np.float32).reshape(128, -1)
        prm[:, _off[name]:_off[name] + arr.shape[1]] = arr
    cT = np.stack([fm(inp["c"][b], 8), fm(inp["c_ctx"], 8)], axis=-1)
    put("cT", cT)
    put("ada_b", np.stack([fm(inp["ada_b"][i], 48) for i in range(2)], axis=1))
    put("norm_mix", np.stack([fm(inp["norm_mix"][i], 8) for i in range(2)], axis=1))
    put("norm_ffn", np.stack([fm(inp["norm_ffn"][i], 8) for i in range(2)], axis=1))
    put("norm_final", fm(inp["norm_final"], 8))
    put("rg_conv_w", np.stack([fm(inp["rg_conv_w"][0, t], 10) for t in range(4)], axis=-1))
    put("rg_conv_b", fm(inp["rg_conv_b"][0], 10))
    put("rg_ba", np.stack([fm(inp["rg_ba"][0, d].reshape(-1), 10) for d in range(2)], axis=1))
    put("rg_bx", np.stack([fm(inp["rg_bx"][0, d].reshape(-1), 10) for d in range(2)], axis=1))
    put("rg_lam", np.stack([fm(inp["rg_lam"][0, d], 10) for d in range(2)], axis=1))
    put("cf_b1", fm(inp["cf_b_pw1"][0], 16))
    put("cf_conv_w", np.stack([fm(inp["cf_conv_w"][0, t], 8) for t in range(31)], axis=-1))
    put("cf_conv_b", fm(inp["cf_conv_b"][0], 8))
    put("cf_ln_g", fm(inp["cf_ln_g"][0], 8))
    put("cf_ln_b", fm(inp["cf_ln_b"][0], 8))
    put("cf_b2", fm(inp["cf_b_pw2"][0], 8))
    cw = np.stack([np.stack([fm(inp["ffn_conv_w"][i].reshape(9, -1)[t], 44) for t in range(9)], axis=-1)
                   for i in range(2)], axis=1)
    put("ffn_cw", cw)
    put("ffn_cb", np.stack([fm(inp["ffn_conv_b"][i], 44) for i in range(2)], axis=1))
    m = {"xin": xin, "cin": cin, "valid": valid, "cvalid": cvalid}
    xoth = np.zeros((3, 4099, D), np.float32)
    ovalid = np.zeros((3, 128, 4099), np.float32)
    gwo = np.zeros((3, 2, NGT, 128, 128), np.float32)
    fmk = np.zeros((128, 2, 3, NBO), np.float32)
    o_lam = np.zeros((128, 3, 10), np.float32)
    o_ba = np.zeros((128, 3, 10), np.float32)
    o_bx = np.zeros((128, 3, 10), np.float32)
    for o in range(3):
        ko = (k + 1 + o) % 4
        dsel = 0 if ko < k else 1
        g0 = 4096 * ko - 2
        lo, hi = max(g0, 0), min(g0 + 4099, SEQ)
        xoth[o, lo - g0:hi - g0] = x[b, lo:hi]
        ovalid[o, :, lo - g0:hi - g0] = 1.0
        gwo[o, 0] = gate_tiles[dsel * 2 + 0]
        gwo[o, 1] = gate_tiles[dsel * 2 + 1]
        o_lam[:, o] = fm(inp["rg_lam"][0, dsel], 10)
        o_ba[:, o] = fm(inp["rg_ba"][0, dsel].reshape(-1), 10)
        o_bx[:, o] = fm(inp["rg_bx"][0, dsel].reshape(-1), 10)
        for j in range(NBO):
            if ko < k and (ko < k - 1 or j <= NBO - 2):
                fmk[:, 0, o, j] = 1.0
            if ko > k and (ko > k + 1 or j >= 1):
                fmk[:, 1, o, j] = 1.0
    put("o_lam", o_lam)
    put("o_ba", o_ba)
    put("o_bx", o_bx)
    m["prm"] = prm
    m["xoth"] = xoth
    m["ovalid"] = ovalid
    m["gwo"] = gwo.reshape(3 * 2 * NGT, 128, 128)
    m["fmask"] = fmk.reshape(128, 2, 3 * NBO)
    return m


def _shared_inputs(inp):
    gw = np.zeros((4, NGT, 128, 128), np.float32)
    mats = [inp["rg_wa"][0, 0], inp["rg_wx"][0, 0], inp["rg_wa"][0, 1], inp["rg_wx"][0, 1]]
    for gi, w in enumerate(mats):
        dense = np.zeros((DR, DR), np.float32)
        for h in range(16):
            dense[h * 80:(h + 1) * 80, h * 80:(h + 1) * 80] = w[h]
        for ti, (j, i) in enumerate(GT):
            gw[gi, ti] = dense[i * 128:(i + 1) * 128, j * 128:(j + 1) * 128]
    return gw, {
        "ident": np.eye(128, dtype=np.float32),
        "ada_w": np.ascontiguousarray(inp["ada_w"], np.float32),
        "rg_w_in": np.ascontiguousarray(inp["rg_w_in"][0], np.float32),
        "gw": gw.reshape(4 * NGT, 128, 128),
        "rg_w_out": np.ascontiguousarray(inp["rg_w_out"][0], np.float32),
        "cf_w1": np.ascontiguousarray(inp["cf_w_pw1"][0], np.float32),
        "cf_w2": np.ascontiguousarray(inp["cf_w_pw2"][0], np.float32),
        "ffn_up": np.ascontiguousarray(inp["ffn_w_up"], np.float32),
        "ffn_dn": np.ascontiguousarray(inp["ffn_w_down"], np.float32),
    }


_SUM_KEYS = ["xin", "cin", "valid", "cvalid", "prm", "ident", "ada_w", "rg_w_in", "gw"]
_PROG = {}


def _prog(mode):
    if mode not in _PROG:
        _PROG[mode] = build_program(mode)
    return _PROG[mode]


def kernel(**inputs):
    inp = {k: np.asarray(v) for k, v in inputs.items()}
    gate_tiles, shared = _shared_inputs(inp)
    percore = [_prep_core_inputs(c, inp, gate_tiles) for c in range(8)]
    cores = list(range(8))
    in2 = [{**shared, **percore[c]} for c in cores]
    r2 = run_bass_kernel_spmd(_prog("solo"), in2, core_ids=cores)
    out = np.zeros((2, SEQ, D), np.float32)
    for c in cores:
        b, k = c // 4, c % 4
        out[b, 4096 * k:4096 * (k + 1)] = np.asarray(r2.results[c]["out"])
    return out
```
